# Optimizing a Trainium2 kernel written in Bass

```python
import math
import jax, jax.numpy as jnp
from jax import lax
import numpy as np

D_MODEL = 2048
BATCH = 2
SEQ = 4096
DEPTH = 1

D_MIX = D_MODEL
HEAD_DIM = 128
ATTN_WIDTH = D_MIX // 2
ATTN_HEADS = ATTN_WIDTH // HEAD_DIM
SSM_WIDTH = D_MIX - ATTN_WIDTH
SSM_GROUP = 16
SSM_GROUPS = SSM_WIDTH // SSM_GROUP
SSM_STATE = 64
DILATED_PATTERNS = ((128, 1), (512, 4), (2048, 16))
BLK = 128
D_FF = 4 * D_MODEL
PLE_DIM = 256
RMS_EPS = 1e-6
NEG_INF = -1e30

kernel_name = "hybrid_dilated_attn_s5_block"


def _rmsnorm(x, g):
    xf = x.astype(jnp.float32)
    y = xf * lax.rsqrt(jnp.mean(xf * xf, axis=-1, keepdims=True) + RMS_EPS)
    return (y * g.astype(jnp.float32)).astype(x.dtype)


def _dilated_branch(q, k, v, window, dilation):
    B, S, H, E = q.shape
    M = S // dilation
    n_keys = window // dilation
    nb = -(-M // BLK)
    Mp = nb * BLK

    def blocks(t):
        t = t.reshape(B, M, dilation, H, E)
        t = jnp.pad(t, ((0, 0), (0, Mp - M), (0, 0), (0, 0), (0, 0)))
        return t.reshape(B, nb, BLK, dilation, H, E)

    def with_prev(t):
        prev = jnp.pad(t[:, :-1], ((0, 0), (1, 0), (0, 0), (0, 0), (0, 0), (0, 0)))
        return jnp.concatenate([prev, t], axis=2)

    qb = blocks(q)
    kk = with_prev(blocks(k))
    vv = with_prev(blocks(v))

    scale = 1.0 / math.sqrt(E)
    s = jnp.einsum('bnqrhe,bnkrhe->bnrhqk', qb, kk,
                   preferred_element_type=jnp.float32) * scale
    i = jnp.arange(BLK)
    j = jnp.arange(2 * BLK)
    blk = jnp.arange(nb)
    dist = BLK + i[:, None] - j[None, :]
    kpos = (blk[:, None] - 1) * BLK + j[None, :]
    mask = ((dist >= 0) & (dist <= n_keys))[None] & (kpos >= 0)[:, None, :]
    s = jnp.where(mask[None, :, None, None], s, NEG_INF)
    lse = jax.nn.logsumexp(s, axis=-1)
    prob = jnp.exp(s - lse[..., None])
    o = jnp.einsum('bnrhqk,bnkrhe->bnqrhe', prob, vv.astype(jnp.float32))
    o = o.reshape(B, Mp, dilation, H, E)[:, :M].reshape(B, S, H, E)
    lse = lse.transpose(0, 1, 4, 2, 3).reshape(B, Mp, dilation, H)[:, :M].reshape(B, S, H)
    return o, lse


def _dilated_attention(q, k, v):
    outs, lses = [], []
    for window, dilation in DILATED_PATTERNS:
        o, l = _dilated_branch(q, k, v, window, dilation)
        outs.append(o)
        lses.append(l)
    w = jax.nn.softmax(jnp.stack(lses, axis=0), axis=0)
    o = jnp.sum(w[..., None] * jnp.stack(outs, axis=0), axis=0)
    B, S, H, E = q.shape
    return o.reshape(B, S, H * E).astype(q.dtype)


def _ssm_combine(e1, e2):
    a1r, a1i, b1r, b1i = e1
    a2r, a2i, b2r, b2i = e2
    ar = a1r * a2r - a1i * a2i
    ai = a1r * a2i + a1i * a2r
    br = a2r * b1r - a2i * b1i + b2r
    bi = a2r * b1i + a2i * b1r + b2i
    return (ar, ai, br, bi)


def _s5(u, lam_re, lam_im, log_dt, b_re, b_im, c_re, c_im, d_skip, w_glu, b_glu):
    B, S, _ = u.shape
    uf = u.astype(jnp.float32).reshape(B, S, SSM_GROUPS, SSM_GROUP)
    lr0 = lam_re.astype(jnp.float32)
    li0 = lam_im.astype(jnp.float32)
    dt = jnp.exp(log_dt.astype(jnp.float32))[:, None]
    mag = jnp.exp(lr0 * dt)
    abar_r = mag * jnp.cos(li0 * dt)
    abar_i = mag * jnp.sin(li0 * dt)
    num_r = abar_r - 1.0
    num_i = abar_i
    den = lr0 * lr0 + li0 * li0
    coef_r = ((num_r * lr0 + num_i * li0) / den)[..., None]
    coef_i = ((num_i * lr0 - num_r * li0) / den)[..., None]
    br = b_re.astype(jnp.float32)
    bi = b_im.astype(jnp.float32)
    bbar_r = coef_r * br - coef_i * bi
    bbar_i = coef_r * bi + coef_i * br
    bu_r = jnp.einsum('bsgc,gpc->bsgp', uf, bbar_r)
    bu_i = jnp.einsum('bsgc,gpc->bsgp', uf, bbar_i)
    a_r = jnp.broadcast_to(abar_r[None, None], (1, S, SSM_GROUPS, SSM_STATE))
    a_i = jnp.broadcast_to(abar_i[None, None], (1, S, SSM_GROUPS, SSM_STATE))
    _, _, st_r, st_i = lax.associative_scan(_ssm_combine, (a_r, a_i, bu_r, bu_i), axis=1)
    y = (jnp.einsum('gcp,bsgp->bsgc', c_re.astype(jnp.float32), st_r)
         - jnp.einsum('gcp,bsgp->bsgc', c_im.astype(jnp.float32), st_i))
    y = y.reshape(B, S, SSM_WIDTH) + d_skip.astype(jnp.float32) * u.astype(jnp.float32)
    y = jax.nn.gelu(y).astype(u.dtype)
    gate = jax.nn.sigmoid(y @ w_glu + b_glu)
    return y * gate


def setup_inputs(seed: int = 0) -> dict:
    key = jax.random.key(seed)
    ks = jax.random.split(key, 32)
    f32 = jnp.float32
    L = DEPTH

    def nrm(k, shape, scale):
        return jax.random.normal(k, shape, f32) * scale

    def gain(k, shape):
        return 1.0 + 0.02 * jax.random.normal(k, shape, f32)

    G, P, C = SSM_GROUPS, SSM_STATE, SSM_GROUP
    n_idx = jnp.arange(P, dtype=f32)
    return {
        "x": nrm(ks[0], (BATCH, SEQ, D_MODEL), 1.0),
        "p": nrm(ks[1], (DEPTH, BATCH, SEQ, PLE_DIM), 1.0),
        "mix_norm_pre": gain(ks[2], (L, D_MODEL)),
        "w_in": nrm(ks[3], (L, D_MODEL, 3 * ATTN_WIDTH + SSM_WIDTH), D_MODEL ** -0.5),
        "lam_re": -0.5 + 0.01 * jax.random.normal(ks[4], (L, G, P), f32),
        "lam_im": math.pi * n_idx + 0.01 * jax.random.normal(ks[5], (L, G, P), f32),
        "log_dt": jax.random.uniform(ks[6], (L, G), f32, math.log(1e-3), math.log(1e-1)),
        "ssm_b_re": nrm(ks[7], (L, G, P, C), (2 * C) ** -0.5),
        "ssm_b_im": nrm(ks[8], (L, G, P, C), (2 * C) ** -0.5),
        "ssm_c_re": nrm(ks[9], (L, G, C, P), (2 * P) ** -0.5),
        "ssm_c_im": nrm(ks[10], (L, G, C, P), (2 * P) ** -0.5),
        "ssm_d": nrm(ks[11], (L, SSM_WIDTH), 1.0),
        "w_glu": nrm(ks[12], (L, SSM_WIDTH, SSM_WIDTH), SSM_WIDTH ** -0.5),
        "b_glu": nrm(ks[13], (L, SSM_WIDTH), 0.01),
        "attn_out_norm": gain(ks[14], (L, ATTN_WIDTH)),
        "ssm_out_norm": gain(ks[15], (L, SSM_WIDTH)),
        "w_out": nrm(ks[16], (L, D_MIX, D_MODEL), D_MIX ** -0.5),
        "mix_norm_post": gain(ks[17], (L, D_MODEL)),
        "mlp_norm_pre": gain(ks[18], (L, D_MODEL)),
        "w_up": nrm(ks[19], (L, D_MODEL, D_FF), D_MODEL ** -0.5),
        "w_down": nrm(ks[20], (L, D_FF, D_MODEL), D_FF ** -0.5),
        "mlp_norm_post": gain(ks[21], (L, D_MODEL)),
        "ple_norm_pre": gain(ks[22], (L, D_MODEL)),
        "w_ple_gate": nrm(ks[23], (L, D_MODEL, D_MODEL), D_MODEL ** -0.5),
        "w_ple_proj": nrm(ks[24], (L, PLE_DIM, D_MODEL), PLE_DIM ** -0.5),
        "ple_norm_post": gain(ks[25], (L, D_MODEL)),
    }


def reference(x, p, mix_norm_pre, w_in, lam_re, lam_im, log_dt, ssm_b_re, ssm_b_im,
              ssm_c_re, ssm_c_im, ssm_d, w_glu, b_glu, attn_out_norm, ssm_out_norm,
              w_out, mix_norm_post, mlp_norm_pre, w_up, w_down, mlp_norm_post,
              ple_norm_pre, w_ple_gate, w_ple_proj, ple_norm_post):
    B, S, _ = x.shape
    h = x
    for i in range(DEPTH):
        hn = _rmsnorm(h, mix_norm_pre[i])
        proj = hn @ w_in[i]
        q = proj[..., :ATTN_WIDTH].reshape(B, S, ATTN_HEADS, HEAD_DIM)
        k = proj[..., ATTN_WIDTH:2 * ATTN_WIDTH].reshape(B, S, ATTN_HEADS, HEAD_DIM)
        v = proj[..., 2 * ATTN_WIDTH:3 * ATTN_WIDTH].reshape(B, S, ATTN_HEADS, HEAD_DIM)
        u = proj[..., 3 * ATTN_WIDTH:]
        attn = _dilated_attention(q, k, v)
        ssm = _s5(u, lam_re[i], lam_im[i], log_dt[i], ssm_b_re[i], ssm_b_im[i],
                  ssm_c_re[i], ssm_c_im[i], ssm_d[i], w_glu[i], b_glu[i])
        mixed = jnp.concatenate([_rmsnorm(attn, attn_out_norm[i]),
                                 _rmsnorm(ssm, ssm_out_norm[i])], axis=-1)
        h = h + _rmsnorm(mixed @ w_out[i], mix_norm_post[i])
        hn = _rmsnorm(h, mlp_norm_pre[i])
        ff = jnp.square(jax.nn.relu(hn @ w_up[i])) @ w_down[i]
        h = h + _rmsnorm(ff, mlp_norm_post[i])
        gate = jax.nn.sigmoid(_rmsnorm(h, ple_norm_pre[i]) @ w_ple_gate[i])
        e = p[i] @ w_ple_proj[i]
        h = h + _rmsnorm(gate * e, ple_norm_post[i])
    return h
```

```python
import math
import os
KSTOP = int(os.environ.get('KSTOP', '0'))
KNG = int(os.environ.get('KNG', '8'))
KNH = int(os.environ.get('KNH', '8'))
KNOF = int(os.environ.get('KNOF', '0'))
KSIDE = int(os.environ.get('KSIDE', '1'))
from contextlib import ExitStack

import numpy as np
import concourse.bass as bass
import concourse.mybir as mybir
from concourse.bass_utils import run_bass_kernel_spmd

F32 = mybir.dt.float32
BF16 = mybir.dt.bfloat16
ALU = mybir.AluOpType
AF = mybir.ActivationFunctionType

D = 2048
KT = 16
NE = 4096
NO = 1024
NCH = 4
DFF = 8192
EPS = 1e-6
MAGIC = 12582912.0
TWO_PI = 2.0 * math.pi

COLS = [("mix_pre", 16), ("attn_n", 8), ("ssm_n", 8), ("mix_post", 16), ("mlp_pre", 16),
        ("mlp_post", 16), ("ple_pre", 16), ("ple_post", 16), ("ssm_d", 8), ("b_glu", 8)]
COL_OFF = {}
_o = 0
for _n, _c in COLS:
    COL_OFF[_n] = _o
    _o += _c
NCOLS = _o


def vtile_list():
    tiles = []
    for d in (1, 4, 16):
        M = NO // d
        for r in range(d):
            m = -128
            while m < M:
                nk = min(128, M - m)
                tiles.append((d, r, m, nk))
                m += 128
    return tiles


VT_LIST = vtile_list()
VT_IDX = {(d, r, m): i for i, (d, r, m, nk) in enumerate(VT_LIST)}
NVT = len(VT_LIST)


class Prog:
    ENG = ("pe", "act", "dve", "pool", "sp")

    def __init__(self, nc, stack, n_dma_sems=32):
        self.nc = nc
        self.q = {e: [] for e in self.ENG}
        self.cnt = {e: 0 for e in self.ENG}
        self.sem = {e: stack.enter_context(nc.semaphore("s_" + e)) for e in self.ENG}
        self.waited = {}
        n_sw = 16
        self.dsem = [stack.enter_context(nc.semaphore("d%d" % i)) for i in range(n_dma_sems + n_sw)]
        self.dcnt = [0] * (n_dma_sems + n_sw)
        self.drange = {"sp": (0, n_dma_sems), "pool": (n_dma_sems, n_dma_sems + n_sw)}
        self.dnext = {"sp": 0, "pool": n_dma_sems}
        self.ninst = 0

    def _waits(self, eng, deps):
        for d in deps:
            if d is None:
                continue
            if isinstance(d, list) or (isinstance(d, tuple) and len(d) and not isinstance(d[0], str)):
                self._waits(eng, d)
                continue
            kind, key, val = d
            wk = (eng, kind, key)
            if self.waited.get(wk, 0) >= val:
                continue
            self.waited[wk] = val
            sem = self.sem[key] if kind == "e" else self.dsem[key]
            self.q[eng].append(lambda E, sem=sem, val=val: E.wait_ge(sem, val))
            self.ninst += 1

    def op(self, eng, fn, deps=(), inc=True):
        self._waits(eng, deps)
        self.ninst += 1
        if inc:
            self.cnt[eng] += 1
            v = self.cnt[eng]
            sem = self.sem[eng]
            self.q[eng].append(lambda E, fn=fn, sem=sem: fn(E).then_inc(sem, 1))
            return ("e", eng, v)
        self.q[eng].append(lambda E, fn=fn: fn(E))
        return None

    def dma(self, eng, out, in_, deps=(), **kw):
        lo, hi = self.drange[eng]
        i = self.dnext[eng]
        self.dnext[eng] = lo + (i + 1 - lo) % (hi - lo)
        if self.dcnt[i] > 0:
            self._waits(eng, [("d", i, self.dcnt[i])])
        self._waits(eng, deps)
        self.dcnt[i] += 16
        sem = self.dsem[i]
        self.ninst += 1
        self.q[eng].append(lambda E, out=out, in_=in_, sem=sem, kw=kw: E.dma_start(out=out, in_=in_, **kw).then_inc(sem, 16))
        return ("d", i, self.dcnt[i])

    def all_tokens(self):
        toks = [("e", e, self.cnt[e]) for e in self.ENG if self.cnt[e] > 0]
        toks += [("d", i, self.dcnt[i]) for i in range(len(self.dsem)) if self.dcnt[i] > 0]
        return toks

    def barrier(self):
        toks = self.all_tokens()
        for e in self.ENG:
            self._waits(e, toks)

    def emit(self):
        nc = self.nc
        self._waits("sp", self.all_tokens())
        with nc.Block() as block:
            @block.tensor
            def _(E):
                for f in self.q["pe"]:
                    f(E)

            @block.scalar
            def _(E):
                for f in self.q["act"]:
                    f(E)

            @block.vector
            def _(E):
                for f in self.q["dve"]:
                    f(E)

            @block.gpsimd
            def _(E):
                for f in self.q["pool"]:
                    f(E)

            @block.sync
            def _(E):
                for f in self.q["sp"]:
                    f(E)


class Seq:
    def __init__(self, P, eng, deps=()):
        self.P = P
        self.eng = eng
        self.last = list(deps)

    def op(self, fn, deps=(), eng=None):
        e = eng or self.eng
        t = self.P.op(e, fn, [self.last, list(deps)])
        self.last = [t]
        return t

    def par(self, fns):
        toks = [self.P.op(self.eng, fn, [self.last]) for fn in fns]
        self.last = toks
        return toks


class Arena:
    def __init__(self, big, total):
        self.big = big
        self.total = total
        self.top = 0
        self.marks = []
        self.peak = 0

    def push(self):
        self.marks.append(self.top)

    def pop(self):
        self.top = self.marks.pop()

    def alloc(self, shape, dt):
        n = 1
        for s in shape[1:]:
            n *= s
        es = 2 if dt == BF16 else 4
        nbytes = (n * es + 63) // 64 * 64
        off = self.top
        self.top += nbytes
        self.peak = max(self.peak, self.top)
        assert self.top <= self.total, ("arena overflow", self.top, self.total)
        ap = self.big[:, off // 4:(off + nbytes) // 4]
        if dt == BF16:
            ap = ap.bitcast(BF16)
        ap = ap[:, 0:n]
        fs = shape[1:]
        if len(fs) == 2:
            ap = ap.rearrange("p (a b) -> p a b", b=fs[1])
        elif len(fs) == 3:
            ap = ap.rearrange("p (a b c) -> p a b c", b=fs[1], c=fs[2])
        elif len(fs) == 4:
            ap = ap.rearrange("p (a b c d) -> p a b c d", b=fs[1], c=fs[2], d=fs[3])
        if shape[0] < 128:
            ap = ap[0:shape[0]]
        return ap


def bcast_last(ap2, n):
    return ap2.unsqueeze(2).to_broadcast([ap2.shape[0], ap2.shape[1], n])


def build_nc(taps=()):
    nc = bass.Bass("TRN2", target_bir_lowering=False)
    dr = lambda name, shape, dt=F32: nc.dram_tensor(name, shape, dt, kind="ExternalInput").ap()
    xT = dr("xT", [D, NE])
    pT = dr("pT", [256, NO])
    cols_d = dr("cols", [128, NCOLS])
    sp_d = dr("sp", [128, 96])
    bpr_d = dr("bpr", [128, 1024])
    bpi_d = dr("bpi", [128, 1024])
    cpr_d = dr("cpr", [128, 1024])
    cpi_d = dr("cpi", [128, 1024])
    consts_d = dr("consts", [128, 4 * 128 + 4 + NVT])
    w_in = dr("w_in", [D, 4096])
    w_glu = dr("w_glu", [1024, 1024])
    w_out = dr("w_out", [D, D])
    w_up = dr("w_up", [D, DFF])
    w_down = dr("w_down", [DFF, D])
    w_pg = dr("w_pg", [D, D])
    w_pp = dr("w_pp", [256, D])
    out_d = nc.dram_tensor("out", [D, NO], F32, kind="ExternalOutput").ap()
    xg_s = nc.dram_tensor("xg_s", [D, 3072], BF16).ap()
    rs_s = nc.dram_tensor("rs_s", [128, 3072], F32).ap()
    h_s = nc.dram_tensor("h_s", [D, NO], F32).ap()
    ff_s = nc.dram_tensor("ff_s", [D, NO], F32).ap()
    yg_s = nc.dram_tensor("yg_s", [1024, NO], F32).ap()
    tap_out = {}

    w_in_v = w_in.rearrange("(kt p) m -> p kt m", p=128)

    with ExitStack() as top:
        P = Prog(nc, top)
        TOTAL = 207 * 1024
        big = top.enter_context(nc.sbuf_tensor("big", [128, TOTAL // 4], F32))
        A = Arena(big, TOTAL)
        banks = [top.enter_context(nc.psum_tensor("bank%d" % i, [128, 512], F32)) for i in range(8)]
        bfree = [None] * 8

        def tap(name, ap, shape):
            if name not in taps:
                return
            P.barrier()
            t = nc.dram_tensor("tap_" + name, list(shape), ap.dtype, kind="ExternalOutput").ap()
            tap_out[name] = t
            P.dma("sp", t, ap)
            P.barrier()

        cols = A.alloc([128, NCOLS], F32)
        cst = A.alloc([128, 4 * 128 + 4 + NVT], F32)
        ident_f = cst[:, 0:128]
        mprev_f = cst[:, 128:256]
        mcur_f = cst[:, 256:384]
        bmask = cst[:, 384:512]
        pmask = cst[:, 512:516]
        vcols = cst[:, 516:516 + NVT]
        ident_b = A.alloc([128, 128], BF16)
        ones_b = A.alloc([128, 128], BF16)
        mprev_b = A.alloc([128, 128], BF16)
        mcur_b = A.alloc([128, 128], BF16)
        epsc = A.alloc([128, 1], F32)
        t_cols = P.dma("sp", cols, cols_d)
        t_cst = P.dma("sp", cst, consts_d)
        t0 = P.op("dve", lambda E: E.tensor_copy(ident_b, ident_f), [t_cst])
        t1 = P.op("dve", lambda E: E.tensor_copy(mprev_b, mprev_f), [t_cst])
        t2 = P.op("dve", lambda E: E.tensor_copy(mcur_b, mcur_f), [t_cst])
        t3 = P.op("pool", lambda E: E.memset(ones_b, 1.0))
        t4 = P.op("pool", lambda E: E.memset(epsc, EPS))
        P.barrier()

        if KSTOP == 3:
            tap("cst", cst, [128, 4 * 128 + 4 + NVT])
            P.emit()
            return nc, tap_out

        def gcol(name, i):
            o = COL_OFF[name] + i
            return cols[:, o:o + 1]

        def rms_stats(src_fn, ntiles, ntok, rs_out, nfeat, sqbufs, stat_banks, deps):
            toks = []
            nnt = ntok // 512
            sqfree = [None] * len(sqbufs)
            j = 0
            last_mm = [None] * nnt
            for i in range(ntiles):
                for nt in range(nnt):
                    sq = sqbufs[j % len(sqbufs)]
                    dp = deps(i, nt) if callable(deps) else deps
                    t_sq = P.op("act", lambda E, sq=sq, i=i, nt=nt: E.activation(sq, src_fn(i, nt), AF.Square), [dp, sqfree[j % len(sqbufs)]])
                    b = stat_banks[nt]
                    t_mm = P.op("pe", lambda E, sq=sq, b=b, i=i: E.matmul(banks[b][:, :], ones_b, sq, start=(i == 0), stop=(i == ntiles - 1)),
                                [t_sq, bfree[b] if i == 0 else None])
                    sqfree[j % len(sqbufs)] = t_mm
                    last_mm[nt] = t_mm
                    j += 1
            for nt in range(nnt):
                b = stat_banks[nt]
                ta = P.op("act", lambda E, b=b, nt=nt: E.activation(rs_out[:, nt * 512:(nt + 1) * 512], banks[b][:, :], AF.Sqrt, bias=epsc, scale=1.0 / nfeat), [last_mm[nt]])
                bfree[b] = ta
                tb = P.op("dve", lambda E, nt=nt: E.reciprocal(rs_out[:, nt * 512:(nt + 1) * 512], rs_out[:, nt * 512:(nt + 1) * 512]), [ta])
                toks.append(tb)
            return toks

        A.push()
        yg = None
        spv = A.alloc([128, 96], F32)
        G = A.alloc([128, 8, 2, 8, 128], BF16)
        H = A.alloc([128, 8, 8, 128], BF16)
        ctab = A.alloc([128, 32, 128], F32)
        stab = A.alloc([128, 32, 128], F32)
        sm = A.alloc([128, 48, 32], F32)
        Zin_r = sm[:, 0, :]
        Zin_i = sm[:, 1, :]
        rho8 = sm[:, 2, :]
        e8r = sm[:, 3, :]
        e8i = sm[:, 4, :]
        apow = [(sm[:, 5 + 2 * t, :], sm[:, 6 + 2 * t, :]) for t in range(9)]
        _n = [23]

        def smalloc():
            i = _n[0]
            _n[0] += 1
            assert i < 48
            return sm[:, i, :]

        t_sp = P.dma("sp", spv, sp_d)
        S = Seq(P, "dve", [t_sp, t_cols])
        S.op(lambda E: E.memset(Zin_r, 0.0))
        S.op(lambda E: E.memset(Zin_i, 0.0))

        def cmul(S, or_, oi_, ar, ai, br, bi, t1, t2):
            S.par([lambda E: E.tensor_tensor(or_, ar, br, ALU.mult),
                   lambda E: E.tensor_tensor(t1, ai, bi, ALU.mult),
                   lambda E: E.tensor_tensor(oi_, ar, bi, ALU.mult),
                   lambda E: E.tensor_tensor(t2, ai, br, ALU.mult)])
            S.par([lambda E: E.tensor_tensor(or_, or_, t1, ALU.subtract),
                   lambda E: E.tensor_tensor(oi_, oi_, t2, ALU.add)])

        A.push()
        CPr = A.alloc([128, 32, 32], F32)
        CPni = A.alloc([128, 32, 32], F32)
        CPi0 = A.alloc([128, 32, 32], F32)
        t_cpr = P.dma("sp", CPr, cpr_d.rearrange("p (a b) -> p a b", b=32))
        t_cpi = P.dma("sp", CPi0, cpi_d.rearrange("p (a b) -> p a b", b=32))
        S.last = [S.last, t_cpr, t_cpi]
        S.op(lambda E: E.memset(CPni, 0.0))
        S.op(lambda E: E.tensor_tensor(CPni, CPni, CPi0, ALU.subtract))
        lamr = spv[:, 0:32]
        lami = spv[:, 32:64]
        ldt = spv[:, 64:96]
        dt_ = smalloc(); zr = smalloc(); zi = smalloc(); em1 = smalloc(); mag = smalloc()
        c1 = smalloc(); s1 = smalloc(); sh = smalloc(); ta = smalloc(); tb = smalloc()
        numr = smalloc(); numi = smalloc(); cfr = smalloc(); cfi = smalloc(); tc = smalloc(); td = smalloc()
        S.op(lambda E: E.activation(dt_, ldt, AF.Exp), eng="act")
        S.op(lambda E: E.tensor_tensor(zr, lamr, dt_, ALU.mult))
        S.op(lambda E: E.tensor_tensor(zi, lami, dt_, ALU.mult))
        S.op(lambda E: E.tensor_scalar(em1, zr, 1.0 / 120.0, None, ALU.mult))
        for cst_ in (1.0 / 24.0, 1.0 / 6.0, 0.5, 1.0):
            S.op(lambda E, c=cst_: E.scalar_tensor_tensor(em1, em1, c, zr, ALU.add, ALU.mult))
        S.op(lambda E: E.tensor_scalar(mag, em1, 1.0, None, ALU.add))

        def sin_of(dst, src, shift, scale):
            S.op(lambda E: E.tensor_scalar(ta, src, scale, shift, ALU.mult, ALU.add))
            S.op(lambda E: E.tensor_scalar(tb, ta, 1.0 / TWO_PI, MAGIC, ALU.mult, ALU.add))
            S.op(lambda E: E.tensor_scalar(tb, tb, MAGIC, None, ALU.subtract))
            S.op(lambda E: E.scalar_tensor_tensor(ta, tb, -TWO_PI, ta, ALU.mult, ALU.add))
            S.op(lambda E: E.tensor_scalar(ta, ta, math.pi, -math.pi, ALU.min, ALU.max))
            S.op(lambda E: E.activation(dst, ta, AF.Sin), eng="act")

        sin_of(s1, zi, 0.0, 1.0)
        sin_of(c1, zi, math.pi / 2.0, 1.0)
        sin_of(sh, zi, 0.0, 0.5)
        S.op(lambda E: E.tensor_tensor(ta, sh, sh, ALU.mult))
        S.op(lambda E: E.tensor_tensor(tb, em1, c1, ALU.mult))
        S.op(lambda E: E.scalar_tensor_tensor(numr, ta, -2.0, tb, ALU.mult, ALU.add))
        S.op(lambda E: E.tensor_tensor(numi, mag, s1, ALU.mult))
        S.op(lambda E: E.tensor_tensor(ta, lamr, lamr, ALU.mult))
        S.op(lambda E: E.tensor_tensor(tb, lami, lami, ALU.mult))
        S.op(lambda E: E.tensor_tensor(ta, ta, tb, ALU.add))
        S.op(lambda E: E.reciprocal(ta, ta))
        S.op(lambda E: E.tensor_tensor(tb, numr, lamr, ALU.mult))
        S.op(lambda E: E.tensor_tensor(tc, numi, lami, ALU.mult))
        S.op(lambda E: E.tensor_tensor(tb, tb, tc, ALU.add))
        S.op(lambda E: E.tensor_tensor(cfr, tb, ta, ALU.mult))
        S.op(lambda E: E.tensor_tensor(tb, numi, lamr, ALU.mult))
        S.op(lambda E: E.tensor_tensor(tc, numr, lami, ALU.mult))
        S.op(lambda E: E.tensor_tensor(tb, tb, tc, ALU.subtract))
        S.op(lambda E: E.tensor_tensor(cfi, tb, ta, ALU.mult))
        S.op(lambda E: E.memset(apow[0][0], 1.0))
        S.op(lambda E: E.memset(apow[0][1], 0.0))
        S.op(lambda E: E.tensor_tensor(apow[1][0], mag, c1, ALU.mult))
        S.op(lambda E: E.tensor_tensor(apow[1][1], mag, s1, ALU.mult))
        for t in range(1, 8):
            cmul(S, apow[t + 1][0], apow[t + 1][1], apow[t][0], apow[t][1], apow[1][0], apow[1][1], tc, td)
        S.op(lambda E: E.tensor_tensor(rho8, mag, mag, ALU.mult))
        S.op(lambda E: E.tensor_tensor(rho8, rho8, rho8, ALU.mult))
        S.op(lambda E: E.tensor_tensor(rho8, rho8, rho8, ALU.mult))
        e2r = smalloc(); e2i = smalloc()
        cmul(S, e2r, e2i, c1, s1, c1, s1, tc, td)
        e4r = smalloc(); e4i = smalloc()
        cmul(S, e4r, e4i, e2r, e2i, e2r, e2i, tc, td)
        cmul(S, e8r, e8i, e4r, e4i, e4r, e4i, tc, td)
        if KSTOP == 4:
            tap("sm", sm.rearrange("p a b -> p (a b)"), [128, 48 * 32])
            P.emit()
            return nc, tap_out
        big1 = A.alloc([128, 32, 64], F32)
        big2 = A.alloc([128, 32, 64], F32)
        S.op(lambda E: E.memset(ctab[:, :, 0:1], 1.0))
        S.op(lambda E: E.memset(stab[:, :, 0:1], 0.0))
        Er, Ei = e8r, e8i
        epp = [(smalloc(), smalloc()), (smalloc(), smalloc())]
        epi = 0
        L = 1
        while L < 128:
            br_ = bcast_last(Er, L)
            bi_ = bcast_last(Ei, L)
            cmul(S, ctab[:, :, L:2 * L], stab[:, :, L:2 * L], ctab[:, :, 0:L], stab[:, :, 0:L], br_, bi_,
                 big1[:, :, 0:L], big2[:, :, 0:L])
            if 2 * L < 128:
                nr, ni = epp[epi % 2]
                epi += 1
                cmul(S, nr, ni, Er, Ei, Er, Ei, tc, td)
                Er, Ei = nr, ni
            L *= 2
        if KSTOP == 5:
            tap("ctab", ctab.rearrange("p a b -> p (a b)"), [128, 32 * 128])
            tap("sm", sm.rearrange("p a b -> p (a b)"), [128, 48 * 32])
            P.emit()
            return nc, tap_out
        BPr = A.alloc([128, 32, 32], F32)
        BPi = A.alloc([128, 32, 32], F32)
        BBr = A.alloc([128, 32, 32], F32)
        BBi = A.alloc([128, 32, 32], F32)
        ABr = A.alloc([128, 32, 32], F32)
        ABi = A.alloc([128, 32, 32], F32)
        T1 = A.alloc([128, 32, 32], F32)
        T2 = A.alloc([128, 32, 32], F32)
        tmpH4 = A.alloc([128, 4, 128], F32)
        t_bpr = P.dma("sp", BPr, bpr_d.rearrange("p (a b) -> p a b", b=32))
        t_bpi = P.dma("sp", BPi, bpi_d.rearrange("p (a b) -> p a b", b=32))
        S.last = [S.last, t_bpr, t_bpi]
        cmul(S, BBr, BBi, BPr, BPi, bcast_last(cfr, 32), bcast_last(cfi, 32), T1, T2)
        if KSTOP == 6:
            tap("BBr", BBr.rearrange("p a b -> p (a b)"), [128, 1024])
            P.emit()
            return nc, tap_out
        ABh = [A.alloc([128, 1024], BF16) for _ in range(2)]
        ABl = [A.alloc([128, 1024], BF16) for _ in range(2)]
        Chl = [A.alloc([128, 1024], BF16) for _ in range(4)]
        for pp, src_ in ((0, CPr), (1, CPni)):
            srcf = src_.rearrange("p a b -> p (a b)")
            S.op(lambda E, pp=pp, srcf=srcf: E.tensor_copy(Chl[2 * pp], srcf))
            S.op(lambda E, pp=pp, srcf=srcf: E.tensor_tensor(T1.rearrange("p a b -> p (a b)"), srcf, Chl[2 * pp], ALU.subtract))
            S.op(lambda E, pp=pp: E.tensor_copy(Chl[2 * pp + 1], T1.rearrange("p a b -> p (a b)")))
        gh_done = []
        for tau in range(8 if KSTOP != 7 else 1):
            cmul(S, ABr, ABi, BBr, BBi, bcast_last(apow[tau][0], 32), bcast_last(apow[tau][1], 32), T1, T2)
            for pp, src_ in ((0, ABr), (1, ABi)):
                srcf = src_.rearrange("p a b -> p (a b)")
                S.op(lambda E, pp=pp, srcf=srcf: E.tensor_copy(ABh[pp], srcf))
                S.op(lambda E, pp=pp, srcf=srcf: E.tensor_tensor(T1.rearrange("p a b -> p (a b)"), srcf, ABh[pp], ALU.subtract))
                S.op(lambda E, pp=pp: E.tensor_copy(ABl[pp], T1.rearrange("p a b -> p (a b)")))
            t_hl = S.last
            t_ab = S.last
            ABrf = ABr.rearrange("p a b -> p (a b)")
            ABif = ABi.rearrange("p a b -> p (a b)")
            CPrf = CPr.rearrange("p a b -> p (a b)")
            CPnif = CPni.rearrange("p a b -> p (a b)")
            rd = []
            for part, src in ((0, ABrf), (1, ABif)):
                for half in range(2):
                    b = part * 2 + half
                    t_tr = None
                    for q in range(4):
                        ct = half * 4 + q
                        sl = slice(q * 128, q * 128 + 128)
                        t_tr = P.op("pe", lambda E, b=b, sl=sl, src=src, ct=ct: E.transpose(banks[b][:, sl], src[:, ct * 128:(ct + 1) * 128], ident_f),
                                    [t_ab, bfree[b]], inc=(q == 3))
                    t_cp = P.op("act", lambda E, b=b, tau=tau, part=part, half=half: E.copy(
                        G[:, tau, part, half * 4:half * 4 + 4, :].rearrange("p a b -> p (a b)"), banks[b][:, :]), [t_tr])
                    rd.append((b, t_cp))
            for half in range(2):
                b = 4 + half
                t_mm = None
                pairs = [(ABh[0], Chl[0]), (ABh[0], Chl[1]), (ABl[0], Chl[0]), (ABh[1], Chl[2]), (ABh[1], Chl[3]), (ABl[1], Chl[2])]
                for q in range(4):
                    ct = half * 4 + q
                    sl = slice(q * 128, q * 128 + 128)
                    for pi_, (aa, cc) in enumerate(pairs):
                        t_mm = P.op("pe", lambda E, b=b, sl=sl, ct=ct, aa=aa, cc=cc, pi_=pi_, q=q: E.matmul(
                            banks[b][:, sl], aa[:, ct * 128:(ct + 1) * 128], cc[:, ct * 128:(ct + 1) * 128],
                            start=(pi_ == 0 and q == 0), stop=(pi_ == 5), skip_group_check=True),
                            [t_hl, bfree[b]] if (q == 0 and pi_ == 0) else [], inc=(q == 3 and pi_ == 5))
                bm4 = bmask.unsqueeze(1).to_broadcast([128, 4, 128])
                bk4 = banks[b][:, :].rearrange("p (a b) -> p a b", b=128)
                if tau == 0:
                    t_e1 = P.op("dve", lambda E, bk4=bk4, bm4=bm4: E.tensor_tensor(tmpH4, bk4, bm4, ALU.mult), [t_mm, S.last])
                    rd.append((b, t_e1))
                    tl = t_e1
                    for q in range(4):
                        ct = half * 4 + q
                        tl = P.op("dve", lambda E, ct=ct, q=q: E.scalar_tensor_tensor(H[:, 0, ct, :], ident_f, gcol("ssm_d", ct), tmpH4[:, q, :], ALU.mult, ALU.add), [tl])
                    S.last = [tl]
                else:
                    t_ev = P.op("dve", lambda E, bk4=bk4, bm4=bm4, tau=tau, half=half: E.tensor_tensor(H[:, tau, half * 4:half * 4 + 4, :], bk4, bm4, ALU.mult), [t_mm])
                    rd.append((b, t_ev))
            for b in range(6):
                bfree[b] = [t for (bb, t) in rd if bb == b]
            gh_done.append([t for (_, t) in rd])
            S.last = [S.last, gh_done[-1]]
            if tau == 7:
                tap("ABr", ABr.rearrange("p a b -> p (a b)"), [128, 1024])
                tap("ABi", ABi.rearrange("p a b -> p (a b)"), [128, 1024])
                tap("CPr", CPr.rearrange("p a b -> p (a b)"), [128, 1024])
                tap("CPni", CPni.rearrange("p a b -> p (a b)"), [128, 1024])
                tap("Chl0", Chl[0], [128, 1024])
                tap("ABh0", ABh[0], [128, 1024])
        A.pop()
        P.barrier()
        tap("G", G.rearrange("p a b c d -> p (a b c d)"), [128, 8 * 2 * 8 * 128])
        tap("H", H.rearrange("p a b c -> p (a b c)"), [128, 8 * 8 * 128])
        tap("ctab", ctab.rearrange("p a b -> p (a b)"), [128, 32 * 128])
        tap("stab", stab.rearrange("p a b -> p (a b)"), [128, 32 * 128])
        tap("sm", sm.rearrange("p a b -> p (a b)"), [128, 48 * 32])

        ucs = [A.alloc([128, 8, 1024], BF16) for _ in range(2)]
        um_bufs = [A.alloc([128, 4, 1024], BF16) for _ in range(2)]
        wblk = A.alloc([128, 6, 512], F32)
        Wr, Wi, Sr, Si, t1a, t2a = [wblk[:, q, :].rearrange("p (a b) -> p a b", b=128) for q in range(6)]
        Zpr = A.alloc([128, 4, 128], BF16)
        Zpi = A.alloc([128, 4, 128], BF16)
        w0 = A.alloc([128, 6, 4], F32)
        udi = A.alloc([128, 1024], BF16)
        um_free = [None, None]
        ssm_state = {"dve": [], "zp_free": None}

        def norm_proj_chunk(c, wcol0, n_mt, dst_fn, save_scratch, env, nts=(0, 1)):
            xring, sqb, xg, rs, wring = env["xring"], env["sqb"], env["xg"], env["rs"], env["wring"]
            for nt in nts:
                tok0 = 1024 * c + 512 * nt
                last_mm = None
                xg_toks = []
                for kt in range(KT):
                    xi = env["xi"]; env["xi"] = (xi + 1) % len(xring)
                    xb = xring[xi]
                    t_ld = P.dma("sp", xb, xT[kt * 128:(kt + 1) * 128, tok0:tok0 + 512], [env["xfree"][xi]])
                    si = kt % 2
                    t_sq = P.op("act", lambda E, xb=xb, si=si: E.activation(sqb[si], xb, AF.Square), [t_ld, env["sqfree"][si]])
                    last_mm = P.op("pe", lambda E, si=si, kt=kt: E.matmul(banks[6][:, :], ones_b, sqb[si], start=(kt == 0), stop=(kt == KT - 1)),
                                   [t_sq, bfree[6] if kt == 0 else None])
                    env["sqfree"][si] = last_mm
                    t_xg = P.op("dve", lambda E, xb=xb, kt=kt: E.tensor_scalar(xg[:, kt, :], xb, gcol("mix_pre", kt), None, ALU.mult),
                                [t_ld, env["xg_free"]])
                    xg_toks.append(t_xg)
                    env["xfree"][xi] = [t_sq, t_xg]
                ta_ = P.op("act", lambda E: E.activation(rs, banks[6][:, :], AF.Sqrt, bias=epsc, scale=1.0 / D), [last_mm, env["rs_free"]])
                bfree[6] = ta_
                t_rs = P.op("dve", lambda E: E.reciprocal(rs, rs), [ta_])
                readers = []
                if save_scratch:
                    tcol = tok0 - 1024
                    readers.append(P.dma("sp", xg_s.rearrange("(kt p) t -> p kt t", p=128)[:, :, tcol:tcol + 512], xg, [xg_toks]))
                    readers.append(P.dma("sp", rs_s[:, tcol:tcol + 512], rs, [t_rs]))
                evs = []
                for m in range(n_mt):
                    if m % 2 == 0:
                        wi_ = env["wi"]; env["wi"] = (wi_ + 1) % len(wring)
                        wsl = wring[wi_]
                        t_w = P.dma("pool", wsl, w_in_v[:, :, wcol0 + m * 128:wcol0 + m * 128 + 256], [env["wfree"][wi_]])
                        env["wfree"][wi_] = []
                        cur = (wi_, wsl, t_w)
                    wi_, wsl, t_w = cur
                    b = 4 + (env["bi"] % 2); env["bi"] += 1
                    t_mm = None
                    for kt in range(KT):
                        t_mm = P.op("pe", lambda E, b=b, wsl=wsl, kt=kt, mo=(m % 2) * 128: E.matmul(banks[b][:, :], wsl[:, kt, mo:mo + 128], xg[:, kt, :],
                                                                                                  start=(kt == 0), stop=(kt == KT - 1)),
                                    [t_w, xg_toks, bfree[b]] if kt == 0 else [], inc=(kt == KT - 1))
                    env["wfree"][wi_].append(t_mm)
                    dst = dst_fn(m, nt)
                    t_ev = P.op("dve", lambda E, b=b, dst=dst: E.tensor_tensor(dst, banks[b][:, :], rs, ALU.mult), [t_mm, t_rs, env["dst_free"]])
                    bfree[b] = t_ev
                    evs.append(t_ev)
                    readers.append(t_mm)
                env["xg_free"] = readers
                env["rs_free"] = [evs, readers]
                env["done"].setdefault(c, []).append(evs)
            return

        def ssm_chunk(c, own, u_ready, uc, Fm=None, side=None):
            readers = []

            def do_ct(ct):
                ui = ct % 2
                um = um_bufs[ui]
                vb0, vb1 = (0, 1) if (own or ct % 2 == 0) else (2, 3)
                tm = []
                for j in range(4):
                    tm.append(P.op("act", lambda E, j=j, ct=ct: E.activation(um[:, j, :].rearrange("p (i k) -> p i k", k=128), uc[:, ct, :].rearrange("p (k i) -> p i k", i=8),
                                                                                AF.Copy, scale=pmask[:, j:j + 1]),
                                   [u_ready, um_free[ui]]))
                if own:
                    tm.append(P.op("act", lambda E, ct=ct: E.copy(udi.rearrange("p (i k) -> p i k", k=128), uc[:, ct, :].rearrange("p (k i) -> p i k", i=8)),
                                   [u_ready, ssm_state["zp_free"]]))
                t_v = None
                for part in range(2):
                    for i in range(8):
                        last = (part == 1 and i == 7)
                        vb = vb0 if part == 0 else vb1
                        t_v = P.op("pe", lambda E, part=part, i=i, ct=ct, vb=vb: E.matmul(banks[vb][:, :].rearrange("p (a b) -> p a b", b=128), G[:, 7 - i, part, ct, :], um[:, :, i * 128:(i + 1) * 128],
                                                                              start=(i == 0), stop=(i == 7)),
                                   [tm, bfree[vb0], bfree[vb1]] if (part == 0 and i == 0) else [], inc=last)
                um_free[ui] = t_v
                cs = ctab[:, 4 * ct:4 * ct + 4, :]
                sn = stab[:, 4 * ct:4 * ct + 4, :]
                Vr = banks[vb0][:, :].rearrange("p (a b) -> p a b", b=128)
                Vi = banks[vb1][:, :].rearrange("p (a b) -> p a b", b=128)
                prev = ssm_state["dve"]
                B0, B1, B2, B3, Sr_, Si_ = Wr, Wi, Sr, Si, t1a, t2a
                dv = lambda fn, deps: P.op("dve", fn, deps)
                zr_ = Zin_r[:, 4 * ct:4 * ct + 4]
                zi_ = Zin_i[:, 4 * ct:4 * ct + 4]
                er_ = e8r[:, 4 * ct:4 * ct + 4]
                ei_ = e8i[:, 4 * ct:4 * ct + 4]
                a1 = dv(lambda E: E.tensor_tensor(B0, Vr, cs, ALU.mult), [t_v, prev])
                a2 = dv(lambda E: E.tensor_tensor(B1, Vi, sn, ALU.mult), [t_v, prev])
                a3 = dv(lambda E: E.tensor_tensor(B2, Vi, cs, ALU.mult), [t_v, prev])
                a4 = dv(lambda E: E.tensor_tensor(B3, Vr, sn, ALU.mult), [t_v, prev])
                bfree[vb0] = [a1, a2, a3, a4]
                bfree[vb1] = [a1, a2, a3, a4]
                w1 = dv(lambda E: E.tensor_tensor(w0[:, 2, :], er_, zr_, ALU.mult), [prev])
                w2 = dv(lambda E: E.tensor_tensor(w0[:, 3, :], ei_, zi_, ALU.mult), [prev])
                w3 = dv(lambda E: E.tensor_tensor(w0[:, 4, :], er_, zi_, ALU.mult), [prev])
                w4 = dv(lambda E: E.tensor_tensor(w0[:, 5, :], ei_, zr_, ALU.mult), [prev])
                a5 = dv(lambda E: E.tensor_tensor(B0, B0, B1, ALU.add), [a1, a2])
                a6 = dv(lambda E: E.tensor_tensor(B2, B2, B3, ALU.subtract), [a3, a4])
                w5 = dv(lambda E: E.tensor_tensor(w0[:, 0, :], w0[:, 2, :], w0[:, 3, :], ALU.subtract), [w1, w2])
                w6 = dv(lambda E: E.tensor_tensor(w0[:, 1, :], w0[:, 4, :], w0[:, 5, :], ALU.add), [w3, w4])
                sr_t, si_t = [], []
                for j in range(4):
                    pr = 4 * ct + j
                    d0 = rho8[:, pr:pr + 1].to_broadcast([128, 128])
                    sr_t.append(dv(lambda E, j=j, d0=d0: E.tensor_tensor_scan(Sr_[:, j, :], d0, B0[:, j, :], w0[:, 0, j:j + 1], ALU.mult, ALU.add), [a5, w5, prev]))
                    si_t.append(dv(lambda E, j=j, d0=d0: E.tensor_tensor_scan(Si_[:, j, :], d0, B2[:, j, :], w0[:, 1, j:j + 1], ALU.mult, ALU.add), [a6, w6, prev]))
                c0 = []
                if own:
                    c0.append(dv(lambda E: E.tensor_copy(Zpr[:, :, 0:1], zr_.unsqueeze(2)), [ssm_state["zp_free"], prev]))
                    c0.append(dv(lambda E: E.tensor_copy(Zpi[:, :, 0:1], zi_.unsqueeze(2)), [ssm_state["zp_free"], prev]))
                c127 = cs[:, :, 127:128]
                s127 = sn[:, :, 127:128]
                z1 = dv(lambda E: E.tensor_tensor(w0[:, 2, :].unsqueeze(2), c127, Sr_[:, :, 127:128], ALU.mult), [sr_t, w5])
                z2 = dv(lambda E: E.tensor_tensor(w0[:, 3, :].unsqueeze(2), s127, Si_[:, :, 127:128], ALU.mult), [si_t, w5])
                z3 = dv(lambda E: E.tensor_tensor(w0[:, 4, :].unsqueeze(2), s127, Sr_[:, :, 127:128], ALU.mult), [sr_t, w6])
                z4 = dv(lambda E: E.tensor_tensor(w0[:, 5, :].unsqueeze(2), c127, Si_[:, :, 127:128], ALU.mult), [si_t, w6])
                z5 = dv(lambda E: E.tensor_tensor(zr_, w0[:, 2, :], w0[:, 3, :], ALU.subtract), [z1, z2, w1, w4, c0])
                z6 = dv(lambda E: E.tensor_tensor(zi_, w0[:, 4, :], w0[:, 5, :], ALU.add), [z3, z4, w2, w3, c0])
                all_t = [a1, a2, a3, a4, w1, w2, w3, w4, a5, a6, w5, w6, sr_t, si_t, c0, z1, z2, z3, z4, z5, z6]
                if own:
                    cs7 = cs[:, :, 0:127]; sn7 = sn[:, :, 0:127]
                    d1 = dv(lambda E: E.tensor_tensor(B1[:, :, 0:127], cs7, Sr_[:, :, 0:127], ALU.mult), [sr_t, a5])
                    d2 = dv(lambda E: E.tensor_tensor(B3[:, :, 0:127], sn7, Si_[:, :, 0:127], ALU.mult), [si_t, a6])
                    d3 = dv(lambda E: E.tensor_tensor(Zpr[:, :, 1:128], B1[:, :, 0:127], B3[:, :, 0:127], ALU.subtract), [d1, d2, ssm_state["zp_free"]])
                    d4 = dv(lambda E: E.tensor_tensor(B1[:, :, 0:127], sn7, Sr_[:, :, 0:127], ALU.mult), [d3])
                    d5 = dv(lambda E: E.tensor_tensor(B3[:, :, 0:127], cs7, Si_[:, :, 0:127], ALU.mult), [d3])
                    d6 = dv(lambda E: E.tensor_tensor(Zpi[:, :, 1:128], B1[:, :, 0:127], B3[:, :, 0:127], ALU.add), [d4, d5])
                    t_zp = [d3, d6, c0]
                    all_t += [d1, d2, d3, d4, d5, d6]

                class _L:
                    last = all_t
                Sq = _L()
                if own:
                    t_y = [None, None]
                    for hb in range(2):
                        b = 2 + hb
                        first = True
                        for j in range(4 * hb, 4 * hb + 4):
                            jc = (j % 4) * 128
                            for i in range(j + 1):
                                P.op("pe", lambda E, b=b, j=j, i=i, ct=ct, jc=jc, first=first: E.matmul(banks[b][:, jc:jc + 128], H[:, j - i, ct, :], udi[:, i * 128:(i + 1) * 128],
                                                                                                 start=first, stop=False, skip_group_check=True),
                                     [u_ready, bfree[b], t_zp, tm] if first else [], inc=False)
                                first = False
                            for pl in range(4):
                                for part in range(2):
                                    zp = Zpr if part == 0 else Zpi
                                    lastm = (j == 4 * hb + 3 and pl == 3 and part == 1)
                                    t_ = P.op("pe", lambda E, b=b, j=j, pl=pl, part=part, zp=zp, ct=ct, jc=jc: E.matmul(
                                        banks[b][32 * pl:32 * pl + 32, jc:jc + 128], Fm[:, j, part, (4 * ct + pl) * 32:(4 * ct + pl) * 32 + 32], zp[:, pl, :],
                                        start=False, stop=(pl == 3 and part == 1), tile_position=(0, 32 * pl), skip_group_check=True), [], inc=lastm)
                                    if lastm:
                                        t_y[hb] = t_
                    ssm_state["zp_free"] = t_y[1]
                    readers.append(t_y[1])
                    yi = ssm_state.get("yi", 0)
                    ssm_state["yi"] = yi + 1
                    yrow = yring[yi % 2]
                    t_fin = []
                    for hb in range(2):
                        b = 2 + hb
                        yb = banks[b][:, :]
                        dst = yrow.rearrange("p (k j) -> p j k", j=8)[:, 4 * hb:4 * hb + 4, :]
                        G1 = gtmp[0]; G2 = gtmp[1]
                        ta1 = P.op("act", lambda E, yb=yb, G1=G1: E.activation(G1, yb, AF.Square), [t_y[hb], ssm_state.get("g_free")])
                        tb1 = P.op("dve", lambda E, G1=G1: E.tensor_scalar(G1, G1, 0.044715, 1.0, ALU.mult, ALU.add), [ta1, Sq.last])
                        tb2 = P.op("dve", lambda E, G1=G1, yb=yb: E.tensor_tensor(G1, G1, yb, ALU.mult), [tb1])
                        ta2 = P.op("act", lambda E, G1=G1, G2=G2: E.activation(G2, G1, AF.Sigmoid, scale=1.5957691216057308), [tb2])
                        tb3 = P.op("dve", lambda E, G2=G2, yb=yb, dst=dst: E.tensor_tensor(dst, yb.rearrange("p (j k) -> p j k", k=128), G2.rearrange("p (j k) -> p j k", k=128), ALU.mult),
                                   [ta2, yfree[yi % 2]])
                        bfree[b] = tb3
                        ssm_state["g_free"] = tb3
                        Sq.last = [tb3]
                        t_fin.append(tb3)
                    yfree[yi % 2] = P.dma("sp", yg_s[ct * 128:(ct + 1) * 128, :], yrow, [t_fin])
                else:
                    readers.append(t_v)
                ssm_state["dve"] = Sq.last

            for ct_ in range(8):
                do_ct(ct_)
                if side and ct_ in side:
                    side[ct_]()
            return readers

        A.push()
        env = {"xring": [A.alloc([128, 512], F32) for _ in range(4)], "sqb": [A.alloc([128, 512], BF16) for _ in range(2)],
               "xg": A.alloc([128, KT, 512], BF16), "rs": A.alloc([128, 512], F32),
               "wring": [A.alloc([128, KT, 256], BF16) for _ in range(3)],
               "xi": 0, "wi": 0, "bi": 0, "xfree": [None] * 4, "sqfree": [None] * 2, "wfree": [[], [], []],
               "xg_free": None, "rs_free": None, "dst_free": None}
        env["done"] = {}
        uc_readers = [None, None]

        def proj_job(c, nts):
            def f():
                ucb = ucs[c % 2]
                env["dst_free"] = uc_readers[c % 2]
                norm_proj_chunk(c, 3072, 8, lambda m, nt, ucb=ucb: ucb[:, m, 512 * nt:512 * nt + 512], c >= 1, env, nts=nts)
            return f
        proj_job(0, (0, 1))()
        for c in range(NCH - 1):
            if KSIDE:
                side = {1: proj_job(c + 1, (0,)), 4: proj_job(c + 1, (1,))}
                uc_readers[c % 2] = ssm_chunk(c, False, env["done"][c], ucs[c % 2], side=side)
            else:
                uc_readers[c % 2] = ssm_chunk(c, False, env["done"][c], ucs[c % 2])
                proj_job(c + 1, (0, 1))()
        P.barrier()
        A.pop()
        if "u_own" in taps:
            tap("u_own", ucs[(NCH - 1) % 2].rearrange("p a b -> p (a b)"), [128, 8 * 1024])
        tap("zin", sm.rearrange("p a b -> p (a b)"), [128, 48 * 32])
        Fm = A.alloc([128, 8, 2, 1024], BF16)
        yring = [A.alloc([128, 1024], F32) for _ in range(2)]
        yfree = [None, None]
        gtmp = [A.alloc([128, 512], F32) for _ in range(2)]
        A.push()
        Ft1b = wblk[:, 0:2, :].rearrange("p a (b c) -> p (a b) c", c=32)
        Ft2b = wblk[:, 2:4, :].rearrange("p a (b c) -> p (a b) c", c=32)
        CPr2 = A.alloc([128, 32, 32], F32)
        CPni2 = A.alloc([128, 32, 32], F32)
        CPi02 = wblk[:, 4:6, :].rearrange("p a (b c) -> p (a b) c", c=32)
        t_cpr2 = P.dma("sp", CPr2, cpr_d.rearrange("p (a b) -> p a b", b=32))
        t_cpi2 = P.dma("sp", CPi02, cpi_d.rearrange("p (a b) -> p a b", b=32))
        SF = Seq(P, "dve", [t_cpr2, t_cpi2])
        SF.op(lambda E: E.memset(CPni2, 0.0))
        SF.op(lambda E: E.tensor_tensor(CPni2, CPni2, CPi02, ALU.subtract))
        for j in range(8):
            pr_ = bcast_last(apow[j + 1][0], 32)
            pi_ = bcast_last(apow[j + 1][1], 32)
            SF.op(lambda E, pr_=pr_: E.tensor_tensor(Ft1b, CPr2, pr_, ALU.mult))
            SF.op(lambda E, pi_=pi_: E.tensor_tensor(Ft2b, CPni2, pi_, ALU.mult))
            SF.op(lambda E, j=j: E.tensor_tensor(Fm[:, j, 0, :].rearrange("p (a b) -> p a b", b=32), Ft1b, Ft2b, ALU.add))
            SF.op(lambda E, pr_=pr_: E.tensor_tensor(Ft1b, CPni2, pr_, ALU.mult))
            SF.op(lambda E, pi_=pi_: E.tensor_tensor(Ft2b, CPr2, pi_, ALU.mult))
            SF.op(lambda E, j=j: E.tensor_tensor(Fm[:, j, 1, :].rearrange("p (a b) -> p a b", b=32), Ft1b, Ft2b, ALU.subtract))
        P.barrier()
        A.pop()
        ssm_chunk(NCH - 1, True, P.all_tokens(), ucs[(NCH - 1) % 2], Fm=Fm)
        P.barrier()
        A.pop()

        def make_wring(n):
            return {"slots": [A.alloc([128, 16, 256], BF16) for _ in range(n)], "free": [[] for _ in range(n)], "i": 0}

        def linear(w_view, KTn, m_tiles, rhs_fn, evac_fn, wr, bank_sets, rhs_deps):
            nkg = KTn // 16
            for pi_ in range(0, len(m_tiles), 2):
                mp = m_tiles[pi_:pi_ + 2]
                bset = bank_sets[(pi_ // 2) % len(bank_sets)]
                lastmm = {}
                for g in range(nkg):
                    wi_ = wr["i"]; wr["i"] = (wi_ + 1) % len(wr["slots"])
                    wsl = wr["slots"][wi_]
                    t_w = P.dma("pool", wsl[:, :, 0:128 * len(mp)], w_view[:, g * 16:(g + 1) * 16, mp[0] * 128:mp[0] * 128 + 128 * len(mp)], [wr["free"][wi_]])
                    wr["free"][wi_] = []
                    for mi in range(len(mp)):
                        for nt in range(2):
                            b = bset[mi * 2 + nt]
                            t_mm = None
                            for kl in range(16):
                                kt = g * 16 + kl
                                firstb = (g == 0 and kl == 0)
                                t_mm = P.op("pe", lambda E, b=b, wsl=wsl, kl=kl, mi=mi, kt=kt, nt=nt, firstb=firstb, g=g: E.matmul(
                                    banks[b][:, :], wsl[:, kl, mi * 128:mi * 128 + 128], rhs_fn(kt, nt), start=firstb, stop=(g == nkg - 1 and kl == 15)),
                                    ([t_w, rhs_deps, bfree[b]] if firstb else ([t_w] if kl == 0 else [])), inc=(kl == 15))
                            lastmm[(mi, nt)] = t_mm
                            wr["free"][wi_].append(t_mm)
                for mi, m in enumerate(mp):
                    for nt in range(2):
                        b = bset[mi * 2 + nt]
                        bfree[b] = evac_fn(m, nt, b, lastmm[(mi, nt)])

        def stats_of(src_fn, ntiles, rs_out, nfeat, sqb_, deps, stat_banks=(6, 7)):
            return rms_stats(src_fn, ntiles, 1024, rs_out, nfeat, sqb_, list(stat_banks), deps)

        A.push()
        mixed = A.alloc([128, 16, 1024], BF16)
        A.push()
        yg = A.alloc([128, 8, 1024], F32)
        ygb = A.alloc([128, 8, 1024], BF16)
        wg = A.alloc([128, 8, 1024], BF16)
        gt2 = [A.alloc([128, 512], F32) for _ in range(2)]
        sqb2 = [A.alloc([128, 512], BF16) for _ in range(2)]
        rs2 = A.alloc([128, 1024], F32)
        t_wg = P.dma("pool", wg, w_glu.rearrange("(kt p) m -> p kt m", p=128))
        t_yb = []
        for ct in range(8):
            t_l = P.dma("sp", yg[:, ct, :], yg_s[ct * 128:(ct + 1) * 128, :])
            t_yb.append(P.op("act", lambda E, ct=ct: E.copy(ygb[:, ct, :], yg[:, ct, :]), [t_l]))
        tap("yg", yg.rearrange("p a b -> p (a b)"), [128, 8 * 1024])
        gfree = [None, None]
        glu_done = []
        for m in range(8):
            for nt in range(2):
                b = (m * 2 + nt) % 4
                t_mm = None
                for kt in range(8):
                    t_mm = P.op("pe", lambda E, b=b, kt=kt, m=m, nt=nt: E.matmul(banks[b][:, :], wg[:, kt, m * 128:(m + 1) * 128], ygb[:, kt, nt * 512:(nt + 1) * 512],
                                                                              start=(kt == 0), stop=(kt == 7)),
                                [t_wg, t_yb, bfree[b]] if kt == 0 else [], inc=(kt == 7))
                gi = (m * 2 + nt) % 2
                t_s = P.op("act", lambda E, b=b, gi=gi, m=m: E.activation(gt2[gi], banks[b][:, :], AF.Sigmoid, bias=gcol("b_glu", m)), [t_mm, gfree[gi]])
                bfree[b] = t_s
                t_g = P.op("dve", lambda E, gi=gi, m=m, nt=nt: E.tensor_tensor(yg[:, m, nt * 512:(nt + 1) * 512], yg[:, m, nt * 512:(nt + 1) * 512], gt2[gi], ALU.mult), [t_s])
                gfree[gi] = t_g
                glu_done.append(t_g)
        tap("ssm", yg.rearrange("p a b -> p (a b)"), [128, 8 * 1024])
        t_rs2 = stats_of(lambda i, nt: yg[:, i, nt * 512:(nt + 1) * 512], 8, rs2, 1024.0, sqb2, glu_done)
        for ct in range(8):
            for nt in range(2):
                P.op("dve", lambda E, ct=ct, nt=nt: E.scalar_tensor_tensor(mixed[:, 8 + ct, nt * 512:(nt + 1) * 512], yg[:, ct, nt * 512:(nt + 1) * 512],
                                                                          gcol("ssm_n", ct), rs2[:, nt * 512:(nt + 1) * 512], ALU.mult, ALU.mult), [t_rs2])
        P.barrier()
        A.pop()

        A.push()
        attnT = A.alloc([128, 8, 1024], F32)
        head_state = {"gi": 0, "vtok_free": None, "rec_free": None}
        SCALE = 1.0 / math.sqrt(128.0)
        xg_sv = xg_s.rearrange("(kt p) t -> p kt t", p=128)
        for hg in range(2):
            A.push()
            KTt = A.alloc([128, 4, 3072], BF16)
            VTt = A.alloc([128, 4, 3072], BF16)
            QTt = A.alloc([128, 4, 1024], BF16)
            A.push()
            xg3 = A.alloc([128, 16, 1024], BF16)
            rs3 = A.alloc([128, 1024], F32)
            wr3 = make_wring(3)
            for sc in range(3):
                t_x3 = P.dma("sp", xg3, xg_sv[:, :, 1024 * sc:1024 * sc + 1024], [P.all_tokens()] if (sc > 0 or hg > 0) else [])
                t_r3 = P.dma("sp", rs3, rs_s[:, 1024 * sc:1024 * sc + 1024], [P.all_tokens()] if (sc > 0 or hg > 0) else [])
                jobs = [(1024 + 512 * hg, KTt), (1024 + 512 * hg + 256, KTt), (2048 + 512 * hg, VTt), (2048 + 512 * hg + 256, VTt)]
                if sc == 2:
                    jobs += [(512 * hg, QTt), (512 * hg + 256, QTt)]
                for (wc0, dstT) in jobs:
                    hl0 = ((wc0 % 1024) - 512 * hg) // 128

                    def ev3(m, nt, b, t_mm, dstT=dstT, hl0=hl0, sc=sc):
                        hl = hl0 + m
                        if dstT is QTt:
                            dst = dstT[:, hl, nt * 512:(nt + 1) * 512]
                        else:
                            dst = dstT[:, hl, 1024 * sc + nt * 512:1024 * sc + (nt + 1) * 512]
                        return P.op("dve", lambda E, dst=dst, b=b, nt=nt: E.tensor_tensor(dst, banks[b][:, :], rs3[:, nt * 512:(nt + 1) * 512], ALU.mult), [t_mm, t_r3])
                    linear(w_in_v[:, :, wc0:wc0 + 256], 16, [0, 1], lambda kt, nt: xg3[:, kt, nt * 512:(nt + 1) * 512], ev3, wr3,
                           [[0, 1, 2, 3], [4, 5, 6, 7]], [t_x3])
            P.barrier()
            A.pop()
            if hg == 0:
                tap("KT0", KTt[:, 0, :], [128, 3072])
                tap("VT0", VTt[:, 0, :], [128, 3072])
                tap("QT0", QTt[:, 0, :], [128, 1024])
            A.push()
            Vtoks = [A.alloc([128, NVT + 3, 128], BF16) for _ in range(2)]
            NES = 6
            es = [A.alloc([128, 4, 128], BF16) for _ in range(NES)]
            pm = [A.alloc([128, 4, 128], BF16) for _ in range(NES)]
            rec = A.alloc([128, 512], F32)
            TB = banks[0][:, :].bitcast(BF16)
            es_free = [None] * NES
            pm_free = [None] * NES
            vtok_free = [None, None]
            for hl in range(4):
                h = 4 * hg + hl

                def do_head(hl, h):
                    Vtok = Vtoks[hl % 2]
                    vt_toks = []
                    for g0 in range(0, NVT, 8):
                        n = min(8, NVT - g0)
                        t_tr = None
                        for q in range(n):
                            d, r, m, nk = VT_LIST[g0 + q]
                            te0 = 2048 + r + d * m
                            t_tr = P.op("pe", lambda E, q=q, te0=te0, d=d, nk=nk: E.transpose(TB[0:nk, q * 128:(q + 1) * 128], VTt[:, hl, te0:te0 + d * (nk - 1) + 1:d], ident_b),
                                        [bfree[0], vtok_free[hl % 2]] if q == 0 else [], inc=(q == n - 1))
                        t_cp = P.op("act", lambda E, g0=g0, n=n: E.copy(Vtok[:, g0:g0 + n, :].rearrange("p a b -> p (a b)"), TB[:, 0:128 * n]), [t_tr])
                        bfree[0] = t_cp
                        vt_toks.append(t_cp)
                    tiles = []
                    for d in (1, 4, 16):
                        QB = min(128, NO // d)
                        for r in range(d):
                            for blk in range((NO // d) // QB):
                                m0 = blk * QB
                                for (mk, nk, mask) in ((m0 - 128, 128, mprev_b), (m0, QB, mcur_b)):
                                    vt = VT_IDX[(d, r, mk)] if (d, r, mk) in VT_IDX else None
                                    if vt is None:
                                        raise AssertionError((d, r, mk))
                                    outs = []
                                    if d == 16:
                                        outs = [(0, r, 16, 0, 32), (1, r, 16, 32, 64)]
                                    elif d == 4:
                                        outs = [(blk, r, 4, 0, 128)]
                                    else:
                                        outs = [(blk // 4, (blk % 4) * 128, 1, 0, 128)]
                                    tiles.append(dict(d=d, r=r, m0=m0, mk=mk, nk=nk, QB=QB, mask=mask, vt=vt, outs=outs))
                    started = set()
                    tix = {(T["d"], T["r"], T["m0"], T["mk"]): T for T in tiles}
                    groups = []
                    for half in range(2):
                        for ab in range(2):
                            grp = [tix[(1, 0, 128 * blk, 128 * blk - 128 if ab == 0 else 128 * blk)] for blk in range(4 * half, 4 * half + 4)]
                            groups.append((grp, ("d1", half)))
                    for blk in range(2):
                        for ab in range(2):
                            grp = [tix[(4, r, 128 * blk, 128 * blk - 128 if ab == 0 else 128 * blk)] for r in range(4)]
                            groups.append((grp, ("d4", blk)))
                    for r0 in range(0, 16, 4):
                        for ab in range(2):
                            grp = [tix[(16, r, 0, -128 if ab == 0 else 0)] for r in range(r0, r0 + 4)]
                            groups.append((grp, ("d16", r0)))
                    assert sum(len(g[0]) for g in groups) == len(tiles)
                    for (grp, gkind) in groups:
                        gi = (head_state["gi"]) % NES
                        sb_ = (1, 2, 7)[head_state["gi"] % 3]
                        head_state["gi"] += 1
                        t_s = None
                        for q, T in enumerate(grp):
                            d, r = T["d"], T["r"]
                            k0 = 2048 + r + d * T["mk"]
                            q0 = r + d * T["m0"]
                            t_s = P.op("pe", lambda E, q=q, k0=k0, q0=q0, d=d, nk=T["nk"], QB=T["QB"], sb_=sb_: E.matmul(
                                banks[sb_][0:nk, q * 128:q * 128 + QB], KTt[:, hl, k0:k0 + d * (nk - 1) + 1:d], QTt[:, hl, q0:q0 + d * (QB - 1) + 1:d],
                                start=True, stop=True, skip_group_check=True), [bfree[sb_]] if q == 0 else [], inc=(q == len(grp) - 1))
                        ng = len(grp)
                        t_e = P.op("act", lambda E, gi=gi, sb_=sb_, ng=ng: E.activation(es[gi][:, 0:ng, :].rearrange("p a b -> p (a b)"), banks[sb_][:, 0:128 * ng], AF.Exp, scale=SCALE),
                                   [t_s, es_free[gi]])
                        bfree[sb_] = t_e
                        t_ms = []
                        for q, T in enumerate(grp):
                            nk, QB = T["nk"], T["QB"]
                            t_ms.append(P.op("dve", lambda E, gi=gi, q=q, nk=nk, QB=QB, mask=T["mask"], vt=T["vt"]: E.scalar_tensor_tensor(
                                pm[gi][0:nk, q, 0:QB], es[gi][0:nk, q, 0:QB], vcols[0:nk, vt:vt + 1], mask[0:nk, 0:QB], ALU.mult, ALU.mult),
                                [t_e, pm_free[gi]] if q == 0 else [t_e]))
                        es_free[gi] = t_ms
                        t_pv = None
                        for q, T in enumerate(grp):
                            nk = T["nk"]
                            for (half, off, step, c0, c1) in T["outs"]:
                                ncol = c1 - c0
                                ob = 3 + half
                                st = ob not in started
                                started.add(ob)
                                lhs = Vtok[0:nk, T["vt"], :]
                                t_pv = P.op("pe", lambda E, ob=ob, off=off, step=step, ncol=ncol, lhs=lhs, gi=gi, q=q, nk=nk, c0=c0, c1=c1, st=st: E.matmul(
                                    banks[ob][:, off:off + step * (ncol - 1) + 1:step], lhs, pm[gi][0:nk, q, c0:c1], start=st, stop=False, skip_group_check=True),
                                    [t_ms, vt_toks, bfree[ob]] if st else [t_ms[q]], inc=True)
                        nkg = grp[0]["nk"]
                        if gkind[0] == "d1":
                            dens = [(5 + gkind[1], banks[5 + gkind[1]][:, :].rearrange("p (a b) -> p a b", b=128), pm[gi][0:nkg, :, :])]
                        elif gkind[0] == "d4":
                            dens = [(5 + gkind[1], banks[5 + gkind[1]][:, :].rearrange("p (i r) -> p r i", r=4), pm[gi][0:nkg, :, :])]
                        else:
                            r0 = gkind[1]
                            dens = [(5 + hf, banks[5 + hf][:, :].rearrange("p (i r) -> p r i", r=16)[:, r0:r0 + 4, :], pm[gi][0:nkg, :, 32 * hf:32 * hf + 32]) for hf in range(2)]
                        for (ob, oap, rap) in dens:
                            st = ob not in started
                            started.add(ob)
                            t_pv = P.op("pe", lambda E, oap=oap, rap=rap, nkg=nkg, st=st: E.matmul(oap, ones_b[0:nkg, :], rap, start=st, stop=False, skip_group_check=True),
                                        [t_ms, bfree[ob]] if st else [t_ms], inc=True)
                        pm_free[gi] = t_pv
                    fin = []
                    for half in range(2):
                        t_r = P.op("dve", lambda E, half=half: E.reciprocal(rec, banks[5 + half][:, :]), [t_pv, head_state["rec_free"]])
                        t_f = P.op("dve", lambda E, half=half: E.tensor_tensor(attnT[:, h, half * 512:(half + 1) * 512], banks[3 + half][:, :], rec, ALU.mult), [t_r])
                        head_state["rec_free"] = t_f
                        bfree[5 + half] = t_r
                        bfree[3 + half] = t_f
                        fin.append(t_f)
                    vtok_free[hl % 2] = t_pv
                do_head(hl, h)
            P.barrier()
            A.pop()
            A.pop()
        tap("attn", attnT.rearrange("p a b -> p (a b)"), [128, 8 * 1024])
        A.push()
        sqb4 = [A.alloc([128, 512], BF16) for _ in range(2)]
        rs4 = A.alloc([128, 1024], F32)
        t_rs4 = stats_of(lambda i, nt: attnT[:, i, nt * 512:(nt + 1) * 512], 8, rs4, 1024.0, sqb4, [])
        for hh in range(8):
            for nt in range(2):
                P.op("dve", lambda E, hh=hh, nt=nt: E.scalar_tensor_tensor(mixed[:, hh, nt * 512:(nt + 1) * 512], attnT[:, hh, nt * 512:(nt + 1) * 512],
                                                                          gcol("attn_n", hh), rs4[:, nt * 512:(nt + 1) * 512], ALU.mult, ALU.mult), [t_rs4])
        P.barrier()
        A.pop()
        A.pop()
        tap("mixed", mixed.rearrange("p a b -> p (a b)"), [128, 16 * 1024])

        def residual_pass(src_load, base_load, gname, rs_, dst_store, tmp_ring, deps, inplace_src=None):
            fr = [None] * len(tmp_ring)
            k = 0
            outs = []
            for m in range(16):
                for nt in range(2):
                    a_, b_ = tmp_ring[k % len(tmp_ring)]
                    if inplace_src is not None:
                        a_ = inplace_src(m, nt)
                        t_a = None
                    else:
                        t_a = src_load(m, nt, a_, [fr[k % len(tmp_ring)], deps])
                    t_b = base_load(m, nt, b_, [fr[k % len(tmp_ring)], deps])
                    t1_ = P.op("dve", lambda E, a_=a_, m=m, nt=nt: E.scalar_tensor_tensor(a_, a_, gcol(gname, m), rs_[:, nt * 512:(nt + 1) * 512], ALU.mult, ALU.mult), [t_a, deps])
                    t2_ = P.op("dve", lambda E, a_=a_, b_=b_: E.tensor_tensor(b_, b_, a_, ALU.add), [t1_, t_b])
                    t3_ = dst_store(m, nt, b_, [t2_])
                    fr[k % len(tmp_ring)] = t3_
                    outs.append(t3_)
                    k += 1
            return outs

        def load_full(dst, src_dram, deps):
            toks = []
            for m in range(16):
                toks.append(P.dma("sp", dst[:, m, :], src_dram[m * 128:(m + 1) * 128, :], deps))
            return toks

        def prenorm(hT_, gname, hn_, sqb_, rs_, deps):
            t_rs = stats_of(lambda i, nt: hT_[:, i, nt * 512:(nt + 1) * 512], 16, rs_, float(D), sqb_, deps)
            toks = []
            for m in range(16):
                for nt in range(2):
                    toks.append(P.op("dve", lambda E, m=m, nt=nt: E.scalar_tensor_tensor(hn_[:, m, nt * 512:(nt + 1) * 512], hT_[:, m, nt * 512:(nt + 1) * 512],
                                                                                         gcol(gname, m), rs_[:, nt * 512:(nt + 1) * 512], ALU.mult, ALU.mult), [t_rs]))
            return toks

        A.push()
        mixT = A.alloc([128, 16, 1024], F32)
        wr4 = make_wring(3)
        sqb5 = [A.alloc([128, 512], BF16) for _ in range(2)]
        rs5 = A.alloc([128, 1024], F32)

        def ev4(m, nt, b, t_mm):
            return P.op("act", lambda E, m=m, nt=nt, b=b: E.copy(mixT[:, m, nt * 512:(nt + 1) * 512], banks[b][:, :]), [t_mm])
        linear(w_out.rearrange("(kt p) m -> p kt m", p=128), 16, list(range(16)), lambda kt, nt: mixed[:, kt, nt * 512:(nt + 1) * 512], ev4, wr4,
               [[0, 1, 2, 3], [4, 5, 6, 7]], [])
        P.barrier()
        t_rs5 = stats_of(lambda i, nt: mixT[:, i, nt * 512:(nt + 1) * 512], 16, rs5, float(D), sqb5, [])
        xoff = NE - NO
        xring4 = [A.alloc([128, 512], F32) for _ in range(4)]
        x4free = [None] * 4
        h_toks = []
        k4 = 0
        for m in range(16):
            for nt in range(2):
                sl = slice(nt * 512, (nt + 1) * 512)
                xb = xring4[k4 % 4]
                t_x = P.dma("sp", xb, xT[m * 128:(m + 1) * 128, xoff + nt * 512:xoff + (nt + 1) * 512], [x4free[k4 % 4]])
                t1_ = P.op("dve", lambda E, m=m, sl=sl: E.scalar_tensor_tensor(mixT[:, m, sl], mixT[:, m, sl], gcol("mix_post", m), rs5[:, sl], ALU.mult, ALU.mult), [t_rs5])
                t2_ = P.op("dve", lambda E, m=m, sl=sl, xb=xb: E.tensor_tensor(mixT[:, m, sl], mixT[:, m, sl], xb, ALU.add), [t1_, t_x])
                x4free[k4 % 4] = t2_
                P.dma("pool", h_s[m * 128:(m + 1) * 128, sl], mixT[:, m, sl], [t2_])
                h_toks.append(t2_)
                k4 += 1
        tap("h1", mixT.rearrange("p a b -> p (a b)"), [128, 16 * 1024])
        hn2 = mixed
        rs5b = A.alloc([128, 1024], F32)
        prenorm(mixT, "mlp_pre", hn2, sqb5, rs5b, lambda i, nt: h_toks[2 * i + nt])
        P.barrier()
        A.pop()

        act = A.alloc([128, 64, 1024], BF16)
        wr5 = make_wring(3)
        rtmp = [A.alloc([128, 512], F32) for _ in range(3)]
        rfree = [None] * 3
        rk = [0]

        def ev_up(m, nt, b, t_mm):
            i = rk[0] % 3
            rk[0] += 1
            t_r = P.op("act", lambda E, i=i, b=b: E.activation(rtmp[i], banks[b][:, :], AF.Relu), [t_mm, rfree[i]])
            t_q = P.op("dve", lambda E, i=i, m=m, nt=nt: E.tensor_tensor(act[:, m, nt * 512:(nt + 1) * 512], rtmp[i], rtmp[i], ALU.mult), [t_r])
            rfree[i] = t_q
            return t_r
        linear(w_up.rearrange("(kt p) m -> p kt m", p=128), 16, list(range(64)), lambda kt, nt: hn2[:, kt, nt * 512:(nt + 1) * 512], ev_up, wr5,
               [[0, 1, 2, 3], [4, 5, 6, 7]], [])
        P.barrier()

        def ev_dn(m, nt, b, t_mm):
            i = rk[0] % 3
            rk[0] += 1
            t_c = P.op("act", lambda E, i=i, b=b: E.copy(rtmp[i], banks[b][:, :]), [t_mm, rfree[i]])
            rfree[i] = P.dma("sp", ff_s[m * 128:(m + 1) * 128, nt * 512:(nt + 1) * 512], rtmp[i], [t_c])
            return t_c
        linear(w_down.rearrange("(kt p) m -> p kt m", p=128), 64, list(range(16)), lambda kt, nt: act[:, kt, nt * 512:(nt + 1) * 512], ev_dn, wr5,
               [[0, 1, 2, 3], [4, 5, 6, 7]], [])
        P.barrier()
        A.pop()

        A.push()
        h2 = A.alloc([128, 16, 1024], F32)
        prod = A.alloc([128, 16, 1024], F32)
        hn3 = A.alloc([128, 16, 1024], BF16)
        sqb7 = [A.alloc([128, 512], BF16) for _ in range(2)]
        rs7 = A.alloc([128, 1024], F32)
        t_ff = load_full(prod, ff_s, [])
        t_h2 = load_full(h2, h_s, [])
        t_rs7 = stats_of(lambda i, nt: prod[:, i, nt * 512:(nt + 1) * 512], 16, rs7, float(D), sqb7, lambda i, nt: t_ff[i])
        res_t = []
        for m in range(16):
            for nt in range(2):
                sl = slice(nt * 512, (nt + 1) * 512)
                t1_ = P.op("dve", lambda E, m=m, sl=sl: E.scalar_tensor_tensor(prod[:, m, sl], prod[:, m, sl], gcol("mlp_post", m), rs7[:, sl], ALU.mult, ALU.mult), [t_rs7])
                res_t.append(P.op("dve", lambda E, m=m, sl=sl: E.tensor_tensor(h2[:, m, sl], h2[:, m, sl], prod[:, m, sl], ALU.add), [t1_, t_h2]))
        P.barrier()
        tap("h2", h2.rearrange("p a b -> p (a b)"), [128, 16 * 1024])
        rs7b = A.alloc([128, 1024], F32)
        prenorm(h2, "ple_pre", hn3, sqb7, rs7b, lambda i, nt: res_t[2 * i + nt])
        P.barrier()
        wr6 = make_wring(2)
        wpp = A.alloc([128, 2, D], BF16)
        pTb = A.alloc([128, 2, 1024], BF16)
        gt6 = [A.alloc([128, 512], F32) for _ in range(2)]
        g6free = [None, None]
        g6k = [0]
        t_wpp = P.dma("pool", wpp, w_pp.rearrange("(kt p) m -> p kt m", p=128))
        t_pT = P.dma("pool", pTb, pT.rearrange("(kt p) t -> p kt t", p=128))

        prod_t = {}

        def ev6(m, nt, b, t_mm):
            i = g6k[0] % 2
            g6k[0] += 1
            eb = 4 + (b % 4)
            P.op("pe", lambda E, eb=eb, m=m, nt=nt: E.matmul(banks[eb][:, :], wpp[:, 0, m * 128:(m + 1) * 128], pTb[:, 0, nt * 512:(nt + 1) * 512], start=True, stop=False),
                 [t_wpp, t_pT, bfree[eb]], inc=False)
            t_e = P.op("pe", lambda E, eb=eb, m=m, nt=nt: E.matmul(banks[eb][:, :], wpp[:, 1, m * 128:(m + 1) * 128], pTb[:, 1, nt * 512:(nt + 1) * 512], start=False, stop=True), [])
            t_s = P.op("act", lambda E, i=i, b=b: E.activation(gt6[i], banks[b][:, :], AF.Sigmoid), [t_mm, g6free[i]])
            t_p = P.op("dve", lambda E, i=i, eb=eb, m=m, nt=nt: E.tensor_tensor(prod[:, m, nt * 512:(nt + 1) * 512], gt6[i], banks[eb][:, :], ALU.mult), [t_s, t_e])
            g6free[i] = t_p
            bfree[eb] = t_p
            prod_t[(m, nt)] = t_p
            return t_s
        linear(w_pg.rearrange("(kt p) m -> p kt m", p=128), 16, list(range(16)), lambda kt, nt: hn3[:, kt, nt * 512:(nt + 1) * 512], ev6, wr6,
               [[0, 1, 2, 3]], [])
        P.barrier()
        t_rs8 = stats_of(lambda i, nt: prod[:, i, nt * 512:(nt + 1) * 512], 16, rs7, float(D), sqb7, [])
        for m in range(16):
            for nt in range(2):
                sl = slice(nt * 512, (nt + 1) * 512)
                t1_ = P.op("dve", lambda E, m=m, sl=sl: E.scalar_tensor_tensor(prod[:, m, sl], prod[:, m, sl], gcol("ple_post", m), rs7[:, sl], ALU.mult, ALU.mult), [t_rs8])
                t2_ = P.op("dve", lambda E, m=m, sl=sl: E.tensor_tensor(h2[:, m, sl], h2[:, m, sl], prod[:, m, sl], ALU.add), [t1_])
                P.dma("sp", out_d[m * 128:(m + 1) * 128, sl], h2[:, m, sl], [t2_])
        P.barrier()
        A.pop()
        P.emit()
    return nc, tap_out


def col_layout(v):
    v = np.asarray(v, np.float32).reshape(-1)
    return v.reshape(-1, 128).T


def prep_shared(inp):
    sh = {}
    colsv = [inp["mix_norm_pre"][0], inp["attn_out_norm"][0], inp["ssm_out_norm"][0], inp["mix_norm_post"][0],
             inp["mlp_norm_pre"][0], inp["mlp_norm_post"][0], inp["ple_norm_pre"][0], inp["ple_norm_post"][0],
             inp["ssm_d"][0], inp["b_glu"][0]]
    sh["cols"] = np.ascontiguousarray(np.concatenate([col_layout(v) for v in colsv], axis=1))

    def st(v):
        v = np.asarray(v, np.float32).reshape(32, 2, 64)
        return v.transpose(1, 2, 0).reshape(128, 32)
    ldt = np.broadcast_to(np.asarray(inp["log_dt"][0], np.float32)[:, None], (64, 64))
    sh["sp"] = np.ascontiguousarray(np.concatenate([st(inp["lam_re"][0]), st(inp["lam_im"][0]), st(ldt)], axis=1))

    def padB(B):
        B = np.asarray(B, np.float32).reshape(32, 2, 64, 16)
        o = np.zeros((2, 64, 32, 2, 16), np.float32)
        for gl in range(2):
            o[gl, :, :, gl, :] = B[:, gl].transpose(1, 0, 2)
        return o.reshape(128, 1024)

    def padC(C):
        C = np.asarray(C, np.float32).reshape(32, 2, 16, 64)
        o = np.zeros((2, 64, 32, 2, 16), np.float32)
        for gl in range(2):
            o[gl, :, :, gl, :] = C[:, gl].transpose(2, 0, 1)
        return o.reshape(128, 1024)
    sh["bpr"] = padB(inp["ssm_b_re"][0])
    sh["bpi"] = padB(inp["ssm_b_im"][0])
    sh["cpr"] = padC(inp["ssm_c_re"][0])
    sh["cpi"] = padC(inp["ssm_c_im"][0])
    for k_, n_ in (("w_in", "w_in"), ("w_glu", "w_glu"), ("w_out", "w_out"), ("w_up", "w_up"), ("w_down", "w_down"),
                   ("w_ple_gate", "w_pg"), ("w_ple_proj", "w_pp")):
        sh[n_] = np.ascontiguousarray(np.asarray(inp[k_][0], np.float32))
    return sh


def consts_for_core(j):
    kk = np.arange(128)[:, None]
    ii = np.arange(128)[None, :]
    ident = (kk == ii).astype(np.float32)
    mprev = (kk >= ii).astype(np.float32)
    mcur = (kk <= ii).astype(np.float32)
    bmask = ((kk // 16) == (ii // 16)).astype(np.float32)
    pmask = ((kk // 32) == np.arange(4)[None, :]).astype(np.float32)
    T0 = 1024 * j
    vc = np.zeros((128, NVT), np.float32)
    for i, (d, r, m, nk) in enumerate(VT_LIST):
        t_abs = T0 + r + d * (m + np.arange(nk))
        vc[:nk, i] = (t_abs >= 0).astype(np.float32)
    return np.ascontiguousarray(np.concatenate([ident, mprev, mcur, bmask, pmask, vc], axis=1))


def prep_core(c, inp, sh):
    b, j = c // 4, c % 4
    T0 = 1024 * j
    x = np.asarray(inp["x"], np.float32)
    xe = np.zeros((NE, D), np.float32)
    lo = T0 - (NE - NO)
    s0 = max(lo, 0)
    xe[s0 - lo:, :] = x[b, s0:T0 + NO, :]
    m = dict(sh)
    m["xT"] = np.ascontiguousarray(xe.T)
    m["pT"] = np.ascontiguousarray(np.asarray(inp["p"], np.float32)[0, b, T0:T0 + NO, :].T)
    m["consts"] = consts_for_core(j)
    return m


_CACHE = {}


def kernel(**inputs):
    if "nc" not in _CACHE:
        _CACHE["nc"] = build_nc()[0]
    nc = _CACHE["nc"]
    sh = prep_shared(inputs)
    in_maps = [prep_core(c, inputs, sh) for c in range(8)]
    res = run_bass_kernel_spmd(nc, in_maps, core_ids=list(range(8)))
    out = np.zeros((2, 4096, D), np.float32)
    for c in range(8):
        b, j = c // 4, c % 4
        out[b, 1024 * j:1024 * (j + 1), :] = res.results[c]["out"].T
    return out
```

```python
import math
import os
KSTOP = int(os.environ.get('KSTOP', '0'))
KNG = int(os.environ.get('KNG', '8'))
KNH = int(os.environ.get('KNH', '8'))
KNOF = int(os.environ.get('KNOF', '0'))
KSIDE = int(os.environ.get('KSIDE', '1'))
from contextlib import ExitStack

import numpy as np
import concourse.bass as bass
import concourse.mybir as mybir
from concourse.bass_utils import run_bass_kernel_spmd

F32 = mybir.dt.float32
BF16 = mybir.dt.bfloat16
ALU = mybir.AluOpType
AF = mybir.ActivationFunctionType

D = 2048
KT = 16
NE = 4096
NO = 1024
NCH = 4
DFF = 8192
EPS = 1e-6
MAGIC = 12582912.0
TWO_PI = 2.0 * math.pi

COLS = [("mix_pre", 16), ("attn_n", 8), ("ssm_n", 8), ("mix_post", 16), ("mlp_pre", 16),
        ("mlp_post", 16), ("ple_pre", 16), ("ple_post", 16), ("ssm_d", 8), ("b_glu", 8)]
COL_OFF = {}
_o = 0
for _n, _c in COLS:
    COL_OFF[_n] = _o
    _o += _c
NCOLS = _o


def vtile_list():
    tiles = []
    for d in (1, 4, 16):
        M = NO // d
        for r in range(d):
            m = -128
            while m < M:
                nk = min(128, M - m)
                tiles.append((d, r, m, nk))
                m += 128
    return tiles


VT_LIST = vtile_list()
VT_IDX = {(d, r, m): i for i, (d, r, m, nk) in enumerate(VT_LIST)}
NVT = len(VT_LIST)


class Prog:
    ENG = ("pe", "act", "dve", "pool", "sp")

    def __init__(self, nc, stack, n_dma_sems=32):
        self.nc = nc
        self.q = {e: [] for e in self.ENG}
        self.cnt = {e: 0 for e in self.ENG}
        self.sem = {e: stack.enter_context(nc.semaphore("s_" + e)) for e in self.ENG}
        self.waited = {}
        n_sw = 16
        self.dsem = [stack.enter_context(nc.semaphore("d%d" % i)) for i in range(n_dma_sems + n_sw)]
        self.dcnt = [0] * (n_dma_sems + n_sw)
        self.drange = {"sp": (0, n_dma_sems), "pool": (n_dma_sems, n_dma_sems + n_sw)}
        self.dnext = {"sp": 0, "pool": n_dma_sems}
        self.ninst = 0

    def _waits(self, eng, deps):
        for d in deps:
            if d is None:
                continue
            if isinstance(d, list) or (isinstance(d, tuple) and len(d) and not isinstance(d[0], str)):
                self._waits(eng, d)
                continue
            kind, key, val = d
            wk = (eng, kind, key)
            if self.waited.get(wk, 0) >= val:
                continue
            self.waited[wk] = val
            sem = self.sem[key] if kind == "e" else self.dsem[key]
            self.q[eng].append(lambda E, sem=sem, val=val: E.wait_ge(sem, val))
            self.ninst += 1

    def op(self, eng, fn, deps=(), inc=True):
        self._waits(eng, deps)
        self.ninst += 1
        if inc:
            self.cnt[eng] += 1
            v = self.cnt[eng]
            sem = self.sem[eng]
            self.q[eng].append(lambda E, fn=fn, sem=sem: fn(E).then_inc(sem, 1))
            return ("e", eng, v)
        self.q[eng].append(lambda E, fn=fn: fn(E))
        return None

    def dma(self, eng, out, in_, deps=(), **kw):
        lo, hi = self.drange[eng]
        i = self.dnext[eng]
        self.dnext[eng] = lo + (i + 1 - lo) % (hi - lo)
        if self.dcnt[i] > 0:
            self._waits(eng, [("d", i, self.dcnt[i])])
        self._waits(eng, deps)
        self.dcnt[i] += 16
        sem = self.dsem[i]
        self.ninst += 1
        self.q[eng].append(lambda E, out=out, in_=in_, sem=sem, kw=kw: E.dma_start(out=out, in_=in_, **kw).then_inc(sem, 16))
        return ("d", i, self.dcnt[i])

    def all_tokens(self):
        toks = [("e", e, self.cnt[e]) for e in self.ENG if self.cnt[e] > 0]
        toks += [("d", i, self.dcnt[i]) for i in range(len(self.dsem)) if self.dcnt[i] > 0]
        return toks

    def barrier(self):
        toks = self.all_tokens()
        for e in self.ENG:
            self._waits(e, toks)

    def emit(self):
        nc = self.nc
        self._waits("sp", self.all_tokens())
        with nc.Block() as block:
            @block.tensor
            def _(E):
                for f in self.q["pe"]:
                    f(E)

            @block.scalar
            def _(E):
                for f in self.q["act"]:
                    f(E)

            @block.vector
            def _(E):
                for f in self.q["dve"]:
                    f(E)

            @block.gpsimd
            def _(E):
                for f in self.q["pool"]:
                    f(E)

            @block.sync
            def _(E):
                for f in self.q["sp"]:
                    f(E)


class Seq:
    def __init__(self, P, eng, deps=()):
        self.P = P
        self.eng = eng
        self.last = list(deps)

    def op(self, fn, deps=(), eng=None):
        e = eng or self.eng
        t = self.P.op(e, fn, [self.last, list(deps)])
        self.last = [t]
        return t

    def par(self, fns):
        toks = [self.P.op(self.eng, fn, [self.last]) for fn in fns]
        self.last = toks
        return toks


class Arena:
    def __init__(self, big, total):
        self.big = big
        self.total = total
        self.top = 0
        self.marks = []
        self.peak = 0

    def push(self):
        self.marks.append(self.top)

    def pop(self):
        self.top = self.marks.pop()

    def alloc(self, shape, dt):
        n = 1
        for s in shape[1:]:
            n *= s
        es = 2 if dt == BF16 else 4
        nbytes = (n * es + 63) // 64 * 64
        off = self.top
        self.top += nbytes
        self.peak = max(self.peak, self.top)
        assert self.top <= self.total, ("arena overflow", self.top, self.total)
        ap = self.big[:, off // 4:(off + nbytes) // 4]
        if dt == BF16:
            ap = ap.bitcast(BF16)
        ap = ap[:, 0:n]
        fs = shape[1:]
        if len(fs) == 2:
            ap = ap.rearrange("p (a b) -> p a b", b=fs[1])
        elif len(fs) == 3:
            ap = ap.rearrange("p (a b c) -> p a b c", b=fs[1], c=fs[2])
        elif len(fs) == 4:
            ap = ap.rearrange("p (a b c d) -> p a b c d", b=fs[1], c=fs[2], d=fs[3])
        if shape[0] < 128:
            ap = ap[0:shape[0]]
        return ap


def bcast_last(ap2, n):
    return ap2.unsqueeze(2).to_broadcast([ap2.shape[0], ap2.shape[1], n])


def build_nc(taps=()):
    nc = bass.Bass("TRN2", target_bir_lowering=False)
    dr = lambda name, shape, dt=F32: nc.dram_tensor(name, shape, dt, kind="ExternalInput").ap()
    xT = dr("xT", [D, NE])
    pT = dr("pT", [256, NO])
    cols_d = dr("cols", [128, NCOLS])
    sp_d = dr("sp", [128, 96])
    bpr_d = dr("bpr", [128, 1024])
    bpi_d = dr("bpi", [128, 1024])
    cpr_d = dr("cpr", [128, 1024])
    cpi_d = dr("cpi", [128, 1024])
    consts_d = dr("consts", [128, 4 * 128 + 4 + NVT])
    w_in = dr("w_in", [D, 4096])
    w_glu = dr("w_glu", [1024, 1024])
    w_out = dr("w_out", [D, D])
    w_up = dr("w_up", [D, DFF])
    w_down = dr("w_down", [DFF, D])
    w_pg = dr("w_pg", [D, D])
    w_pp = dr("w_pp", [256, D])
    out_d = nc.dram_tensor("out", [D, NO], F32, kind="ExternalOutput").ap()
    xg_s = nc.dram_tensor("xg_s", [D, 3072], BF16).ap()
    rs_s = nc.dram_tensor("rs_s", [128, 3072], F32).ap()
    h_s = nc.dram_tensor("h_s", [D, NO], F32).ap()
    ff_s = nc.dram_tensor("ff_s", [D, NO], F32).ap()
    yg_s = nc.dram_tensor("yg_s", [1024, NO], F32).ap()
    tap_out = {}

    w_in_v = w_in.rearrange("(kt p) m -> p kt m", p=128)

    with ExitStack() as top:
        P = Prog(nc, top)
        TOTAL = 207 * 1024
        big = top.enter_context(nc.sbuf_tensor("big", [128, TOTAL // 4], F32))
        A = Arena(big, TOTAL)
        banks = [top.enter_context(nc.psum_tensor("bank%d" % i, [128, 512], F32)) for i in range(8)]
        bfree = [None] * 8

        def tap(name, ap, shape):
            if name not in taps:
                return
            P.barrier()
            t = nc.dram_tensor("tap_" + name, list(shape), ap.dtype, kind="ExternalOutput").ap()
            tap_out[name] = t
            P.dma("sp", t, ap)
            P.barrier()

        cols = A.alloc([128, NCOLS], F32)
        cst = A.alloc([128, 4 * 128 + 4 + NVT], F32)
        ident_f = cst[:, 0:128]
        mprev_f = cst[:, 128:256]
        mcur_f = cst[:, 256:384]
        bmask = cst[:, 384:512]
        pmask = cst[:, 512:516]
        vcols = cst[:, 516:516 + NVT]
        ident_b = A.alloc([128, 128], BF16)
        ones_b = A.alloc([128, 128], BF16)
        mprev_b = A.alloc([128, 128], BF16)
        mcur_b = A.alloc([128, 128], BF16)
        epsc = A.alloc([128, 1], F32)
        t_cols = P.dma("sp", cols, cols_d)
        t_cst = P.dma("sp", cst, consts_d)
        t0 = P.op("dve", lambda E: E.tensor_copy(ident_b, ident_f), [t_cst])
        t1 = P.op("dve", lambda E: E.tensor_copy(mprev_b, mprev_f), [t_cst])
        t2 = P.op("dve", lambda E: E.tensor_copy(mcur_b, mcur_f), [t_cst])
        t3 = P.op("pool", lambda E: E.memset(ones_b, 1.0))
        t4 = P.op("pool", lambda E: E.memset(epsc, EPS))
        P.barrier()

        if KSTOP == 3:
            tap("cst", cst, [128, 4 * 128 + 4 + NVT])
            P.emit()
            return nc, tap_out

        def gcol(name, i):
            o = COL_OFF[name] + i
            return cols[:, o:o + 1]

        def rms_stats(src_fn, ntiles, ntok, rs_out, nfeat, sqbufs, stat_banks, deps):
            toks = []
            nnt = ntok // 512
            sqfree = [None] * len(sqbufs)
            j = 0
            last_mm = [None] * nnt
            for i in range(ntiles):
                for nt in range(nnt):
                    sq = sqbufs[j % len(sqbufs)]
                    dp = deps(i, nt) if callable(deps) else deps
                    t_sq = P.op("act", lambda E, sq=sq, i=i, nt=nt: E.activation(sq, src_fn(i, nt), AF.Square), [dp, sqfree[j % len(sqbufs)]])
                    b = stat_banks[nt]
                    t_mm = P.op("pe", lambda E, sq=sq, b=b, i=i: E.matmul(banks[b][:, :], ones_b, sq, start=(i == 0), stop=(i == ntiles - 1)),
                                [t_sq, bfree[b] if i == 0 else None])
                    sqfree[j % len(sqbufs)] = t_mm
                    last_mm[nt] = t_mm
                    j += 1
            for nt in range(nnt):
                b = stat_banks[nt]
                ta = P.op("act", lambda E, b=b, nt=nt: E.activation(rs_out[:, nt * 512:(nt + 1) * 512], banks[b][:, :], AF.Sqrt, bias=epsc, scale=1.0 / nfeat), [last_mm[nt]])
                bfree[b] = ta
                tb = P.op("dve", lambda E, nt=nt: E.reciprocal(rs_out[:, nt * 512:(nt + 1) * 512], rs_out[:, nt * 512:(nt + 1) * 512]), [ta])
                toks.append(tb)
            return toks

        A.push()
        yg = None
        spv = A.alloc([128, 96], F32)
        G = A.alloc([128, 8, 2, 8, 128], BF16)
        H = A.alloc([128, 8, 8, 128], BF16)
        ctab = A.alloc([128, 32, 128], F32)
        stab = A.alloc([128, 32, 128], F32)
        sm = A.alloc([128, 48, 32], F32)
        Zin_r = sm[:, 0, :]
        Zin_i = sm[:, 1, :]
        rho8 = sm[:, 2, :]
        e8r = sm[:, 3, :]
        e8i = sm[:, 4, :]
        apow = [(sm[:, 5 + 2 * t, :], sm[:, 6 + 2 * t, :]) for t in range(9)]
        _n = [23]

        def smalloc():
            i = _n[0]
            _n[0] += 1
            assert i < 48
            return sm[:, i, :]

        t_sp = P.dma("sp", spv, sp_d)
        S = Seq(P, "dve", [t_sp, t_cols])
        S.op(lambda E: E.memset(Zin_r, 0.0))
        S.op(lambda E: E.memset(Zin_i, 0.0))

        def cmul(S, or_, oi_, ar, ai, br, bi, t1, t2):
            S.par([lambda E: E.tensor_tensor(or_, ar, br, ALU.mult),
                   lambda E: E.tensor_tensor(t1, ai, bi, ALU.mult),
                   lambda E: E.tensor_tensor(oi_, ar, bi, ALU.mult),
                   lambda E: E.tensor_tensor(t2, ai, br, ALU.mult)])
            S.par([lambda E: E.tensor_tensor(or_, or_, t1, ALU.subtract),
                   lambda E: E.tensor_tensor(oi_, oi_, t2, ALU.add)])

        A.push()
        CPr = A.alloc([128, 32, 32], F32)
        CPni = A.alloc([128, 32, 32], F32)
        CPi0 = A.alloc([128, 32, 32], F32)
        t_cpr = P.dma("sp", CPr, cpr_d.rearrange("p (a b) -> p a b", b=32))
        t_cpi = P.dma("sp", CPi0, cpi_d.rearrange("p (a b) -> p a b", b=32))
        S.last = [S.last, t_cpr, t_cpi]
        S.op(lambda E: E.memset(CPni, 0.0))
        S.op(lambda E: E.tensor_tensor(CPni, CPni, CPi0, ALU.subtract))
        lamr = spv[:, 0:32]
        lami = spv[:, 32:64]
        ldt = spv[:, 64:96]
        dt_ = smalloc(); zr = smalloc(); zi = smalloc(); em1 = smalloc(); mag = smalloc()
        c1 = smalloc(); s1 = smalloc(); sh = smalloc(); ta = smalloc(); tb = smalloc()
        numr = smalloc(); numi = smalloc(); cfr = smalloc(); cfi = smalloc(); tc = smalloc(); td = smalloc()
        S.op(lambda E: E.activation(dt_, ldt, AF.Exp), eng="act")
        S.op(lambda E: E.tensor_tensor(zr, lamr, dt_, ALU.mult))
        S.op(lambda E: E.tensor_tensor(zi, lami, dt_, ALU.mult))
        S.op(lambda E: E.tensor_scalar(em1, zr, 1.0 / 120.0, None, ALU.mult))
        for cst_ in (1.0 / 24.0, 1.0 / 6.0, 0.5, 1.0):
            S.op(lambda E, c=cst_: E.scalar_tensor_tensor(em1, em1, c, zr, ALU.add, ALU.mult))
        S.op(lambda E: E.tensor_scalar(mag, em1, 1.0, None, ALU.add))

        def sin_of(dst, src, shift, scale):
            S.op(lambda E: E.tensor_scalar(ta, src, scale, shift, ALU.mult, ALU.add))
            S.op(lambda E: E.tensor_scalar(tb, ta, 1.0 / TWO_PI, MAGIC, ALU.mult, ALU.add))
            S.op(lambda E: E.tensor_scalar(tb, tb, MAGIC, None, ALU.subtract))
            S.op(lambda E: E.scalar_tensor_tensor(ta, tb, -TWO_PI, ta, ALU.mult, ALU.add))
            S.op(lambda E: E.tensor_scalar(ta, ta, math.pi, -math.pi, ALU.min, ALU.max))
            S.op(lambda E: E.activation(dst, ta, AF.Sin), eng="act")

        sin_of(s1, zi, 0.0, 1.0)
        sin_of(c1, zi, math.pi / 2.0, 1.0)
        sin_of(sh, zi, 0.0, 0.5)
        S.op(lambda E: E.tensor_tensor(ta, sh, sh, ALU.mult))
        S.op(lambda E: E.tensor_tensor(tb, em1, c1, ALU.mult))
        S.op(lambda E: E.scalar_tensor_tensor(numr, ta, -2.0, tb, ALU.mult, ALU.add))
        S.op(lambda E: E.tensor_tensor(numi, mag, s1, ALU.mult))
        S.op(lambda E: E.tensor_tensor(ta, lamr, lamr, ALU.mult))
        S.op(lambda E: E.tensor_tensor(tb, lami, lami, ALU.mult))
        S.op(lambda E: E.tensor_tensor(ta, ta, tb, ALU.add))
        S.op(lambda E: E.reciprocal(ta, ta))
        S.op(lambda E: E.tensor_tensor(tb, numr, lamr, ALU.mult))
        S.op(lambda E: E.tensor_tensor(tc, numi, lami, ALU.mult))
        S.op(lambda E: E.tensor_tensor(tb, tb, tc, ALU.add))
        S.op(lambda E: E.tensor_tensor(cfr, tb, ta, ALU.mult))
        S.op(lambda E: E.tensor_tensor(tb, numi, lamr, ALU.mult))
        S.op(lambda E: E.tensor_tensor(tc, numr, lami, ALU.mult))
        S.op(lambda E: E.tensor_tensor(tb, tb, tc, ALU.subtract))
        S.op(lambda E: E.tensor_tensor(cfi, tb, ta, ALU.mult))
        S.op(lambda E: E.memset(apow[0][0], 1.0))
        S.op(lambda E: E.memset(apow[0][1], 0.0))
        S.op(lambda E: E.tensor_tensor(apow[1][0], mag, c1, ALU.mult))
        S.op(lambda E: E.tensor_tensor(apow[1][1], mag, s1, ALU.mult))
        for t in range(1, 8):
            cmul(S, apow[t + 1][0], apow[t + 1][1], apow[t][0], apow[t][1], apow[1][0], apow[1][1], tc, td)
        S.op(lambda E: E.tensor_tensor(rho8, mag, mag, ALU.mult))
        S.op(lambda E: E.tensor_tensor(rho8, rho8, rho8, ALU.mult))
        S.op(lambda E: E.tensor_tensor(rho8, rho8, rho8, ALU.mult))
        e2r = smalloc(); e2i = smalloc()
        cmul(S, e2r, e2i, c1, s1, c1, s1, tc, td)
        e4r = smalloc(); e4i = smalloc()
        cmul(S, e4r, e4i, e2r, e2i, e2r, e2i, tc, td)
        cmul(S, e8r, e8i, e4r, e4i, e4r, e4i, tc, td)
        if KSTOP == 4:
            tap("sm", sm.rearrange("p a b -> p (a b)"), [128, 48 * 32])
            P.emit()
            return nc, tap_out
        big1 = A.alloc([128, 32, 64], F32)
        big2 = A.alloc([128, 32, 64], F32)
        S.op(lambda E: E.memset(ctab[:, :, 0:1], 1.0))
        S.op(lambda E: E.memset(stab[:, :, 0:1], 0.0))
        Er, Ei = e8r, e8i
        epp = [(smalloc(), smalloc()), (smalloc(), smalloc())]
        epi = 0
        L = 1
        while L < 128:
            br_ = bcast_last(Er, L)
            bi_ = bcast_last(Ei, L)
            cmul(S, ctab[:, :, L:2 * L], stab[:, :, L:2 * L], ctab[:, :, 0:L], stab[:, :, 0:L], br_, bi_,
                 big1[:, :, 0:L], big2[:, :, 0:L])
            if 2 * L < 128:
                nr, ni = epp[epi % 2]
                epi += 1
                cmul(S, nr, ni, Er, Ei, Er, Ei, tc, td)
                Er, Ei = nr, ni
            L *= 2
        if KSTOP == 5:
            tap("ctab", ctab.rearrange("p a b -> p (a b)"), [128, 32 * 128])
            tap("sm", sm.rearrange("p a b -> p (a b)"), [128, 48 * 32])
            P.emit()
            return nc, tap_out
        BPr = A.alloc([128, 32, 32], F32)
        BPi = A.alloc([128, 32, 32], F32)
        BBr = A.alloc([128, 32, 32], F32)
        BBi = A.alloc([128, 32, 32], F32)
        ABr = A.alloc([128, 32, 32], F32)
        ABi = A.alloc([128, 32, 32], F32)
        T1 = A.alloc([128, 32, 32], F32)
        T2 = A.alloc([128, 32, 32], F32)
        tmpH4 = A.alloc([128, 4, 128], F32)
        t_bpr = P.dma("sp", BPr, bpr_d.rearrange("p (a b) -> p a b", b=32))
        t_bpi = P.dma("sp", BPi, bpi_d.rearrange("p (a b) -> p a b", b=32))
        S.last = [S.last, t_bpr, t_bpi]
        cmul(S, BBr, BBi, BPr, BPi, bcast_last(cfr, 32), bcast_last(cfi, 32), T1, T2)
        if KSTOP == 6:
            tap("BBr", BBr.rearrange("p a b -> p (a b)"), [128, 1024])
            P.emit()
            return nc, tap_out
        ABh = [A.alloc([128, 1024], BF16) for _ in range(2)]
        ABl = [A.alloc([128, 1024], BF16) for _ in range(2)]
        Chl = [A.alloc([128, 1024], BF16) for _ in range(4)]
        for pp, src_ in ((0, CPr), (1, CPni)):
            srcf = src_.rearrange("p a b -> p (a b)")
            S.op(lambda E, pp=pp, srcf=srcf: E.tensor_copy(Chl[2 * pp], srcf))
            S.op(lambda E, pp=pp, srcf=srcf: E.tensor_tensor(T1.rearrange("p a b -> p (a b)"), srcf, Chl[2 * pp], ALU.subtract))
            S.op(lambda E, pp=pp: E.tensor_copy(Chl[2 * pp + 1], T1.rearrange("p a b -> p (a b)")))
        gh_done = []
        for tau in range(8 if KSTOP != 7 else 1):
            cmul(S, ABr, ABi, BBr, BBi, bcast_last(apow[tau][0], 32), bcast_last(apow[tau][1], 32), T1, T2)
            for pp, src_ in ((0, ABr), (1, ABi)):
                srcf = src_.rearrange("p a b -> p (a b)")
                S.op(lambda E, pp=pp, srcf=srcf: E.tensor_copy(ABh[pp], srcf))
                S.op(lambda E, pp=pp, srcf=srcf: E.tensor_tensor(T1.rearrange("p a b -> p (a b)"), srcf, ABh[pp], ALU.subtract))
                S.op(lambda E, pp=pp: E.tensor_copy(ABl[pp], T1.rearrange("p a b -> p (a b)")))
            t_hl = S.last
            t_ab = S.last
            ABrf = ABr.rearrange("p a b -> p (a b)")
            ABif = ABi.rearrange("p a b -> p (a b)")
            CPrf = CPr.rearrange("p a b -> p (a b)")
            CPnif = CPni.rearrange("p a b -> p (a b)")
            rd = []
            for part, src in ((0, ABrf), (1, ABif)):
                for half in range(2):
                    b = part * 2 + half
                    t_tr = None
                    for q in range(4):
                        ct = half * 4 + q
                        sl = slice(q * 128, q * 128 + 128)
                        t_tr = P.op("pe", lambda E, b=b, sl=sl, src=src, ct=ct: E.transpose(banks[b][:, sl], src[:, ct * 128:(ct + 1) * 128], ident_f),
                                    [t_ab, bfree[b]], inc=(q == 3))
                    t_cp = P.op("act", lambda E, b=b, tau=tau, part=part, half=half: E.copy(
                        G[:, tau, part, half * 4:half * 4 + 4, :].rearrange("p a b -> p (a b)"), banks[b][:, :]), [t_tr])
                    rd.append((b, t_cp))
            for half in range(2):
                b = 4 + half
                t_mm = None
                pairs = [(ABh[0], Chl[0]), (ABh[0], Chl[1]), (ABl[0], Chl[0]), (ABh[1], Chl[2]), (ABh[1], Chl[3]), (ABl[1], Chl[2])]
                for q in range(4):
                    ct = half * 4 + q
                    sl = slice(q * 128, q * 128 + 128)
                    for pi_, (aa, cc) in enumerate(pairs):
                        t_mm = P.op("pe", lambda E, b=b, sl=sl, ct=ct, aa=aa, cc=cc, pi_=pi_, q=q: E.matmul(
                            banks[b][:, sl], aa[:, ct * 128:(ct + 1) * 128], cc[:, ct * 128:(ct + 1) * 128],
                            start=(pi_ == 0 and q == 0), stop=(pi_ == 5), skip_group_check=True),
                            [t_hl, bfree[b]] if (q == 0 and pi_ == 0) else [], inc=(q == 3 and pi_ == 5))
                bm4 = bmask.unsqueeze(1).to_broadcast([128, 4, 128])
                bk4 = banks[b][:, :].rearrange("p (a b) -> p a b", b=128)
                if tau == 0:
                    t_e1 = P.op("dve", lambda E, bk4=bk4, bm4=bm4: E.tensor_tensor(tmpH4, bk4, bm4, ALU.mult), [t_mm, S.last])
                    rd.append((b, t_e1))
                    tl = t_e1
                    for q in range(4):
                        ct = half * 4 + q
                        tl = P.op("dve", lambda E, ct=ct, q=q: E.scalar_tensor_tensor(H[:, 0, ct, :], ident_f, gcol("ssm_d", ct), tmpH4[:, q, :], ALU.mult, ALU.add), [tl])
                    S.last = [tl]
                else:
                    t_ev = P.op("dve", lambda E, bk4=bk4, bm4=bm4, tau=tau, half=half: E.tensor_tensor(H[:, tau, half * 4:half * 4 + 4, :], bk4, bm4, ALU.mult), [t_mm])
                    rd.append((b, t_ev))
            for b in range(6):
                bfree[b] = [t for (bb, t) in rd if bb == b]
            gh_done.append([t for (_, t) in rd])
            S.last = [S.last, gh_done[-1]]
            if tau == 7:
                tap("ABr", ABr.rearrange("p a b -> p (a b)"), [128, 1024])
                tap("ABi", ABi.rearrange("p a b -> p (a b)"), [128, 1024])
                tap("CPr", CPr.rearrange("p a b -> p (a b)"), [128, 1024])
                tap("CPni", CPni.rearrange("p a b -> p (a b)"), [128, 1024])
                tap("Chl0", Chl[0], [128, 1024])
                tap("ABh0", ABh[0], [128, 1024])
        A.pop()
        P.barrier()
        tap("G", G.rearrange("p a b c d -> p (a b c d)"), [128, 8 * 2 * 8 * 128])
        tap("H", H.rearrange("p a b c -> p (a b c)"), [128, 8 * 8 * 128])
        tap("ctab", ctab.rearrange("p a b -> p (a b)"), [128, 32 * 128])
        tap("stab", stab.rearrange("p a b -> p (a b)"), [128, 32 * 128])
        tap("sm", sm.rearrange("p a b -> p (a b)"), [128, 48 * 32])

        ucs = [A.alloc([128, 8, 1024], BF16) for _ in range(2)]
        um_bufs = [A.alloc([128, 4, 1024], BF16) for _ in range(2)]
        wblk = A.alloc([128, 6, 512], F32)
        Wr, Wi, Sr, Si, t1a, t2a = [wblk[:, q, :].rearrange("p (a b) -> p a b", b=128) for q in range(6)]
        Zpr = A.alloc([128, 4, 128], BF16)
        Zpi = A.alloc([128, 4, 128], BF16)
        w0 = A.alloc([128, 6, 4], F32)
        udi = A.alloc([128, 1024], BF16)
        um_free = [None, None]
        ssm_state = {"dve": [], "zp_free": None}

        def norm_proj_chunk(c, wcol0, n_mt, dst_fn, save_scratch, env, nts=(0, 1)):
            xring, sqb, xg, rs, wring = env["xring"], env["sqb"], env["xg"], env["rs"], env["wring"]
            for nt in nts:
                tok0 = 1024 * c + 512 * nt
                last_mm = None
                xg_toks = []
                for kt in range(KT):
                    xi = env["xi"]; env["xi"] = (xi + 1) % len(xring)
                    xb = xring[xi]
                    t_ld = P.dma("sp", xb, xT[kt * 128:(kt + 1) * 128, tok0:tok0 + 512], [env["xfree"][xi]])
                    si = kt % 2
                    t_sq = P.op("act", lambda E, xb=xb, si=si: E.activation(sqb[si], xb, AF.Square), [t_ld, env["sqfree"][si]])
                    last_mm = P.op("pe", lambda E, si=si, kt=kt: E.matmul(banks[6][:, :], ones_b, sqb[si], start=(kt == 0), stop=(kt == KT - 1)),
                                   [t_sq, bfree[6] if kt == 0 else None])
                    env["sqfree"][si] = last_mm
                    t_xg = P.op("dve", lambda E, xb=xb, kt=kt: E.tensor_scalar(xg[:, kt, :], xb, gcol("mix_pre", kt), None, ALU.mult),
                                [t_ld, env["xg_free"]])
                    xg_toks.append(t_xg)
                    env["xfree"][xi] = [t_sq, t_xg]
                ta_ = P.op("act", lambda E: E.activation(rs, banks[6][:, :], AF.Sqrt, bias=epsc, scale=1.0 / D), [last_mm, env["rs_free"]])
                bfree[6] = ta_
                t_rs = P.op("dve", lambda E: E.reciprocal(rs, rs), [ta_])
                readers = []
                if save_scratch:
                    tcol = tok0 - 1024
                    readers.append(P.dma("sp", xg_s.rearrange("(kt p) t -> p kt t", p=128)[:, :, tcol:tcol + 512], xg, [xg_toks]))
                    readers.append(P.dma("sp", rs_s[:, tcol:tcol + 512], rs, [t_rs]))
                evs = []
                for m in range(n_mt):
                    if m % 2 == 0:
                        wi_ = env["wi"]; env["wi"] = (wi_ + 1) % len(wring)
                        wsl = wring[wi_]
                        t_w = P.dma("pool", wsl, w_in_v[:, :, wcol0 + m * 128:wcol0 + m * 128 + 256], [env["wfree"][wi_]])
                        env["wfree"][wi_] = []
                        cur = (wi_, wsl, t_w)
                    wi_, wsl, t_w = cur
                    b = 4 + (env["bi"] % 2); env["bi"] += 1
                    t_mm = None
                    for kt in range(KT):
                        t_mm = P.op("pe", lambda E, b=b, wsl=wsl, kt=kt, mo=(m % 2) * 128: E.matmul(banks[b][:, :], wsl[:, kt, mo:mo + 128], xg[:, kt, :],
                                                                                                  start=(kt == 0), stop=(kt == KT - 1)),
                                    [t_w, xg_toks, bfree[b]] if kt == 0 else [], inc=(kt == KT - 1))
                    env["wfree"][wi_].append(t_mm)
                    dst = dst_fn(m, nt)
                    t_ev = P.op("dve", lambda E, b=b, dst=dst: E.tensor_tensor(dst, banks[b][:, :], rs, ALU.mult), [t_mm, t_rs, env["dst_free"]])
                    bfree[b] = t_ev
                    evs.append(t_ev)
                    readers.append(t_mm)
                env["xg_free"] = readers
                env["rs_free"] = [evs, readers]
                env["done"].setdefault(c, []).append(evs)
            return

        def ssm_chunk(c, own, u_ready, uc, Fm=None, side=None):
            readers = []

            def do_ct(ct):
                ui = ct % 2
                um = um_bufs[ui]
                vb0, vb1 = (0, 1) if (own or ct % 2 == 0) else (2, 3)
                tm = []
                for j in range(4):
                    tm.append(P.op("act", lambda E, j=j, ct=ct: E.activation(um[:, j, :].rearrange("p (i k) -> p i k", k=128), uc[:, ct, :].rearrange("p (k i) -> p i k", i=8),
                                                                                AF.Copy, scale=pmask[:, j:j + 1]),
                                   [u_ready, um_free[ui]]))
                if own:
                    tm.append(P.op("act", lambda E, ct=ct: E.copy(udi.rearrange("p (i k) -> p i k", k=128), uc[:, ct, :].rearrange("p (k i) -> p i k", i=8)),
                                   [u_ready, ssm_state["zp_free"]]))
                t_v = None
                for part in range(2):
                    for i in range(8):
                        last = (part == 1 and i == 7)
                        vb = vb0 if part == 0 else vb1
                        t_v = P.op("pe", lambda E, part=part, i=i, ct=ct, vb=vb: E.matmul(banks[vb][:, :].rearrange("p (a b) -> p a b", b=128), G[:, 7 - i, part, ct, :], um[:, :, i * 128:(i + 1) * 128],
                                                                              start=(i == 0), stop=(i == 7)),
                                   [tm, bfree[vb0], bfree[vb1]] if (part == 0 and i == 0) else [], inc=last)
                um_free[ui] = t_v
                cs = ctab[:, 4 * ct:4 * ct + 4, :]
                sn = stab[:, 4 * ct:4 * ct + 4, :]
                Vr = banks[vb0][:, :].rearrange("p (a b) -> p a b", b=128)
                Vi = banks[vb1][:, :].rearrange("p (a b) -> p a b", b=128)
                prev = ssm_state["dve"]
                B0, B1, B2, B3, Sr_, Si_ = Wr, Wi, Sr, Si, t1a, t2a
                dv = lambda fn, deps: P.op("dve", fn, deps)
                zr_ = Zin_r[:, 4 * ct:4 * ct + 4]
                zi_ = Zin_i[:, 4 * ct:4 * ct + 4]
                er_ = e8r[:, 4 * ct:4 * ct + 4]
                ei_ = e8i[:, 4 * ct:4 * ct + 4]
                a1 = dv(lambda E: E.tensor_tensor(B0, Vr, cs, ALU.mult), [t_v, prev])
                a2 = dv(lambda E: E.tensor_tensor(B1, Vi, sn, ALU.mult), [t_v, prev])
                a3 = dv(lambda E: E.tensor_tensor(B2, Vi, cs, ALU.mult), [t_v, prev])
                a4 = dv(lambda E: E.tensor_tensor(B3, Vr, sn, ALU.mult), [t_v, prev])
                bfree[vb0] = [a1, a2, a3, a4]
                bfree[vb1] = [a1, a2, a3, a4]
                w1 = dv(lambda E: E.tensor_tensor(w0[:, 2, :], er_, zr_, ALU.mult), [prev])
                w2 = dv(lambda E: E.tensor_tensor(w0[:, 3, :], ei_, zi_, ALU.mult), [prev])
                w3 = dv(lambda E: E.tensor_tensor(w0[:, 4, :], er_, zi_, ALU.mult), [prev])
                w4 = dv(lambda E: E.tensor_tensor(w0[:, 5, :], ei_, zr_, ALU.mult), [prev])
                a5 = dv(lambda E: E.tensor_tensor(B0, B0, B1, ALU.add), [a1, a2])
                a6 = dv(lambda E: E.tensor_tensor(B2, B2, B3, ALU.subtract), [a3, a4])
                w5 = dv(lambda E: E.tensor_tensor(w0[:, 0, :], w0[:, 2, :], w0[:, 3, :], ALU.subtract), [w1, w2])
                w6 = dv(lambda E: E.tensor_tensor(w0[:, 1, :], w0[:, 4, :], w0[:, 5, :], ALU.add), [w3, w4])
                sr_t, si_t = [], []
                for j in range(4):
                    pr = 4 * ct + j
                    d0 = rho8[:, pr:pr + 1].to_broadcast([128, 128])
                    sr_t.append(dv(lambda E, j=j, d0=d0: E.tensor_tensor_scan(Sr_[:, j, :], d0, B0[:, j, :], w0[:, 0, j:j + 1], ALU.mult, ALU.add), [a5, w5, prev]))
                    si_t.append(dv(lambda E, j=j, d0=d0: E.tensor_tensor_scan(Si_[:, j, :], d0, B2[:, j, :], w0[:, 1, j:j + 1], ALU.mult, ALU.add), [a6, w6, prev]))
                c0 = []
                if own:
                    c0.append(dv(lambda E: E.tensor_copy(Zpr[:, :, 0:1], zr_.unsqueeze(2)), [ssm_state["zp_free"], prev]))
                    c0.append(dv(lambda E: E.tensor_copy(Zpi[:, :, 0:1], zi_.unsqueeze(2)), [ssm_state["zp_free"], prev]))
                c127 = cs[:, :, 127:128]
                s127 = sn[:, :, 127:128]
                z1 = dv(lambda E: E.tensor_tensor(w0[:, 2, :].unsqueeze(2), c127, Sr_[:, :, 127:128], ALU.mult), [sr_t, w5])
                z2 = dv(lambda E: E.tensor_tensor(w0[:, 3, :].unsqueeze(2), s127, Si_[:, :, 127:128], ALU.mult), [si_t, w5])
                z3 = dv(lambda E: E.tensor_tensor(w0[:, 4, :].unsqueeze(2), s127, Sr_[:, :, 127:128], ALU.mult), [sr_t, w6])
                z4 = dv(lambda E: E.tensor_tensor(w0[:, 5, :].unsqueeze(2), c127, Si_[:, :, 127:128], ALU.mult), [si_t, w6])
                z5 = dv(lambda E: E.tensor_tensor(zr_, w0[:, 2, :], w0[:, 3, :], ALU.subtract), [z1, z2, w1, w4, c0])
                z6 = dv(lambda E: E.tensor_tensor(zi_, w0[:, 4, :], w0[:, 5, :], ALU.add), [z3, z4, w2, w3, c0])
                all_t = [a1, a2, a3, a4, w1, w2, w3, w4, a5, a6, w5, w6, sr_t, si_t, c0, z1, z2, z3, z4, z5, z6]
                if own:
                    cs7 = cs[:, :, 0:127]; sn7 = sn[:, :, 0:127]
                    d1 = dv(lambda E: E.tensor_tensor(B1[:, :, 0:127], cs7, Sr_[:, :, 0:127], ALU.mult), [sr_t, a5])
                    d2 = dv(lambda E: E.tensor_tensor(B3[:, :, 0:127], sn7, Si_[:, :, 0:127], ALU.mult), [si_t, a6])
                    d3 = dv(lambda E: E.tensor_tensor(Zpr[:, :, 1:128], B1[:, :, 0:127], B3[:, :, 0:127], ALU.subtract), [d1, d2, ssm_state["zp_free"]])
                    d4 = dv(lambda E: E.tensor_tensor(B1[:, :, 0:127], sn7, Sr_[:, :, 0:127], ALU.mult), [d3])
                    d5 = dv(lambda E: E.tensor_tensor(B3[:, :, 0:127], cs7, Si_[:, :, 0:127], ALU.mult), [d3])
                    d6 = dv(lambda E: E.tensor_tensor(Zpi[:, :, 1:128], B1[:, :, 0:127], B3[:, :, 0:127], ALU.add), [d4, d5])
                    t_zp = [d3, d6, c0]
                    all_t += [d1, d2, d3, d4, d5, d6]

                class _L:
                    last = all_t
                Sq = _L()
                if own:
                    t_y = [None, None]
                    for hb in range(2):
                        b = 2 + hb
                        first = True
                        for j in range(4 * hb, 4 * hb + 4):
                            jc = (j % 4) * 128
                            for i in range(j + 1):
                                P.op("pe", lambda E, b=b, j=j, i=i, ct=ct, jc=jc, first=first: E.matmul(banks[b][:, jc:jc + 128], H[:, j - i, ct, :], udi[:, i * 128:(i + 1) * 128],
                                                                                                 start=first, stop=False, skip_group_check=True),
                                     [u_ready, bfree[b], t_zp, tm] if first else [], inc=False)
                                first = False
                            for pl in range(4):
                                for part in range(2):
                                    zp = Zpr if part == 0 else Zpi
                                    lastm = (j == 4 * hb + 3 and pl == 3 and part == 1)
                                    t_ = P.op("pe", lambda E, b=b, j=j, pl=pl, part=part, zp=zp, ct=ct, jc=jc: E.matmul(
                                        banks[b][32 * pl:32 * pl + 32, jc:jc + 128], Fm[:, j, part, (4 * ct + pl) * 32:(4 * ct + pl) * 32 + 32], zp[:, pl, :],
                                        start=False, stop=(pl == 3 and part == 1), tile_position=(0, 32 * pl), skip_group_check=True), [], inc=lastm)
                                    if lastm:
                                        t_y[hb] = t_
                    ssm_state["zp_free"] = t_y[1]
                    readers.append(t_y[1])
                    yi = ssm_state.get("yi", 0)
                    ssm_state["yi"] = yi + 1
                    yrow = yring[yi % 2]
                    t_fin = []
                    for hb in range(2):
                        b = 2 + hb
                        yb = banks[b][:, :]
                        dst = yrow.rearrange("p (k j) -> p j k", j=8)[:, 4 * hb:4 * hb + 4, :]
                        G1 = gtmp[0]; G2 = gtmp[1]
                        ta1 = P.op("act", lambda E, yb=yb, G1=G1: E.activation(G1, yb, AF.Square), [t_y[hb], ssm_state.get("g_free")])
                        tb1 = P.op("dve", lambda E, G1=G1: E.tensor_scalar(G1, G1, 0.044715, 1.0, ALU.mult, ALU.add), [ta1, Sq.last])
                        tb2 = P.op("dve", lambda E, G1=G1, yb=yb: E.tensor_tensor(G1, G1, yb, ALU.mult), [tb1])
                        ta2 = P.op("act", lambda E, G1=G1, G2=G2: E.activation(G2, G1, AF.Sigmoid, scale=1.5957691216057308), [tb2])
                        tb3 = P.op("dve", lambda E, G2=G2, yb=yb, dst=dst: E.tensor_tensor(dst, yb.rearrange("p (j k) -> p j k", k=128), G2.rearrange("p (j k) -> p j k", k=128), ALU.mult),
                                   [ta2, yfree[yi % 2]])
                        bfree[b] = tb3
                        ssm_state["g_free"] = tb3
                        Sq.last = [tb3]
                        t_fin.append(tb3)
                    yfree[yi % 2] = P.dma("sp", yg_s[ct * 128:(ct + 1) * 128, :], yrow, [t_fin])
                else:
                    readers.append(t_v)
                ssm_state["dve"] = Sq.last

            for ct_ in range(8):
                do_ct(ct_)
                if side and ct_ in side:
                    side[ct_]()
            return readers

        A.push()
        env = {"xring": [A.alloc([128, 512], F32) for _ in range(4)], "sqb": [A.alloc([128, 512], BF16) for _ in range(2)],
               "xg": A.alloc([128, KT, 512], BF16), "rs": A.alloc([128, 512], F32),
               "wring": [A.alloc([128, KT, 256], BF16) for _ in range(3)],
               "xi": 0, "wi": 0, "bi": 0, "xfree": [None] * 4, "sqfree": [None] * 2, "wfree": [[], [], []],
               "xg_free": None, "rs_free": None, "dst_free": None}
        env["done"] = {}
        uc_readers = [None, None]

        def proj_job(c, nts):
            def f():
                ucb = ucs[c % 2]
                env["dst_free"] = uc_readers[c % 2]
                norm_proj_chunk(c, 3072, 8, lambda m, nt, ucb=ucb: ucb[:, m, 512 * nt:512 * nt + 512], c >= 1, env, nts=nts)
            return f
        proj_job(0, (0, 1))()
        for c in range(NCH - 1):
            if KSIDE:
                side = {1: proj_job(c + 1, (0,)), 4: proj_job(c + 1, (1,))}
                uc_readers[c % 2] = ssm_chunk(c, False, env["done"][c], ucs[c % 2], side=side)
            else:
                uc_readers[c % 2] = ssm_chunk(c, False, env["done"][c], ucs[c % 2])
                proj_job(c + 1, (0, 1))()
        P.barrier()
        A.pop()
        if "u_own" in taps:
            tap("u_own", ucs[(NCH - 1) % 2].rearrange("p a b -> p (a b)"), [128, 8 * 1024])
        tap("zin", sm.rearrange("p a b -> p (a b)"), [128, 48 * 32])
        Fm = A.alloc([128, 8, 2, 1024], BF16)
        yring = [A.alloc([128, 1024], F32) for _ in range(2)]
        yfree = [None, None]
        gtmp = [A.alloc([128, 512], F32) for _ in range(2)]
        A.push()
        Ft1b = wblk[:, 0:2, :].rearrange("p a (b c) -> p (a b) c", c=32)
        Ft2b = wblk[:, 2:4, :].rearrange("p a (b c) -> p (a b) c", c=32)
        CPr2 = A.alloc([128, 32, 32], F32)
        CPni2 = A.alloc([128, 32, 32], F32)
        CPi02 = wblk[:, 4:6, :].rearrange("p a (b c) -> p (a b) c", c=32)
        t_cpr2 = P.dma("sp", CPr2, cpr_d.rearrange("p (a b) -> p a b", b=32))
        t_cpi2 = P.dma("sp", CPi02, cpi_d.rearrange("p (a b) -> p a b", b=32))
        SF = Seq(P, "dve", [t_cpr2, t_cpi2])
        SF.op(lambda E: E.memset(CPni2, 0.0))
        SF.op(lambda E: E.tensor_tensor(CPni2, CPni2, CPi02, ALU.subtract))
        for j in range(8):
            pr_ = bcast_last(apow[j + 1][0], 32)
            pi_ = bcast_last(apow[j + 1][1], 32)
            SF.op(lambda E, pr_=pr_: E.tensor_tensor(Ft1b, CPr2, pr_, ALU.mult))
            SF.op(lambda E, pi_=pi_: E.tensor_tensor(Ft2b, CPni2, pi_, ALU.mult))
            SF.op(lambda E, j=j: E.tensor_tensor(Fm[:, j, 0, :].rearrange("p (a b) -> p a b", b=32), Ft1b, Ft2b, ALU.add))
            SF.op(lambda E, pr_=pr_: E.tensor_tensor(Ft1b, CPni2, pr_, ALU.mult))
            SF.op(lambda E, pi_=pi_: E.tensor_tensor(Ft2b, CPr2, pi_, ALU.mult))
            SF.op(lambda E, j=j: E.tensor_tensor(Fm[:, j, 1, :].rearrange("p (a b) -> p a b", b=32), Ft1b, Ft2b, ALU.subtract))
        P.barrier()
        A.pop()
        ssm_chunk(NCH - 1, True, P.all_tokens(), ucs[(NCH - 1) % 2], Fm=Fm)
        P.barrier()
        A.pop()

        def make_wring(n):
            return {"slots": [A.alloc([128, 16, 256], BF16) for _ in range(n)], "free": [[] for _ in range(n)], "i": 0}

        def linear(w_view, KTn, m_tiles, rhs_fn, evac_fn, wr, bank_sets, rhs_deps):
            nkg = KTn // 16
            for pi_ in range(0, len(m_tiles), 2):
                mp = m_tiles[pi_:pi_ + 2]
                bset = bank_sets[(pi_ // 2) % len(bank_sets)]
                lastmm = {}
                for g in range(nkg):
                    wi_ = wr["i"]; wr["i"] = (wi_ + 1) % len(wr["slots"])
                    wsl = wr["slots"][wi_]
                    t_w = P.dma("pool", wsl[:, :, 0:128 * len(mp)], w_view[:, g * 16:(g + 1) * 16, mp[0] * 128:mp[0] * 128 + 128 * len(mp)], [wr["free"][wi_]])
                    wr["free"][wi_] = []
                    for mi in range(len(mp)):
                        for nt in range(2):
                            b = bset[mi * 2 + nt]
                            t_mm = None
                            for kl in range(16):
                                kt = g * 16 + kl
                                firstb = (g == 0 and kl == 0)
                                t_mm = P.op("pe", lambda E, b=b, wsl=wsl, kl=kl, mi=mi, kt=kt, nt=nt, firstb=firstb, g=g: E.matmul(
                                    banks[b][:, :], wsl[:, kl, mi * 128:mi * 128 + 128], rhs_fn(kt, nt), start=firstb, stop=(g == nkg - 1 and kl == 15)),
                                    ([t_w, rhs_deps, bfree[b]] if firstb else ([t_w] if kl == 0 else [])), inc=(kl == 15))
                            lastmm[(mi, nt)] = t_mm
                            wr["free"][wi_].append(t_mm)
                for mi, m in enumerate(mp):
                    for nt in range(2):
                        b = bset[mi * 2 + nt]
                        bfree[b] = evac_fn(m, nt, b, lastmm[(mi, nt)])

        def stats_of(src_fn, ntiles, rs_out, nfeat, sqb_, deps, stat_banks=(6, 7)):
            return rms_stats(src_fn, ntiles, 1024, rs_out, nfeat, sqb_, list(stat_banks), deps)

        A.push()
        mixed = A.alloc([128, 16, 1024], BF16)
        A.push()
        yg = A.alloc([128, 8, 1024], F32)
        ygb = A.alloc([128, 8, 1024], BF16)
        wg = A.alloc([128, 8, 1024], BF16)
        gt2 = [A.alloc([128, 512], F32) for _ in range(2)]
        sqb2 = [A.alloc([128, 512], BF16) for _ in range(2)]
        rs2 = A.alloc([128, 1024], F32)
        t_wg = P.dma("pool", wg, w_glu.rearrange("(kt p) m -> p kt m", p=128))
        t_yb = []
        for ct in range(8):
            t_l = P.dma("sp", yg[:, ct, :], yg_s[ct * 128:(ct + 1) * 128, :])
            t_yb.append(P.op("act", lambda E, ct=ct: E.copy(ygb[:, ct, :], yg[:, ct, :]), [t_l]))
        tap("yg", yg.rearrange("p a b -> p (a b)"), [128, 8 * 1024])
        gfree = [None, None]
        glu_done = []
        for m in range(8):
            for nt in range(2):
                b = (m * 2 + nt) % 4
                t_mm = None
                for kt in range(8):
                    t_mm = P.op("pe", lambda E, b=b, kt=kt, m=m, nt=nt: E.matmul(banks[b][:, :], wg[:, kt, m * 128:(m + 1) * 128], ygb[:, kt, nt * 512:(nt + 1) * 512],
                                                                              start=(kt == 0), stop=(kt == 7)),
                                [t_wg, t_yb, bfree[b]] if kt == 0 else [], inc=(kt == 7))
                gi = (m * 2 + nt) % 2
                t_s = P.op("act", lambda E, b=b, gi=gi, m=m: E.activation(gt2[gi], banks[b][:, :], AF.Sigmoid, bias=gcol("b_glu", m)), [t_mm, gfree[gi]])
                bfree[b] = t_s
                t_g = P.op("dve", lambda E, gi=gi, m=m, nt=nt: E.tensor_tensor(yg[:, m, nt * 512:(nt + 1) * 512], yg[:, m, nt * 512:(nt + 1) * 512], gt2[gi], ALU.mult), [t_s])
                gfree[gi] = t_g
                glu_done.append(t_g)
        tap("ssm", yg.rearrange("p a b -> p (a b)"), [128, 8 * 1024])
        t_rs2 = stats_of(lambda i, nt: yg[:, i, nt * 512:(nt + 1) * 512], 8, rs2, 1024.0, sqb2, glu_done)
        for ct in range(8):
            for nt in range(2):
                P.op("dve", lambda E, ct=ct, nt=nt: E.scalar_tensor_tensor(mixed[:, 8 + ct, nt * 512:(nt + 1) * 512], yg[:, ct, nt * 512:(nt + 1) * 512],
                                                                          gcol("ssm_n", ct), rs2[:, nt * 512:(nt + 1) * 512], ALU.mult, ALU.mult), [t_rs2])
        P.barrier()
        A.pop()

        A.push()
        attnT = A.alloc([128, 8, 1024], F32)
        head_state = {"gi": 0, "vtok_free": None, "rec_free": None}
        SCALE = 1.0 / math.sqrt(128.0)
        xg_sv = xg_s.rearrange("(kt p) t -> p kt t", p=128)
        for hg in range(2):
            A.push()
            KTt = A.alloc([128, 4, 3072], BF16)
            VTt = A.alloc([128, 4, 3072], BF16)
            QTt = A.alloc([128, 4, 1024], BF16)
            A.push()
            xg3 = A.alloc([128, 16, 1024], BF16)
            rs3 = A.alloc([128, 1024], F32)
            wr3 = make_wring(3)
            for sc in range(3):
                t_x3 = P.dma("sp", xg3, xg_sv[:, :, 1024 * sc:1024 * sc + 1024], [P.all_tokens()] if (sc > 0 or hg > 0) else [])
                t_r3 = P.dma("sp", rs3, rs_s[:, 1024 * sc:1024 * sc + 1024], [P.all_tokens()] if (sc > 0 or hg > 0) else [])
                jobs = [(1024 + 512 * hg, KTt), (1024 + 512 * hg + 256, KTt), (2048 + 512 * hg, VTt), (2048 + 512 * hg + 256, VTt)]
                if sc == 2:
                    jobs += [(512 * hg, QTt), (512 * hg + 256, QTt)]
                for (wc0, dstT) in jobs:
                    hl0 = ((wc0 % 1024) - 512 * hg) // 128

                    def ev3(m, nt, b, t_mm, dstT=dstT, hl0=hl0, sc=sc):
                        hl = hl0 + m
                        if dstT is QTt:
                            dst = dstT[:, hl, nt * 512:(nt + 1) * 512]
                        else:
                            dst = dstT[:, hl, 1024 * sc + nt * 512:1024 * sc + (nt + 1) * 512]
                        return P.op("dve", lambda E, dst=dst, b=b, nt=nt: E.tensor_tensor(dst, banks[b][:, :], rs3[:, nt * 512:(nt + 1) * 512], ALU.mult), [t_mm, t_r3])
                    linear(w_in_v[:, :, wc0:wc0 + 256], 16, [0, 1], lambda kt, nt: xg3[:, kt, nt * 512:(nt + 1) * 512], ev3, wr3,
                           [[0, 1, 2, 3], [4, 5, 6, 7]], [t_x3])
            P.barrier()
            A.pop()
            if hg == 0:
                tap("KT0", KTt[:, 0, :], [128, 3072])
                tap("VT0", VTt[:, 0, :], [128, 3072])
                tap("QT0", QTt[:, 0, :], [128, 1024])
            A.push()
            Vtoks = [A.alloc([128, NVT + 3, 128], BF16) for _ in range(2)]
            NES = 6
            es = [A.alloc([128, 4, 128], BF16) for _ in range(NES)]
            pm = [A.alloc([128, 4, 128], BF16) for _ in range(NES)]
            rec = A.alloc([128, 512], F32)
            TB = banks[0][:, :].bitcast(BF16)
            es_free = [None] * NES
            pm_free = [None] * NES
            vtok_free = [None, None]
            for hl in range(4):
                h = 4 * hg + hl

                def do_head(hl, h):
                    Vtok = Vtoks[hl % 2]
                    vt_toks = []
                    for g0 in range(0, NVT, 8):
                        n = min(8, NVT - g0)
                        t_tr = None
                        for q in range(n):
                            d, r, m, nk = VT_LIST[g0 + q]
                            te0 = 2048 + r + d * m
                            t_tr = P.op("pe", lambda E, q=q, te0=te0, d=d, nk=nk: E.transpose(TB[0:nk, q * 128:(q + 1) * 128], VTt[:, hl, te0:te0 + d * (nk - 1) + 1:d], ident_b),
                                        [bfree[0], vtok_free[hl % 2]] if q == 0 else [], inc=(q == n - 1))
                        t_cp = P.op("act", lambda E, g0=g0, n=n: E.copy(Vtok[:, g0:g0 + n, :].rearrange("p a b -> p (a b)"), TB[:, 0:128 * n]), [t_tr])
                        bfree[0] = t_cp
                        vt_toks.append(t_cp)
                    tiles = []
                    for d in (1, 4, 16):
                        QB = min(128, NO // d)
                        for r in range(d):
                            for blk in range((NO // d) // QB):
                                m0 = blk * QB
                                for (mk, nk, mask) in ((m0 - 128, 128, mprev_b), (m0, QB, mcur_b)):
                                    vt = VT_IDX[(d, r, mk)] if (d, r, mk) in VT_IDX else None
                                    if vt is None:
                                        raise AssertionError((d, r, mk))
                                    outs = []
                                    if d == 16:
                                        outs = [(0, r, 16, 0, 32), (1, r, 16, 32, 64)]
                                    elif d == 4:
                                        outs = [(blk, r, 4, 0, 128)]
                                    else:
                                        outs = [(blk // 4, (blk % 4) * 128, 1, 0, 128)]
                                    tiles.append(dict(d=d, r=r, m0=m0, mk=mk, nk=nk, QB=QB, mask=mask, vt=vt, outs=outs))
                    started = set()
                    tix = {(T["d"], T["r"], T["m0"], T["mk"]): T for T in tiles}
                    groups = []
                    for half in range(2):
                        for ab in range(2):
                            grp = [tix[(1, 0, 128 * blk, 128 * blk - 128 if ab == 0 else 128 * blk)] for blk in range(4 * half, 4 * half + 4)]
                            groups.append((grp, ("d1", half)))
                    for blk in range(2):
                        for ab in range(2):
                            grp = [tix[(4, r, 128 * blk, 128 * blk - 128 if ab == 0 else 128 * blk)] for r in range(4)]
                            groups.append((grp, ("d4", blk)))
                    for r0 in range(0, 16, 4):
                        for ab in range(2):
                            grp = [tix[(16, r, 0, -128 if ab == 0 else 0)] for r in range(r0, r0 + 4)]
                            groups.append((grp, ("d16", r0)))
                    assert sum(len(g[0]) for g in groups) == len(tiles)
                    for (grp, gkind) in groups:
                        gi = (head_state["gi"]) % NES
                        sb_ = (1, 2, 7)[head_state["gi"] % 3]
                        head_state["gi"] += 1
                        t_s = None
                        for q, T in enumerate(grp):
                            d, r = T["d"], T["r"]
                            k0 = 2048 + r + d * T["mk"]
                            q0 = r + d * T["m0"]
                            t_s = P.op("pe", lambda E, q=q, k0=k0, q0=q0, d=d, nk=T["nk"], QB=T["QB"], sb_=sb_: E.matmul(
                                banks[sb_][0:nk, q * 128:q * 128 + QB], KTt[:, hl, k0:k0 + d * (nk - 1) + 1:d], QTt[:, hl, q0:q0 + d * (QB - 1) + 1:d],
                                start=True, stop=True, skip_group_check=True), [bfree[sb_]] if q == 0 else [], inc=(q == len(grp) - 1))
                        ng = len(grp)
                        t_e = P.op("act", lambda E, gi=gi, sb_=sb_, ng=ng: E.activation(es[gi][:, 0:ng, :].rearrange("p a b -> p (a b)"), banks[sb_][:, 0:128 * ng], AF.Exp, scale=SCALE),
                                   [t_s, es_free[gi]])
                        bfree[sb_] = t_e
                        t_ms = []
                        for q, T in enumerate(grp):
                            nk, QB = T["nk"], T["QB"]
                            t_ms.append(P.op("dve", lambda E, gi=gi, q=q, nk=nk, QB=QB, mask=T["mask"], vt=T["vt"]: E.scalar_tensor_tensor(
                                pm[gi][0:nk, q, 0:QB], es[gi][0:nk, q, 0:QB], vcols[0:nk, vt:vt + 1], mask[0:nk, 0:QB], ALU.mult, ALU.mult),
                                [t_e, pm_free[gi]] if q == 0 else [t_e]))
                        es_free[gi] = t_ms
                        t_pv = None
                        for q, T in enumerate(grp):
                            nk = T["nk"]
                            for (half, off, step, c0, c1) in T["outs"]:
                                ncol = c1 - c0
                                ob = 3 + half
                                st = ob not in started
                                started.add(ob)
                                lhs = Vtok[0:nk, T["vt"], :]
                                t_pv = P.op("pe", lambda E, ob=ob, off=off, step=step, ncol=ncol, lhs=lhs, gi=gi, q=q, nk=nk, c0=c0, c1=c1, st=st: E.matmul(
                                    banks[ob][:, off:off + step * (ncol - 1) + 1:step], lhs, pm[gi][0:nk, q, c0:c1], start=st, stop=False, skip_group_check=True),
                                    [t_ms, vt_toks, bfree[ob]] if st else [t_ms[q]], inc=True)
                        nkg = grp[0]["nk"]
                        if gkind[0] == "d1":
                            dens = [(5 + gkind[1], banks[5 + gkind[1]][:, :], pm[gi][0:nkg].rearrange("p a b -> p (a b)"))]
                        elif gkind[0] == "d4":
                            dens = [(5 + gkind[1], banks[5 + gkind[1]][:, :].rearrange("p (i r) -> p r i", r=4)[:, :, 64 * hf:64 * hf + 64], pm[gi][0:nkg, :, 64 * hf:64 * hf + 64])
                                    for hf in range(2)]
                        else:
                            r0 = gkind[1]
                            dens = [(5 + hf, banks[5 + hf][:, :].rearrange("p (i r) -> p r i", r=16)[:, r0:r0 + 4, :], pm[gi][0:nkg, :, 32 * hf:32 * hf + 32]) for hf in range(2)]
                        for (ob, oap, rap) in dens:
                            st = ob not in started
                            started.add(ob)
                            t_pv = P.op("pe", lambda E, oap=oap, rap=rap, nkg=nkg, st=st: E.matmul(oap, ones_b[0:nkg, :], rap, start=st, stop=False, skip_group_check=True),
                                        [t_ms, bfree[ob]] if st else [t_ms], inc=True)
                        pm_free[gi] = t_pv
                    fin = []
                    for half in range(2):
                        t_r = P.op("dve", lambda E, half=half: E.reciprocal(rec, banks[5 + half][:, :]), [t_pv, head_state["rec_free"]])
                        t_f = P.op("dve", lambda E, half=half: E.tensor_tensor(attnT[:, h, half * 512:(half + 1) * 512], banks[3 + half][:, :], rec, ALU.mult), [t_r])
                        head_state["rec_free"] = t_f
                        bfree[5 + half] = t_r
                        bfree[3 + half] = t_f
                        fin.append(t_f)
                    vtok_free[hl % 2] = t_pv
                do_head(hl, h)
            P.barrier()
            A.pop()
            A.pop()
        tap("attn", attnT.rearrange("p a b -> p (a b)"), [128, 8 * 1024])
        A.push()
        sqb4 = [A.alloc([128, 512], BF16) for _ in range(2)]
        rs4 = A.alloc([128, 1024], F32)
        t_rs4 = stats_of(lambda i, nt: attnT[:, i, nt * 512:(nt + 1) * 512], 8, rs4, 1024.0, sqb4, [])
        for hh in range(8):
            for nt in range(2):
                P.op("dve", lambda E, hh=hh, nt=nt: E.scalar_tensor_tensor(mixed[:, hh, nt * 512:(nt + 1) * 512], attnT[:, hh, nt * 512:(nt + 1) * 512],
                                                                          gcol("attn_n", hh), rs4[:, nt * 512:(nt + 1) * 512], ALU.mult, ALU.mult), [t_rs4])
        P.barrier()
        A.pop()
        A.pop()
        tap("mixed", mixed.rearrange("p a b -> p (a b)"), [128, 16 * 1024])

        def residual_pass(src_load, base_load, gname, rs_, dst_store, tmp_ring, deps, inplace_src=None):
            fr = [None] * len(tmp_ring)
            k = 0
            outs = []
            for m in range(16):
                for nt in range(2):
                    a_, b_ = tmp_ring[k % len(tmp_ring)]
                    if inplace_src is not None:
                        a_ = inplace_src(m, nt)
                        t_a = None
                    else:
                        t_a = src_load(m, nt, a_, [fr[k % len(tmp_ring)], deps])
                    t_b = base_load(m, nt, b_, [fr[k % len(tmp_ring)], deps])
                    t1_ = P.op("dve", lambda E, a_=a_, m=m, nt=nt: E.scalar_tensor_tensor(a_, a_, gcol(gname, m), rs_[:, nt * 512:(nt + 1) * 512], ALU.mult, ALU.mult), [t_a, deps])
                    t2_ = P.op("dve", lambda E, a_=a_, b_=b_: E.tensor_tensor(b_, b_, a_, ALU.add), [t1_, t_b])
                    t3_ = dst_store(m, nt, b_, [t2_])
                    fr[k % len(tmp_ring)] = t3_
                    outs.append(t3_)
                    k += 1
            return outs

        def load_full(dst, src_dram, deps):
            toks = []
            for m in range(16):
                toks.append(P.dma("sp", dst[:, m, :], src_dram[m * 128:(m + 1) * 128, :], deps))
            return toks

        def prenorm(hT_, gname, hn_, sqb_, rs_, deps):
            t_rs = stats_of(lambda i, nt: hT_[:, i, nt * 512:(nt + 1) * 512], 16, rs_, float(D), sqb_, deps)
            toks = []
            for m in range(16):
                for nt in range(2):
                    toks.append(P.op("dve", lambda E, m=m, nt=nt: E.scalar_tensor_tensor(hn_[:, m, nt * 512:(nt + 1) * 512], hT_[:, m, nt * 512:(nt + 1) * 512],
                                                                                         gcol(gname, m), rs_[:, nt * 512:(nt + 1) * 512], ALU.mult, ALU.mult), [t_rs]))
            return toks

        A.push()
        mixT = A.alloc([128, 16, 1024], F32)
        wr4 = make_wring(3)
        sqb5 = [A.alloc([128, 512], BF16) for _ in range(2)]
        rs5 = A.alloc([128, 1024], F32)

        def ev4(m, nt, b, t_mm):
            return P.op("act", lambda E, m=m, nt=nt, b=b: E.copy(mixT[:, m, nt * 512:(nt + 1) * 512], banks[b][:, :]), [t_mm])
        linear(w_out.rearrange("(kt p) m -> p kt m", p=128), 16, list(range(16)), lambda kt, nt: mixed[:, kt, nt * 512:(nt + 1) * 512], ev4, wr4,
               [[0, 1, 2, 3], [4, 5, 6, 7]], [])
        P.barrier()
        t_rs5 = stats_of(lambda i, nt: mixT[:, i, nt * 512:(nt + 1) * 512], 16, rs5, float(D), sqb5, [])
        xoff = NE - NO
        xring4 = [A.alloc([128, 512], F32) for _ in range(4)]
        x4free = [None] * 4
        h_toks = []
        k4 = 0
        for m in range(16):
            for nt in range(2):
                sl = slice(nt * 512, (nt + 1) * 512)
                xb = xring4[k4 % 4]
                t_x = P.dma("sp", xb, xT[m * 128:(m + 1) * 128, xoff + nt * 512:xoff + (nt + 1) * 512], [x4free[k4 % 4]])
                t1_ = P.op("dve", lambda E, m=m, sl=sl: E.scalar_tensor_tensor(mixT[:, m, sl], mixT[:, m, sl], gcol("mix_post", m), rs5[:, sl], ALU.mult, ALU.mult), [t_rs5])
                t2_ = P.op("dve", lambda E, m=m, sl=sl, xb=xb: E.tensor_tensor(mixT[:, m, sl], mixT[:, m, sl], xb, ALU.add), [t1_, t_x])
                x4free[k4 % 4] = t2_
                P.dma("pool", h_s[m * 128:(m + 1) * 128, sl], mixT[:, m, sl], [t2_])
                h_toks.append(t2_)
                k4 += 1
        tap("h1", mixT.rearrange("p a b -> p (a b)"), [128, 16 * 1024])
        hn2 = mixed
        rs5b = A.alloc([128, 1024], F32)
        prenorm(mixT, "mlp_pre", hn2, sqb5, rs5b, lambda i, nt: h_toks[2 * i + nt])
        P.barrier()
        A.pop()

        act = A.alloc([128, 64, 1024], BF16)
        wr5 = make_wring(3)
        rtmp = [A.alloc([128, 512], F32) for _ in range(3)]
        rfree = [None] * 3
        rk = [0]

        def ev_up(m, nt, b, t_mm):
            i = rk[0] % 3
            rk[0] += 1
            t_r = P.op("act", lambda E, i=i, b=b: E.activation(rtmp[i], banks[b][:, :], AF.Relu), [t_mm, rfree[i]])
            t_q = P.op("dve", lambda E, i=i, m=m, nt=nt: E.tensor_tensor(act[:, m, nt * 512:(nt + 1) * 512], rtmp[i], rtmp[i], ALU.mult), [t_r])
            rfree[i] = t_q
            return t_r
        linear(w_up.rearrange("(kt p) m -> p kt m", p=128), 16, list(range(64)), lambda kt, nt: hn2[:, kt, nt * 512:(nt + 1) * 512], ev_up, wr5,
               [[0, 1, 2, 3], [4, 5, 6, 7]], [])
        P.barrier()

        def ev_dn(m, nt, b, t_mm):
            i = rk[0] % 3
            rk[0] += 1
            t_c = P.op("act", lambda E, i=i, b=b: E.copy(rtmp[i], banks[b][:, :]), [t_mm, rfree[i]])
            rfree[i] = P.dma("sp", ff_s[m * 128:(m + 1) * 128, nt * 512:(nt + 1) * 512], rtmp[i], [t_c])
            return t_c
        linear(w_down.rearrange("(kt p) m -> p kt m", p=128), 64, list(range(16)), lambda kt, nt: act[:, kt, nt * 512:(nt + 1) * 512], ev_dn, wr5,
               [[0, 1, 2, 3], [4, 5, 6, 7]], [])
        P.barrier()
        A.pop()

        A.push()
        h2 = A.alloc([128, 16, 1024], F32)
        prod = A.alloc([128, 16, 1024], F32)
        hn3 = A.alloc([128, 16, 1024], BF16)
        sqb7 = [A.alloc([128, 512], BF16) for _ in range(2)]
        rs7 = A.alloc([128, 1024], F32)
        t_ff = load_full(prod, ff_s, [])
        t_h2 = load_full(h2, h_s, [])
        t_rs7 = stats_of(lambda i, nt: prod[:, i, nt * 512:(nt + 1) * 512], 16, rs7, float(D), sqb7, lambda i, nt: t_ff[i])
        res_t = []
        for m in range(16):
            for nt in range(2):
                sl = slice(nt * 512, (nt + 1) * 512)
                t1_ = P.op("dve", lambda E, m=m, sl=sl: E.scalar_tensor_tensor(prod[:, m, sl], prod[:, m, sl], gcol("mlp_post", m), rs7[:, sl], ALU.mult, ALU.mult), [t_rs7])
                res_t.append(P.op("dve", lambda E, m=m, sl=sl: E.tensor_tensor(h2[:, m, sl], h2[:, m, sl], prod[:, m, sl], ALU.add), [t1_, t_h2]))
        P.barrier()
        tap("h2", h2.rearrange("p a b -> p (a b)"), [128, 16 * 1024])
        rs7b = A.alloc([128, 1024], F32)
        prenorm(h2, "ple_pre", hn3, sqb7, rs7b, lambda i, nt: res_t[2 * i + nt])
        P.barrier()
        wr6 = make_wring(2)
        wpp = A.alloc([128, 2, D], BF16)
        pTb = A.alloc([128, 2, 1024], BF16)
        gt6 = [A.alloc([128, 512], F32) for _ in range(2)]
        g6free = [None, None]
        g6k = [0]
        t_wpp = P.dma("pool", wpp, w_pp.rearrange("(kt p) m -> p kt m", p=128))
        t_pT = P.dma("pool", pTb, pT.rearrange("(kt p) t -> p kt t", p=128))

        prod_t = {}

        def ev6(m, nt, b, t_mm):
            i = g6k[0] % 2
            g6k[0] += 1
            eb = 4 + (b % 4)
            P.op("pe", lambda E, eb=eb, m=m, nt=nt: E.matmul(banks[eb][:, :], wpp[:, 0, m * 128:(m + 1) * 128], pTb[:, 0, nt * 512:(nt + 1) * 512], start=True, stop=False),
                 [t_wpp, t_pT, bfree[eb]], inc=False)
            t_e = P.op("pe", lambda E, eb=eb, m=m, nt=nt: E.matmul(banks[eb][:, :], wpp[:, 1, m * 128:(m + 1) * 128], pTb[:, 1, nt * 512:(nt + 1) * 512], start=False, stop=True), [])
            t_s = P.op("act", lambda E, i=i, b=b: E.activation(gt6[i], banks[b][:, :], AF.Sigmoid), [t_mm, g6free[i]])
            t_p = P.op("dve", lambda E, i=i, eb=eb, m=m, nt=nt: E.tensor_tensor(prod[:, m, nt * 512:(nt + 1) * 512], gt6[i], banks[eb][:, :], ALU.mult), [t_s, t_e])
            g6free[i] = t_p
            bfree[eb] = t_p
            prod_t[(m, nt)] = t_p
            return t_s
        linear(w_pg.rearrange("(kt p) m -> p kt m", p=128), 16, list(range(16)), lambda kt, nt: hn3[:, kt, nt * 512:(nt + 1) * 512], ev6, wr6,
               [[0, 1, 2, 3]], [])
        P.barrier()
        t_rs8 = stats_of(lambda i, nt: prod[:, i, nt * 512:(nt + 1) * 512], 16, rs7, float(D), sqb7, [])
        for m in range(16):
            for nt in range(2):
                sl = slice(nt * 512, (nt + 1) * 512)
                t1_ = P.op("dve", lambda E, m=m, sl=sl: E.scalar_tensor_tensor(prod[:, m, sl], prod[:, m, sl], gcol("ple_post", m), rs7[:, sl], ALU.mult, ALU.mult), [t_rs8])
                t2_ = P.op("dve", lambda E, m=m, sl=sl: E.tensor_tensor(h2[:, m, sl], h2[:, m, sl], prod[:, m, sl], ALU.add), [t1_])
                P.dma("sp", out_d[m * 128:(m + 1) * 128, sl], h2[:, m, sl], [t2_])
        P.barrier()
        A.pop()
        P.emit()
    return nc, tap_out


def col_layout(v):
    v = np.asarray(v, np.float32).reshape(-1)
    return v.reshape(-1, 128).T


def prep_shared(inp):
    sh = {}
    colsv = [inp["mix_norm_pre"][0], inp["attn_out_norm"][0], inp["ssm_out_norm"][0], inp["mix_norm_post"][0],
             inp["mlp_norm_pre"][0], inp["mlp_norm_post"][0], inp["ple_norm_pre"][0], inp["ple_norm_post"][0],
             inp["ssm_d"][0], inp["b_glu"][0]]
    sh["cols"] = np.ascontiguousarray(np.concatenate([col_layout(v) for v in colsv], axis=1))

    def st(v):
        v = np.asarray(v, np.float32).reshape(32, 2, 64)
        return v.transpose(1, 2, 0).reshape(128, 32)
    ldt = np.broadcast_to(np.asarray(inp["log_dt"][0], np.float32)[:, None], (64, 64))
    sh["sp"] = np.ascontiguousarray(np.concatenate([st(inp["lam_re"][0]), st(inp["lam_im"][0]), st(ldt)], axis=1))

    def padB(B):
        B = np.asarray(B, np.float32).reshape(32, 2, 64, 16)
        o = np.zeros((2, 64, 32, 2, 16), np.float32)
        for gl in range(2):
            o[gl, :, :, gl, :] = B[:, gl].transpose(1, 0, 2)
        return o.reshape(128, 1024)

    def padC(C):
        C = np.asarray(C, np.float32).reshape(32, 2, 16, 64)
        o = np.zeros((2, 64, 32, 2, 16), np.float32)
        for gl in range(2):
            o[gl, :, :, gl, :] = C[:, gl].transpose(2, 0, 1)
        return o.reshape(128, 1024)
    sh["bpr"] = padB(inp["ssm_b_re"][0])
    sh["bpi"] = padB(inp["ssm_b_im"][0])
    sh["cpr"] = padC(inp["ssm_c_re"][0])
    sh["cpi"] = padC(inp["ssm_c_im"][0])
    for k_, n_ in (("w_in", "w_in"), ("w_glu", "w_glu"), ("w_out", "w_out"), ("w_up", "w_up"), ("w_down", "w_down"),
                   ("w_ple_gate", "w_pg"), ("w_ple_proj", "w_pp")):
        sh[n_] = np.ascontiguousarray(np.asarray(inp[k_][0], np.float32))
    return sh


def consts_for_core(j):
    kk = np.arange(128)[:, None]
    ii = np.arange(128)[None, :]
    ident = (kk == ii).astype(np.float32)
    mprev = (kk >= ii).astype(np.float32)
    mcur = (kk <= ii).astype(np.float32)
    bmask = ((kk // 16) == (ii // 16)).astype(np.float32)
    pmask = ((kk // 32) == np.arange(4)[None, :]).astype(np.float32)
    T0 = 1024 * j
    vc = np.zeros((128, NVT), np.float32)
    for i, (d, r, m, nk) in enumerate(VT_LIST):
        t_abs = T0 + r + d * (m + np.arange(nk))
        vc[:nk, i] = (t_abs >= 0).astype(np.float32)
    return np.ascontiguousarray(np.concatenate([ident, mprev, mcur, bmask, pmask, vc], axis=1))


def prep_core(c, inp, sh):
    b, j = c // 4, c % 4
    T0 = 1024 * j
    x = np.asarray(inp["x"], np.float32)
    xe = np.zeros((NE, D), np.float32)
    lo = T0 - (NE - NO)
    s0 = max(lo, 0)
    xe[s0 - lo:, :] = x[b, s0:T0 + NO, :]
    m = dict(sh)
    m["xT"] = np.ascontiguousarray(xe.T)
    m["pT"] = np.ascontiguousarray(np.asarray(inp["p"], np.float32)[0, b, T0:T0 + NO, :].T)
    m["consts"] = consts_for_core(j)
    return m


_CACHE = {}


def kernel(**inputs):
    if "nc" not in _CACHE:
        _CACHE["nc"] = build_nc()[0]
    nc = _CACHE["nc"]
    sh = prep_shared(inputs)
    in_maps = [prep_core(c, inputs, sh) for c in range(8)]
    res = run_bass_kernel_spmd(nc, in_maps, core_ids=list(range(8)))
    out = np.zeros((2, 4096, D), np.float32)
    for c in range(8):
        b, j = c // 4, c % 4
        out[b, 1024 * j:1024 * (j + 1), :] = res.results[c]["out"].T
    return out
```

```python
import math
import os
KSTOP = int(os.environ.get('KSTOP', '0'))
KNG = int(os.environ.get('KNG', '8'))
KNH = int(os.environ.get('KNH', '8'))
KNOF = int(os.environ.get('KNOF', '0'))
KSIDE = int(os.environ.get('KSIDE', '1'))
from contextlib import ExitStack

import numpy as np
import concourse.bass as bass
import concourse.mybir as mybir
from concourse.bass_utils import run_bass_kernel_spmd

F32 = mybir.dt.float32
BF16 = mybir.dt.bfloat16
ALU = mybir.AluOpType
AF = mybir.ActivationFunctionType

D = 2048
KT = 16
NE = 4096
NO = 1024
NCH = 4
DFF = 8192
EPS = 1e-6
MAGIC = 12582912.0
TWO_PI = 2.0 * math.pi

COLS = [("mix_pre", 16), ("attn_n", 8), ("ssm_n", 8), ("mix_post", 16), ("mlp_pre", 16),
        ("mlp_post", 16), ("ple_pre", 16), ("ple_post", 16), ("ssm_d", 8), ("b_glu", 8)]
COL_OFF = {}
_o = 0
for _n, _c in COLS:
    COL_OFF[_n] = _o
    _o += _c
NCOLS = _o


def vtile_list():
    tiles = []
    for d in (1, 4, 16):
        M = NO // d
        for r in range(d):
            m = -128
            while m < M:
                nk = min(128, M - m)
                tiles.append((d, r, m, nk))
                m += 128
    return tiles


VT_LIST = vtile_list()
VT_IDX = {(d, r, m): i for i, (d, r, m, nk) in enumerate(VT_LIST)}
NVT = len(VT_LIST)


class Prog:
    ENG = ("pe", "act", "dve", "pool", "sp")

    def __init__(self, nc, stack, n_dma_sems=32):
        self.nc = nc
        self.q = {e: [] for e in self.ENG}
        self.cnt = {e: 0 for e in self.ENG}
        self.sem = {e: stack.enter_context(nc.semaphore("s_" + e)) for e in self.ENG}
        self.waited = {}
        n_sw = 16
        self.dsem = [stack.enter_context(nc.semaphore("d%d" % i)) for i in range(n_dma_sems + n_sw)]
        self.dcnt = [0] * (n_dma_sems + n_sw)
        self.drange = {"sp": (0, n_dma_sems), "pool": (n_dma_sems, n_dma_sems + n_sw)}
        self.dnext = {"sp": 0, "pool": n_dma_sems}
        self.ninst = 0

    def _waits(self, eng, deps):
        for d in deps:
            if d is None:
                continue
            if isinstance(d, list) or (isinstance(d, tuple) and len(d) and not isinstance(d[0], str)):
                self._waits(eng, d)
                continue
            kind, key, val = d
            wk = (eng, kind, key)
            if self.waited.get(wk, 0) >= val:
                continue
            self.waited[wk] = val
            sem = self.sem[key] if kind == "e" else self.dsem[key]
            self.q[eng].append(lambda E, sem=sem, val=val: E.wait_ge(sem, val))
            self.ninst += 1

    def op(self, eng, fn, deps=(), inc=True):
        self._waits(eng, deps)
        self.ninst += 1
        if inc:
            self.cnt[eng] += 1
            v = self.cnt[eng]
            sem = self.sem[eng]
            self.q[eng].append(lambda E, fn=fn, sem=sem: fn(E).then_inc(sem, 1))
            return ("e", eng, v)
        self.q[eng].append(lambda E, fn=fn: fn(E))
        return None

    def dma(self, eng, out, in_, deps=(), **kw):
        lo, hi = self.drange[eng]
        i = self.dnext[eng]
        self.dnext[eng] = lo + (i + 1 - lo) % (hi - lo)
        if self.dcnt[i] > 0:
            self._waits(eng, [("d", i, self.dcnt[i])])
        self._waits(eng, deps)
        self.dcnt[i] += 16
        sem = self.dsem[i]
        self.ninst += 1
        self.q[eng].append(lambda E, out=out, in_=in_, sem=sem, kw=kw: E.dma_start(out=out, in_=in_, **kw).then_inc(sem, 16))
        return ("d", i, self.dcnt[i])

    def all_tokens(self):
        toks = [("e", e, self.cnt[e]) for e in self.ENG if self.cnt[e] > 0]
        toks += [("d", i, self.dcnt[i]) for i in range(len(self.dsem)) if self.dcnt[i] > 0]
        return toks

    def barrier(self):
        toks = self.all_tokens()
        for e in self.ENG:
            self._waits(e, toks)

    def emit(self):
        nc = self.nc
        self._waits("sp", self.all_tokens())
        with nc.Block() as block:
            @block.tensor
            def _(E):
                for f in self.q["pe"]:
                    f(E)

            @block.scalar
            def _(E):
                for f in self.q["act"]:
                    f(E)

            @block.vector
            def _(E):
                for f in self.q["dve"]:
                    f(E)

            @block.gpsimd
            def _(E):
                for f in self.q["pool"]:
                    f(E)

            @block.sync
            def _(E):
                for f in self.q["sp"]:
                    f(E)


class Seq:
    def __init__(self, P, eng, deps=()):
        self.P = P
        self.eng = eng
        self.last = list(deps)

    def op(self, fn, deps=(), eng=None):
        e = eng or self.eng
        t = self.P.op(e, fn, [self.last, list(deps)])
        self.last = [t]
        return t

    def par(self, fns):
        toks = [self.P.op(self.eng, fn, [self.last]) for fn in fns]
        self.last = toks
        return toks


class Arena:
    def __init__(self, big, total):
        self.big = big
        self.total = total
        self.top = 0
        self.marks = []
        self.peak = 0

    def push(self):
        self.marks.append(self.top)

    def pop(self):
        self.top = self.marks.pop()

    def alloc(self, shape, dt):
        n = 1
        for s in shape[1:]:
            n *= s
        es = 2 if dt == BF16 else 4
        nbytes = (n * es + 63) // 64 * 64
        off = self.top
        self.top += nbytes
        self.peak = max(self.peak, self.top)
        assert self.top <= self.total, ("arena overflow", self.top, self.total)
        ap = self.big[:, off // 4:(off + nbytes) // 4]
        if dt == BF16:
            ap = ap.bitcast(BF16)
        ap = ap[:, 0:n]
        fs = shape[1:]
        if len(fs) == 2:
            ap = ap.rearrange("p (a b) -> p a b", b=fs[1])
        elif len(fs) == 3:
            ap = ap.rearrange("p (a b c) -> p a b c", b=fs[1], c=fs[2])
        elif len(fs) == 4:
            ap = ap.rearrange("p (a b c d) -> p a b c d", b=fs[1], c=fs[2], d=fs[3])
        if shape[0] < 128:
            ap = ap[0:shape[0]]
        return ap


def bcast_last(ap2, n):
    return ap2.unsqueeze(2).to_broadcast([ap2.shape[0], ap2.shape[1], n])


def build_nc(taps=()):
    nc = bass.Bass("TRN2", target_bir_lowering=False)
    dr = lambda name, shape, dt=F32: nc.dram_tensor(name, shape, dt, kind="ExternalInput").ap()
    xT = dr("xT", [D, NE])
    pT = dr("pT", [256, NO])
    cols_d = dr("cols", [128, NCOLS])
    sp_d = dr("sp", [128, 96])
    bpr_d = dr("bpr", [128, 1024])
    bpi_d = dr("bpi", [128, 1024])
    cpr_d = dr("cpr", [128, 1024])
    cpi_d = dr("cpi", [128, 1024])
    consts_d = dr("consts", [128, 4 * 128 + 4 + NVT])
    w_in = dr("w_in", [D, 4096])
    w_glu = dr("w_glu", [1024, 1024])
    w_out = dr("w_out", [D, D])
    w_up = dr("w_up", [D, DFF])
    w_down = dr("w_down", [DFF, D])
    w_pg = dr("w_pg", [D, D])
    w_pp = dr("w_pp", [256, D])
    out_d = nc.dram_tensor("out", [D, NO], F32, kind="ExternalOutput").ap()
    xg_s = nc.dram_tensor("xg_s", [D, 3072], BF16).ap()
    rs_s = nc.dram_tensor("rs_s", [128, 3072], F32).ap()
    h_s = nc.dram_tensor("h_s", [D, NO], F32).ap()
    ff_s = nc.dram_tensor("ff_s", [D, NO], F32).ap()
    yg_s = nc.dram_tensor("yg_s", [1024, NO], F32).ap()
    tap_out = {}

    w_in_v = w_in.rearrange("(kt p) m -> p kt m", p=128)

    with ExitStack() as top:
        P = Prog(nc, top)
        TOTAL = 207 * 1024
        big = top.enter_context(nc.sbuf_tensor("big", [128, TOTAL // 4], F32))
        A = Arena(big, TOTAL)
        banks = [top.enter_context(nc.psum_tensor("bank%d" % i, [128, 512], F32)) for i in range(8)]
        bfree = [None] * 8

        def tap(name, ap, shape):
            if name not in taps:
                return
            P.barrier()
            t = nc.dram_tensor("tap_" + name, list(shape), ap.dtype, kind="ExternalOutput").ap()
            tap_out[name] = t
            P.dma("sp", t, ap)
            P.barrier()

        cols = A.alloc([128, NCOLS], F32)
        cst = A.alloc([128, 4 * 128 + 4 + NVT], F32)
        ident_f = cst[:, 0:128]
        mprev_f = cst[:, 128:256]
        mcur_f = cst[:, 256:384]
        bmask = cst[:, 384:512]
        pmask = cst[:, 512:516]
        vcols = cst[:, 516:516 + NVT]
        ident_b = A.alloc([128, 128], BF16)
        ones_b = A.alloc([128, 128], BF16)
        mprev_b = A.alloc([128, 128], BF16)
        mcur_b = A.alloc([128, 128], BF16)
        epsc = A.alloc([128, 1], F32)
        t_cols = P.dma("sp", cols, cols_d)
        t_cst = P.dma("sp", cst, consts_d)
        t0 = P.op("dve", lambda E: E.tensor_copy(ident_b, ident_f), [t_cst])
        t1 = P.op("dve", lambda E: E.tensor_copy(mprev_b, mprev_f), [t_cst])
        t2 = P.op("dve", lambda E: E.tensor_copy(mcur_b, mcur_f), [t_cst])
        t3 = P.op("pool", lambda E: E.memset(ones_b, 1.0))
        t4 = P.op("pool", lambda E: E.memset(epsc, EPS))
        P.barrier()

        if KSTOP == 3:
            tap("cst", cst, [128, 4 * 128 + 4 + NVT])
            P.emit()
            return nc, tap_out

        def gcol(name, i):
            o = COL_OFF[name] + i
            return cols[:, o:o + 1]

        def rms_stats(src_fn, ntiles, ntok, rs_out, nfeat, sqbufs, stat_banks, deps):
            toks = []
            nnt = ntok // 512
            sqfree = [None] * len(sqbufs)
            j = 0
            last_mm = [None] * nnt
            for i in range(ntiles):
                for nt in range(nnt):
                    sq = sqbufs[j % len(sqbufs)]
                    dp = deps(i, nt) if callable(deps) else deps
                    t_sq = P.op("act", lambda E, sq=sq, i=i, nt=nt: E.activation(sq, src_fn(i, nt), AF.Square), [dp, sqfree[j % len(sqbufs)]])
                    b = stat_banks[nt]
                    t_mm = P.op("pe", lambda E, sq=sq, b=b, i=i: E.matmul(banks[b][:, :], ones_b, sq, start=(i == 0), stop=(i == ntiles - 1)),
                                [t_sq, bfree[b] if i == 0 else None])
                    sqfree[j % len(sqbufs)] = t_mm
                    last_mm[nt] = t_mm
                    j += 1
            for nt in range(nnt):
                b = stat_banks[nt]
                ta = P.op("act", lambda E, b=b, nt=nt: E.activation(rs_out[:, nt * 512:(nt + 1) * 512], banks[b][:, :], AF.Sqrt, bias=epsc, scale=1.0 / nfeat), [last_mm[nt]])
                bfree[b] = ta
                tb = P.op("dve", lambda E, nt=nt: E.reciprocal(rs_out[:, nt * 512:(nt + 1) * 512], rs_out[:, nt * 512:(nt + 1) * 512]), [ta])
                toks.append(tb)
            return toks

        A.push()
        yg = None
        spv = A.alloc([128, 96], F32)
        G = A.alloc([128, 8, 2, 8, 128], BF16)
        H = A.alloc([128, 8, 8, 128], BF16)
        ctab = A.alloc([128, 32, 128], F32)
        stab = A.alloc([128, 32, 128], F32)
        sm = A.alloc([128, 48, 32], F32)
        Zin_r = sm[:, 0, :]
        Zin_i = sm[:, 1, :]
        rho8 = sm[:, 2, :]
        e8r = sm[:, 3, :]
        e8i = sm[:, 4, :]
        apow = [(sm[:, 5 + 2 * t, :], sm[:, 6 + 2 * t, :]) for t in range(9)]
        _n = [23]

        def smalloc():
            i = _n[0]
            _n[0] += 1
            assert i < 48
            return sm[:, i, :]

        t_sp = P.dma("sp", spv, sp_d)
        S = Seq(P, "dve", [t_sp, t_cols])
        S.op(lambda E: E.memset(Zin_r, 0.0))
        S.op(lambda E: E.memset(Zin_i, 0.0))

        def cmul(S, or_, oi_, ar, ai, br, bi, t1, t2):
            S.par([lambda E: E.tensor_tensor(or_, ar, br, ALU.mult),
                   lambda E: E.tensor_tensor(t1, ai, bi, ALU.mult),
                   lambda E: E.tensor_tensor(oi_, ar, bi, ALU.mult),
                   lambda E: E.tensor_tensor(t2, ai, br, ALU.mult)])
            S.par([lambda E: E.tensor_tensor(or_, or_, t1, ALU.subtract),
                   lambda E: E.tensor_tensor(oi_, oi_, t2, ALU.add)])

        A.push()
        CPr = A.alloc([128, 32, 32], F32)
        CPni = A.alloc([128, 32, 32], F32)
        CPi0 = A.alloc([128, 32, 32], F32)
        t_cpr = P.dma("sp", CPr, cpr_d.rearrange("p (a b) -> p a b", b=32))
        t_cpi = P.dma("sp", CPi0, cpi_d.rearrange("p (a b) -> p a b", b=32))
        S.last = [S.last, t_cpr, t_cpi]
        S.op(lambda E: E.memset(CPni, 0.0))
        S.op(lambda E: E.tensor_tensor(CPni, CPni, CPi0, ALU.subtract))
        lamr = spv[:, 0:32]
        lami = spv[:, 32:64]
        ldt = spv[:, 64:96]
        dt_ = smalloc(); zr = smalloc(); zi = smalloc(); em1 = smalloc(); mag = smalloc()
        c1 = smalloc(); s1 = smalloc(); sh = smalloc(); ta = smalloc(); tb = smalloc()
        numr = smalloc(); numi = smalloc(); cfr = smalloc(); cfi = smalloc(); tc = smalloc(); td = smalloc()
        S.op(lambda E: E.activation(dt_, ldt, AF.Exp), eng="act")
        S.op(lambda E: E.tensor_tensor(zr, lamr, dt_, ALU.mult))
        S.op(lambda E: E.tensor_tensor(zi, lami, dt_, ALU.mult))
        S.op(lambda E: E.tensor_scalar(em1, zr, 1.0 / 120.0, None, ALU.mult))
        for cst_ in (1.0 / 24.0, 1.0 / 6.0, 0.5, 1.0):
            S.op(lambda E, c=cst_: E.scalar_tensor_tensor(em1, em1, c, zr, ALU.add, ALU.mult))
        S.op(lambda E: E.tensor_scalar(mag, em1, 1.0, None, ALU.add))

        def sin_of(dst, src, shift, scale):
            S.op(lambda E: E.tensor_scalar(ta, src, scale, shift, ALU.mult, ALU.add))
            S.op(lambda E: E.tensor_scalar(tb, ta, 1.0 / TWO_PI, MAGIC, ALU.mult, ALU.add))
            S.op(lambda E: E.tensor_scalar(tb, tb, MAGIC, None, ALU.subtract))
            S.op(lambda E: E.scalar_tensor_tensor(ta, tb, -TWO_PI, ta, ALU.mult, ALU.add))
            S.op(lambda E: E.tensor_scalar(ta, ta, math.pi, -math.pi, ALU.min, ALU.max))
            S.op(lambda E: E.activation(dst, ta, AF.Sin), eng="act")

        sin_of(s1, zi, 0.0, 1.0)
        sin_of(c1, zi, math.pi / 2.0, 1.0)
        sin_of(sh, zi, 0.0, 0.5)
        S.op(lambda E: E.tensor_tensor(ta, sh, sh, ALU.mult))
        S.op(lambda E: E.tensor_tensor(tb, em1, c1, ALU.mult))
        S.op(lambda E: E.scalar_tensor_tensor(numr, ta, -2.0, tb, ALU.mult, ALU.add))
        S.op(lambda E: E.tensor_tensor(numi, mag, s1, ALU.mult))
        S.op(lambda E: E.tensor_tensor(ta, lamr, lamr, ALU.mult))
        S.op(lambda E: E.tensor_tensor(tb, lami, lami, ALU.mult))
        S.op(lambda E: E.tensor_tensor(ta, ta, tb, ALU.add))
        S.op(lambda E: E.reciprocal(ta, ta))
        S.op(lambda E: E.tensor_tensor(tb, numr, lamr, ALU.mult))
        S.op(lambda E: E.tensor_tensor(tc, numi, lami, ALU.mult))
        S.op(lambda E: E.tensor_tensor(tb, tb, tc, ALU.add))
        S.op(lambda E: E.tensor_tensor(cfr, tb, ta, ALU.mult))
        S.op(lambda E: E.tensor_tensor(tb, numi, lamr, ALU.mult))
        S.op(lambda E: E.tensor_tensor(tc, numr, lami, ALU.mult))
        S.op(lambda E: E.tensor_tensor(tb, tb, tc, ALU.subtract))
        S.op(lambda E: E.tensor_tensor(cfi, tb, ta, ALU.mult))
        S.op(lambda E: E.memset(apow[0][0], 1.0))
        S.op(lambda E: E.memset(apow[0][1], 0.0))
        S.op(lambda E: E.tensor_tensor(apow[1][0], mag, c1, ALU.mult))
        S.op(lambda E: E.tensor_tensor(apow[1][1], mag, s1, ALU.mult))
        for t in range(1, 8):
            cmul(S, apow[t + 1][0], apow[t + 1][1], apow[t][0], apow[t][1], apow[1][0], apow[1][1], tc, td)
        S.op(lambda E: E.tensor_tensor(rho8, mag, mag, ALU.mult))
        S.op(lambda E: E.tensor_tensor(rho8, rho8, rho8, ALU.mult))
        S.op(lambda E: E.tensor_tensor(rho8, rho8, rho8, ALU.mult))
        e2r = smalloc(); e2i = smalloc()
        cmul(S, e2r, e2i, c1, s1, c1, s1, tc, td)
        e4r = smalloc(); e4i = smalloc()
        cmul(S, e4r, e4i, e2r, e2i, e2r, e2i, tc, td)
        cmul(S, e8r, e8i, e4r, e4i, e4r, e4i, tc, td)
        if KSTOP == 4:
            tap("sm", sm.rearrange("p a b -> p (a b)"), [128, 48 * 32])
            P.emit()
            return nc, tap_out
        big1 = A.alloc([128, 32, 64], F32)
        big2 = A.alloc([128, 32, 64], F32)
        S.op(lambda E: E.memset(ctab[:, :, 0:1], 1.0))
        S.op(lambda E: E.memset(stab[:, :, 0:1], 0.0))
        Er, Ei = e8r, e8i
        epp = [(smalloc(), smalloc()), (smalloc(), smalloc())]
        epi = 0
        L = 1
        while L < 128:
            br_ = bcast_last(Er, L)
            bi_ = bcast_last(Ei, L)
            cmul(S, ctab[:, :, L:2 * L], stab[:, :, L:2 * L], ctab[:, :, 0:L], stab[:, :, 0:L], br_, bi_,
                 big1[:, :, 0:L], big2[:, :, 0:L])
            if 2 * L < 128:
                nr, ni = epp[epi % 2]
                epi += 1
                cmul(S, nr, ni, Er, Ei, Er, Ei, tc, td)
                Er, Ei = nr, ni
            L *= 2
        if KSTOP == 5:
            tap("ctab", ctab.rearrange("p a b -> p (a b)"), [128, 32 * 128])
            tap("sm", sm.rearrange("p a b -> p (a b)"), [128, 48 * 32])
            P.emit()
            return nc, tap_out
        BPr = A.alloc([128, 32, 32], F32)
        BPi = A.alloc([128, 32, 32], F32)
        BBr = A.alloc([128, 32, 32], F32)
        BBi = A.alloc([128, 32, 32], F32)
        ABr = A.alloc([128, 32, 32], F32)
        ABi = A.alloc([128, 32, 32], F32)
        T1 = A.alloc([128, 32, 32], F32)
        T2 = A.alloc([128, 32, 32], F32)
        tmpH4 = A.alloc([128, 4, 128], F32)
        t_bpr = P.dma("sp", BPr, bpr_d.rearrange("p (a b) -> p a b", b=32))
        t_bpi = P.dma("sp", BPi, bpi_d.rearrange("p (a b) -> p a b", b=32))
        S.last = [S.last, t_bpr, t_bpi]
        cmul(S, BBr, BBi, BPr, BPi, bcast_last(cfr, 32), bcast_last(cfi, 32), T1, T2)
        if KSTOP == 6:
            tap("BBr", BBr.rearrange("p a b -> p (a b)"), [128, 1024])
            P.emit()
            return nc, tap_out
        ABh = [A.alloc([128, 1024], BF16) for _ in range(2)]
        ABl = [A.alloc([128, 1024], BF16) for _ in range(2)]
        Chl = [A.alloc([128, 1024], BF16) for _ in range(4)]
        for pp, src_ in ((0, CPr), (1, CPni)):
            srcf = src_.rearrange("p a b -> p (a b)")
            S.op(lambda E, pp=pp, srcf=srcf: E.tensor_copy(Chl[2 * pp], srcf))
            S.op(lambda E, pp=pp, srcf=srcf: E.tensor_tensor(T1.rearrange("p a b -> p (a b)"), srcf, Chl[2 * pp], ALU.subtract))
            S.op(lambda E, pp=pp: E.tensor_copy(Chl[2 * pp + 1], T1.rearrange("p a b -> p (a b)")))
        gh_done = []
        for tau in range(8 if KSTOP != 7 else 1):
            cmul(S, ABr, ABi, BBr, BBi, bcast_last(apow[tau][0], 32), bcast_last(apow[tau][1], 32), T1, T2)
            for pp, src_ in ((0, ABr), (1, ABi)):
                srcf = src_.rearrange("p a b -> p (a b)")
                S.op(lambda E, pp=pp, srcf=srcf: E.tensor_copy(ABh[pp], srcf))
                S.op(lambda E, pp=pp, srcf=srcf: E.tensor_tensor(T1.rearrange("p a b -> p (a b)"), srcf, ABh[pp], ALU.subtract))
                S.op(lambda E, pp=pp: E.tensor_copy(ABl[pp], T1.rearrange("p a b -> p (a b)")))
            t_hl = S.last
            t_ab = S.last
            ABrf = ABr.rearrange("p a b -> p (a b)")
            ABif = ABi.rearrange("p a b -> p (a b)")
            CPrf = CPr.rearrange("p a b -> p (a b)")
            CPnif = CPni.rearrange("p a b -> p (a b)")
            rd = []
            for part, src in ((0, ABrf), (1, ABif)):
                for half in range(2):
                    b = part * 2 + half
                    t_tr = None
                    for q in range(4):
                        ct = half * 4 + q
                        sl = slice(q * 128, q * 128 + 128)
                        t_tr = P.op("pe", lambda E, b=b, sl=sl, src=src, ct=ct: E.transpose(banks[b][:, sl], src[:, ct * 128:(ct + 1) * 128], ident_f),
                                    [t_ab, bfree[b]], inc=(q == 3))
                    t_cp = P.op("act", lambda E, b=b, tau=tau, part=part, half=half: E.copy(
                        G[:, tau, part, half * 4:half * 4 + 4, :].rearrange("p a b -> p (a b)"), banks[b][:, :]), [t_tr])
                    rd.append((b, t_cp))
            for half in range(2):
                b = 4 + half
                t_mm = None
                pairs = [(ABh[0], Chl[0]), (ABh[0], Chl[1]), (ABl[0], Chl[0]), (ABh[1], Chl[2]), (ABh[1], Chl[3]), (ABl[1], Chl[2])]
                for q in range(4):
                    ct = half * 4 + q
                    sl = slice(q * 128, q * 128 + 128)
                    for pi_, (aa, cc) in enumerate(pairs):
                        t_mm = P.op("pe", lambda E, b=b, sl=sl, ct=ct, aa=aa, cc=cc, pi_=pi_, q=q: E.matmul(
                            banks[b][:, sl], aa[:, ct * 128:(ct + 1) * 128], cc[:, ct * 128:(ct + 1) * 128],
                            start=(pi_ == 0 and q == 0), stop=(pi_ == 5), skip_group_check=True),
                            [t_hl, bfree[b]] if (q == 0 and pi_ == 0) else [], inc=(q == 3 and pi_ == 5))
                bm4 = bmask.unsqueeze(1).to_broadcast([128, 4, 128])
                bk4 = banks[b][:, :].rearrange("p (a b) -> p a b", b=128)
                if tau == 0:
                    t_e1 = P.op("dve", lambda E, bk4=bk4, bm4=bm4: E.tensor_tensor(tmpH4, bk4, bm4, ALU.mult), [t_mm, S.last])
                    rd.append((b, t_e1))
                    tl = t_e1
                    for q in range(4):
                        ct = half * 4 + q
                        tl = P.op("dve", lambda E, ct=ct, q=q: E.scalar_tensor_tensor(H[:, 0, ct, :], ident_f, gcol("ssm_d", ct), tmpH4[:, q, :], ALU.mult, ALU.add), [tl])
                    S.last = [tl]
                else:
                    t_ev = P.op("dve", lambda E, bk4=bk4, bm4=bm4, tau=tau, half=half: E.tensor_tensor(H[:, tau, half * 4:half * 4 + 4, :], bk4, bm4, ALU.mult), [t_mm])
                    rd.append((b, t_ev))
            for b in range(6):
                bfree[b] = [t for (bb, t) in rd if bb == b]
            gh_done.append([t for (_, t) in rd])
            S.last = [S.last, gh_done[-1]]
            if tau == 7:
                tap("ABr", ABr.rearrange("p a b -> p (a b)"), [128, 1024])
                tap("ABi", ABi.rearrange("p a b -> p (a b)"), [128, 1024])
                tap("CPr", CPr.rearrange("p a b -> p (a b)"), [128, 1024])
                tap("CPni", CPni.rearrange("p a b -> p (a b)"), [128, 1024])
                tap("Chl0", Chl[0], [128, 1024])
                tap("ABh0", ABh[0], [128, 1024])
        A.pop()
        P.barrier()
        tap("G", G.rearrange("p a b c d -> p (a b c d)"), [128, 8 * 2 * 8 * 128])
        tap("H", H.rearrange("p a b c -> p (a b c)"), [128, 8 * 8 * 128])
        tap("ctab", ctab.rearrange("p a b -> p (a b)"), [128, 32 * 128])
        tap("stab", stab.rearrange("p a b -> p (a b)"), [128, 32 * 128])
        tap("sm", sm.rearrange("p a b -> p (a b)"), [128, 48 * 32])

        ucs = [A.alloc([128, 8, 1024], BF16) for _ in range(2)]
        um_bufs = [A.alloc([128, 4, 1024], BF16) for _ in range(2)]
        wblk = A.alloc([128, 6, 512], F32)
        Wr, Wi, Sr, Si, t1a, t2a = [wblk[:, q, :].rearrange("p (a b) -> p a b", b=128) for q in range(6)]
        Zpr = A.alloc([128, 4, 128], BF16)
        Zpi = A.alloc([128, 4, 128], BF16)
        w0 = A.alloc([128, 6, 4], F32)
        udi = A.alloc([128, 1024], BF16)
        um_free = [None, None]
        ssm_state = {"dve": [], "zp_free": None}

        def norm_proj_chunk(c, wcol0, n_mt, dst_fn, save_scratch, env, nts=(0, 1)):
            xring, sqb, xg, rs, wring = env["xring"], env["sqb"], env["xg"], env["rs"], env["wring"]
            for nt in nts:
                tok0 = 1024 * c + 512 * nt
                last_mm = None
                xg_toks = []
                for kt in range(KT):
                    xi = env["xi"]; env["xi"] = (xi + 1) % len(xring)
                    xb = xring[xi]
                    t_ld = P.dma("sp", xb, xT[kt * 128:(kt + 1) * 128, tok0:tok0 + 512], [env["xfree"][xi]])
                    si = kt % 2
                    t_sq = P.op("act", lambda E, xb=xb, si=si: E.activation(sqb[si], xb, AF.Square), [t_ld, env["sqfree"][si]])
                    last_mm = P.op("pe", lambda E, si=si, kt=kt: E.matmul(banks[6][:, :], ones_b, sqb[si], start=(kt == 0), stop=(kt == KT - 1)),
                                   [t_sq, bfree[6] if kt == 0 else None])
                    env["sqfree"][si] = last_mm
                    t_xg = P.op("dve", lambda E, xb=xb, kt=kt: E.tensor_scalar(xg[:, kt, :], xb, gcol("mix_pre", kt), None, ALU.mult),
                                [t_ld, env["xg_free"]])
                    xg_toks.append(t_xg)
                    env["xfree"][xi] = [t_sq, t_xg]
                ta_ = P.op("act", lambda E: E.activation(rs, banks[6][:, :], AF.Sqrt, bias=epsc, scale=1.0 / D), [last_mm, env["rs_free"]])
                bfree[6] = ta_
                t_rs = P.op("dve", lambda E: E.reciprocal(rs, rs), [ta_])
                readers = []
                if save_scratch:
                    tcol = tok0 - 1024
                    readers.append(P.dma("sp", xg_s.rearrange("(kt p) t -> p kt t", p=128)[:, :, tcol:tcol + 512], xg, [xg_toks]))
                    readers.append(P.dma("sp", rs_s[:, tcol:tcol + 512], rs, [t_rs]))
                evs = []
                for m in range(n_mt):
                    if m % 2 == 0:
                        wi_ = env["wi"]; env["wi"] = (wi_ + 1) % len(wring)
                        wsl = wring[wi_]
                        t_w = P.dma("pool", wsl, w_in_v[:, :, wcol0 + m * 128:wcol0 + m * 128 + 256], [env["wfree"][wi_]])
                        env["wfree"][wi_] = []
                        cur = (wi_, wsl, t_w)
                    wi_, wsl, t_w = cur
                    b = (4, 5, 7)[env["bi"] % 3]; env["bi"] += 1
                    t_mm = None
                    for kt in range(KT):
                        t_mm = P.op("pe", lambda E, b=b, wsl=wsl, kt=kt, mo=(m % 2) * 128: E.matmul(banks[b][:, :], wsl[:, kt, mo:mo + 128], xg[:, kt, :],
                                                                                                  start=(kt == 0), stop=(kt == KT - 1)),
                                    [t_w, xg_toks, bfree[b]] if kt == 0 else [], inc=(kt == KT - 1))
                    env["wfree"][wi_].append(t_mm)
                    dst = dst_fn(m, nt)
                    t_ev = P.op("dve", lambda E, b=b, dst=dst: E.tensor_tensor(dst, banks[b][:, :], rs, ALU.mult), [t_mm, t_rs, env["dst_free"]])
                    bfree[b] = t_ev
                    evs.append(t_ev)
                    readers.append(t_mm)
                env["xg_free"] = readers
                env["rs_free"] = [evs, readers]
                env["done"].setdefault(c, []).append(evs)
            return

        def ssm_chunk(c, own, u_ready, uc, Fm=None, side=None):
            readers = []

            def do_ct(ct):
                ui = ct % 2
                um = um_bufs[ui]
                vb0, vb1 = (0, 1) if (own or ct % 2 == 0) else (2, 3)
                tm = []
                for j in range(4):
                    tm.append(P.op("act", lambda E, j=j, ct=ct: E.activation(um[:, j, :].rearrange("p (i k) -> p i k", k=128), uc[:, ct, :].rearrange("p (k i) -> p i k", i=8),
                                                                                AF.Copy, scale=pmask[:, j:j + 1]),
                                   [u_ready, um_free[ui]]))
                if own:
                    tm.append(P.op("act", lambda E, ct=ct: E.copy(udi.rearrange("p (i k) -> p i k", k=128), uc[:, ct, :].rearrange("p (k i) -> p i k", i=8)),
                                   [u_ready, ssm_state["zp_free"]]))
                t_v = None
                for part in range(2):
                    for i in range(8):
                        last = (part == 1 and i == 7)
                        vb = vb0 if part == 0 else vb1
                        t_v = P.op("pe", lambda E, part=part, i=i, ct=ct, vb=vb: E.matmul(banks[vb][:, :].rearrange("p (a b) -> p a b", b=128), G[:, 7 - i, part, ct, :], um[:, :, i * 128:(i + 1) * 128],
                                                                              start=(i == 0), stop=(i == 7)),
                                   [tm, bfree[vb0], bfree[vb1]] if (part == 0 and i == 0) else [], inc=last)
                um_free[ui] = t_v
                cs = ctab[:, 4 * ct:4 * ct + 4, :]
                sn = stab[:, 4 * ct:4 * ct + 4, :]
                Vr = banks[vb0][:, :].rearrange("p (a b) -> p a b", b=128)
                Vi = banks[vb1][:, :].rearrange("p (a b) -> p a b", b=128)
                prev = ssm_state["dve"]
                B0, B1, B2, B3, Sr_, Si_ = Wr, Wi, Sr, Si, t1a, t2a
                dv = lambda fn, deps: P.op("dve", fn, deps)
                zr_ = Zin_r[:, 4 * ct:4 * ct + 4]
                zi_ = Zin_i[:, 4 * ct:4 * ct + 4]
                er_ = e8r[:, 4 * ct:4 * ct + 4]
                ei_ = e8i[:, 4 * ct:4 * ct + 4]
                a1 = dv(lambda E: E.tensor_tensor(B0, Vr, cs, ALU.mult), [t_v, prev])
                a2 = dv(lambda E: E.tensor_tensor(B1, Vi, sn, ALU.mult), [t_v, prev])
                a3 = dv(lambda E: E.tensor_tensor(B2, Vi, cs, ALU.mult), [t_v, prev])
                a4 = dv(lambda E: E.tensor_tensor(B3, Vr, sn, ALU.mult), [t_v, prev])
                bfree[vb0] = [a1, a2, a3, a4]
                bfree[vb1] = [a1, a2, a3, a4]
                w1 = dv(lambda E: E.tensor_tensor(w0[:, 2, :], er_, zr_, ALU.mult), [prev])
                w2 = dv(lambda E: E.tensor_tensor(w0[:, 3, :], ei_, zi_, ALU.mult), [prev])
                w3 = dv(lambda E: E.tensor_tensor(w0[:, 4, :], er_, zi_, ALU.mult), [prev])
                w4 = dv(lambda E: E.tensor_tensor(w0[:, 5, :], ei_, zr_, ALU.mult), [prev])
                a5 = dv(lambda E: E.tensor_tensor(B0, B0, B1, ALU.add), [a1, a2])
                a6 = dv(lambda E: E.tensor_tensor(B2, B2, B3, ALU.subtract), [a3, a4])
                w5 = dv(lambda E: E.tensor_tensor(w0[:, 0, :], w0[:, 2, :], w0[:, 3, :], ALU.subtract), [w1, w2])
                w6 = dv(lambda E: E.tensor_tensor(w0[:, 1, :], w0[:, 4, :], w0[:, 5, :], ALU.add), [w3, w4])
                sr_t, si_t = [], []
                for j in range(4):
                    pr = 4 * ct + j
                    d0 = rho8[:, pr:pr + 1].to_broadcast([128, 128])
                    sr_t.append(dv(lambda E, j=j, d0=d0: E.tensor_tensor_scan(Sr_[:, j, :], d0, B0[:, j, :], w0[:, 0, j:j + 1], ALU.mult, ALU.add), [a5, w5, prev]))
                    si_t.append(dv(lambda E, j=j, d0=d0: E.tensor_tensor_scan(Si_[:, j, :], d0, B2[:, j, :], w0[:, 1, j:j + 1], ALU.mult, ALU.add), [a6, w6, prev]))
                c0 = []
                if own:
                    c0.append(dv(lambda E: E.tensor_copy(Zpr[:, :, 0:1], zr_.unsqueeze(2)), [ssm_state["zp_free"], prev]))
                    c0.append(dv(lambda E: E.tensor_copy(Zpi[:, :, 0:1], zi_.unsqueeze(2)), [ssm_state["zp_free"], prev]))
                c127 = cs[:, :, 127:128]
                s127 = sn[:, :, 127:128]
                z1 = dv(lambda E: E.tensor_tensor(w0[:, 2, :].unsqueeze(2), c127, Sr_[:, :, 127:128], ALU.mult), [sr_t, w5])
                z2 = dv(lambda E: E.tensor_tensor(w0[:, 3, :].unsqueeze(2), s127, Si_[:, :, 127:128], ALU.mult), [si_t, w5])
                z3 = dv(lambda E: E.tensor_tensor(w0[:, 4, :].unsqueeze(2), s127, Sr_[:, :, 127:128], ALU.mult), [sr_t, w6])
                z4 = dv(lambda E: E.tensor_tensor(w0[:, 5, :].unsqueeze(2), c127, Si_[:, :, 127:128], ALU.mult), [si_t, w6])
                z5 = dv(lambda E: E.tensor_tensor(zr_, w0[:, 2, :], w0[:, 3, :], ALU.subtract), [z1, z2, w1, w4, c0])
                z6 = dv(lambda E: E.tensor_tensor(zi_, w0[:, 4, :], w0[:, 5, :], ALU.add), [z3, z4, w2, w3, c0])
                all_t = [a1, a2, a3, a4, w1, w2, w3, w4, a5, a6, w5, w6, sr_t, si_t, c0, z1, z2, z3, z4, z5, z6]
                if own:
                    cs7 = cs[:, :, 0:127]; sn7 = sn[:, :, 0:127]
                    d1 = dv(lambda E: E.tensor_tensor(B1[:, :, 0:127], cs7, Sr_[:, :, 0:127], ALU.mult), [sr_t, a5])
                    d2 = dv(lambda E: E.tensor_tensor(B3[:, :, 0:127], sn7, Si_[:, :, 0:127], ALU.mult), [si_t, a6])
                    d3 = dv(lambda E: E.tensor_tensor(Zpr[:, :, 1:128], B1[:, :, 0:127], B3[:, :, 0:127], ALU.subtract), [d1, d2, ssm_state["zp_free"]])
                    d4 = dv(lambda E: E.tensor_tensor(B1[:, :, 0:127], sn7, Sr_[:, :, 0:127], ALU.mult), [d3])
                    d5 = dv(lambda E: E.tensor_tensor(B3[:, :, 0:127], cs7, Si_[:, :, 0:127], ALU.mult), [d3])
                    d6 = dv(lambda E: E.tensor_tensor(Zpi[:, :, 1:128], B1[:, :, 0:127], B3[:, :, 0:127], ALU.add), [d4, d5])
                    t_zp = [d3, d6, c0]
                    all_t += [d1, d2, d3, d4, d5, d6]

                class _L:
                    last = all_t
                Sq = _L()
                if own:
                    t_y = [None, None]
                    for hb in range(2):
                        b = 2 + hb
                        first = True
                        for j in range(4 * hb, 4 * hb + 4):
                            jc = (j % 4) * 128
                            for i in range(j + 1):
                                P.op("pe", lambda E, b=b, j=j, i=i, ct=ct, jc=jc, first=first: E.matmul(banks[b][:, jc:jc + 128], H[:, j - i, ct, :], udi[:, i * 128:(i + 1) * 128],
                                                                                                 start=first, stop=False, skip_group_check=True),
                                     [u_ready, bfree[b], t_zp, tm] if first else [], inc=False)
                                first = False
                            for pl in range(4):
                                for part in range(2):
                                    zp = Zpr if part == 0 else Zpi
                                    lastm = (j == 4 * hb + 3 and pl == 3 and part == 1)
                                    t_ = P.op("pe", lambda E, b=b, j=j, pl=pl, part=part, zp=zp, ct=ct, jc=jc: E.matmul(
                                        banks[b][32 * pl:32 * pl + 32, jc:jc + 128], Fm[:, j, part, (4 * ct + pl) * 32:(4 * ct + pl) * 32 + 32], zp[:, pl, :],
                                        start=False, stop=(pl == 3 and part == 1), tile_position=(0, 32 * pl), skip_group_check=True), [], inc=lastm)
                                    if lastm:
                                        t_y[hb] = t_
                    ssm_state["zp_free"] = t_y[1]
                    readers.append(t_y[1])
                    yi = ssm_state.get("yi", 0)
                    ssm_state["yi"] = yi + 1
                    yrow = yring[yi % 2]
                    t_fin = []
                    for hb in range(2):
                        b = 2 + hb
                        yb = banks[b][:, :]
                        dst = yrow.rearrange("p (k j) -> p j k", j=8)[:, 4 * hb:4 * hb + 4, :]
                        G1 = gtmp[0]; G2 = gtmp[1]
                        ta1 = P.op("act", lambda E, yb=yb, G1=G1: E.activation(G1, yb, AF.Square), [t_y[hb], ssm_state.get("g_free")])
                        tb1 = P.op("dve", lambda E, G1=G1: E.tensor_scalar(G1, G1, 0.044715, 1.0, ALU.mult, ALU.add), [ta1, Sq.last])
                        tb2 = P.op("dve", lambda E, G1=G1, yb=yb: E.tensor_tensor(G1, G1, yb, ALU.mult), [tb1])
                        ta2 = P.op("act", lambda E, G1=G1, G2=G2: E.activation(G2, G1, AF.Sigmoid, scale=1.5957691216057308), [tb2])
                        tb3 = P.op("dve", lambda E, G2=G2, yb=yb, dst=dst: E.tensor_tensor(dst, yb.rearrange("p (j k) -> p j k", k=128), G2.rearrange("p (j k) -> p j k", k=128), ALU.mult),
                                   [ta2, yfree[yi % 2]])
                        bfree[b] = tb3
                        ssm_state["g_free"] = tb3
                        Sq.last = [tb3]
                        t_fin.append(tb3)
                    yfree[yi % 2] = P.dma("sp", yg_s[ct * 128:(ct + 1) * 128, :], yrow, [t_fin])
                else:
                    readers.append(t_v)
                ssm_state["dve"] = Sq.last

            for ct_ in range(8):
                do_ct(ct_)
                if side and ct_ in side:
                    side[ct_]()
            return readers

        A.push()
        env = {"xring": [A.alloc([128, 512], F32) for _ in range(4)], "sqb": [A.alloc([128, 512], BF16) for _ in range(2)],
               "xg": A.alloc([128, KT, 512], BF16), "rs": A.alloc([128, 512], F32),
               "wring": [A.alloc([128, KT, 256], BF16) for _ in range(3)],
               "xi": 0, "wi": 0, "bi": 0, "xfree": [None] * 4, "sqfree": [None] * 2, "wfree": [[], [], []],
               "xg_free": None, "rs_free": None, "dst_free": None}
        env["done"] = {}
        uc_readers = [None, None]

        def proj_job(c, nts):
            def f():
                ucb = ucs[c % 2]
                env["dst_free"] = uc_readers[c % 2]
                norm_proj_chunk(c, 3072, 8, lambda m, nt, ucb=ucb: ucb[:, m, 512 * nt:512 * nt + 512], c >= 1, env, nts=nts)
            return f
        proj_job(0, (0, 1))()
        for c in range(NCH - 1):
            if KSIDE:
                side = {1: proj_job(c + 1, (0,)), 4: proj_job(c + 1, (1,))}
                uc_readers[c % 2] = ssm_chunk(c, False, env["done"][c], ucs[c % 2], side=side)
            else:
                uc_readers[c % 2] = ssm_chunk(c, False, env["done"][c], ucs[c % 2])
                proj_job(c + 1, (0, 1))()
        P.barrier()
        A.pop()
        if "u_own" in taps:
            tap("u_own", ucs[(NCH - 1) % 2].rearrange("p a b -> p (a b)"), [128, 8 * 1024])
        tap("zin", sm.rearrange("p a b -> p (a b)"), [128, 48 * 32])
        Fm = A.alloc([128, 8, 2, 1024], BF16)
        yring = [A.alloc([128, 1024], F32) for _ in range(2)]
        yfree = [None, None]
        gtmp = [A.alloc([128, 512], F32) for _ in range(2)]
        A.push()
        Ft1b = wblk[:, 0:2, :].rearrange("p a (b c) -> p (a b) c", c=32)
        Ft2b = wblk[:, 2:4, :].rearrange("p a (b c) -> p (a b) c", c=32)
        CPr2 = A.alloc([128, 32, 32], F32)
        CPni2 = A.alloc([128, 32, 32], F32)
        CPi02 = wblk[:, 4:6, :].rearrange("p a (b c) -> p (a b) c", c=32)
        t_cpr2 = P.dma("sp", CPr2, cpr_d.rearrange("p (a b) -> p a b", b=32))
        t_cpi2 = P.dma("sp", CPi02, cpi_d.rearrange("p (a b) -> p a b", b=32))
        SF = Seq(P, "dve", [t_cpr2, t_cpi2])
        SF.op(lambda E: E.memset(CPni2, 0.0))
        SF.op(lambda E: E.tensor_tensor(CPni2, CPni2, CPi02, ALU.subtract))
        for j in range(8):
            pr_ = bcast_last(apow[j + 1][0], 32)
            pi_ = bcast_last(apow[j + 1][1], 32)
            SF.op(lambda E, pr_=pr_: E.tensor_tensor(Ft1b, CPr2, pr_, ALU.mult))
            SF.op(lambda E, pi_=pi_: E.tensor_tensor(Ft2b, CPni2, pi_, ALU.mult))
            SF.op(lambda E, j=j: E.tensor_tensor(Fm[:, j, 0, :].rearrange("p (a b) -> p a b", b=32), Ft1b, Ft2b, ALU.add))
            SF.op(lambda E, pr_=pr_: E.tensor_tensor(Ft1b, CPni2, pr_, ALU.mult))
            SF.op(lambda E, pi_=pi_: E.tensor_tensor(Ft2b, CPr2, pi_, ALU.mult))
            SF.op(lambda E, j=j: E.tensor_tensor(Fm[:, j, 1, :].rearrange("p (a b) -> p a b", b=32), Ft1b, Ft2b, ALU.subtract))
        P.barrier()
        A.pop()
        ssm_chunk(NCH - 1, True, P.all_tokens(), ucs[(NCH - 1) % 2], Fm=Fm)
        P.barrier()
        A.pop()

        def make_wring(n):
            return {"slots": [A.alloc([128, 16, 256], BF16) for _ in range(n)], "free": [[] for _ in range(n)], "i": 0}

        def linear(w_view, KTn, m_tiles, rhs_fn, evac_fn, wr, bank_sets, rhs_deps):
            nkg = KTn // 16
            for pi_ in range(0, len(m_tiles), 2):
                mp = m_tiles[pi_:pi_ + 2]
                bset = bank_sets[(pi_ // 2) % len(bank_sets)]
                lastmm = {}
                for g in range(nkg):
                    wi_ = wr["i"]; wr["i"] = (wi_ + 1) % len(wr["slots"])
                    wsl = wr["slots"][wi_]
                    t_w = P.dma("pool", wsl[:, :, 0:128 * len(mp)], w_view[:, g * 16:(g + 1) * 16, mp[0] * 128:mp[0] * 128 + 128 * len(mp)], [wr["free"][wi_]])
                    wr["free"][wi_] = []
                    for mi in range(len(mp)):
                        for nt in range(2):
                            b = bset[mi * 2 + nt]
                            t_mm = None
                            for kl in range(16):
                                kt = g * 16 + kl
                                firstb = (g == 0 and kl == 0)
                                t_mm = P.op("pe", lambda E, b=b, wsl=wsl, kl=kl, mi=mi, kt=kt, nt=nt, firstb=firstb, g=g: E.matmul(
                                    banks[b][:, :], wsl[:, kl, mi * 128:mi * 128 + 128], rhs_fn(kt, nt), start=firstb, stop=(g == nkg - 1 and kl == 15)),
                                    ([t_w, rhs_deps, bfree[b]] if firstb else ([t_w] if kl == 0 else [])), inc=(kl == 15))
                            lastmm[(mi, nt)] = t_mm
                            wr["free"][wi_].append(t_mm)
                for mi, m in enumerate(mp):
                    for nt in range(2):
                        b = bset[mi * 2 + nt]
                        bfree[b] = evac_fn(m, nt, b, lastmm[(mi, nt)])

        def stats_of(src_fn, ntiles, rs_out, nfeat, sqb_, deps, stat_banks=(6, 7)):
            return rms_stats(src_fn, ntiles, 1024, rs_out, nfeat, sqb_, list(stat_banks), deps)

        A.push()
        mixed = A.alloc([128, 16, 1024], BF16)
        A.push()
        yg = A.alloc([128, 8, 1024], F32)
        ygb = A.alloc([128, 8, 1024], BF16)
        wg = A.alloc([128, 8, 1024], BF16)
        gt2 = [A.alloc([128, 512], F32) for _ in range(2)]
        sqb2 = [A.alloc([128, 512], BF16) for _ in range(2)]
        rs2 = A.alloc([128, 1024], F32)
        t_wg = P.dma("pool", wg, w_glu.rearrange("(kt p) m -> p kt m", p=128))
        t_yb = []
        for ct in range(8):
            t_l = P.dma("sp", yg[:, ct, :], yg_s[ct * 128:(ct + 1) * 128, :])
            t_yb.append(P.op("act", lambda E, ct=ct: E.copy(ygb[:, ct, :], yg[:, ct, :]), [t_l]))
        tap("yg", yg.rearrange("p a b -> p (a b)"), [128, 8 * 1024])
        gfree = [None, None]
        glu_done = []
        for m in range(8):
            for nt in range(2):
                b = (m * 2 + nt) % 4
                t_mm = None
                for kt in range(8):
                    t_mm = P.op("pe", lambda E, b=b, kt=kt, m=m, nt=nt: E.matmul(banks[b][:, :], wg[:, kt, m * 128:(m + 1) * 128], ygb[:, kt, nt * 512:(nt + 1) * 512],
                                                                              start=(kt == 0), stop=(kt == 7)),
                                [t_wg, t_yb, bfree[b]] if kt == 0 else [], inc=(kt == 7))
                gi = (m * 2 + nt) % 2
                t_s = P.op("act", lambda E, b=b, gi=gi, m=m: E.activation(gt2[gi], banks[b][:, :], AF.Sigmoid, bias=gcol("b_glu", m)), [t_mm, gfree[gi]])
                bfree[b] = t_s
                t_g = P.op("dve", lambda E, gi=gi, m=m, nt=nt: E.tensor_tensor(yg[:, m, nt * 512:(nt + 1) * 512], yg[:, m, nt * 512:(nt + 1) * 512], gt2[gi], ALU.mult), [t_s])
                gfree[gi] = t_g
                glu_done.append(t_g)
        tap("ssm", yg.rearrange("p a b -> p (a b)"), [128, 8 * 1024])
        t_rs2 = stats_of(lambda i, nt: yg[:, i, nt * 512:(nt + 1) * 512], 8, rs2, 1024.0, sqb2, glu_done)
        for ct in range(8):
            for nt in range(2):
                P.op("dve", lambda E, ct=ct, nt=nt: E.scalar_tensor_tensor(mixed[:, 8 + ct, nt * 512:(nt + 1) * 512], yg[:, ct, nt * 512:(nt + 1) * 512],
                                                                          gcol("ssm_n", ct), rs2[:, nt * 512:(nt + 1) * 512], ALU.mult, ALU.mult), [t_rs2])
        P.barrier()
        A.pop()

        A.push()
        attnT = A.alloc([128, 8, 1024], F32)
        head_state = {"gi": 0, "vtok_free": None, "rec_free": None}
        SCALE = 1.0 / math.sqrt(128.0)
        xg_sv = xg_s.rearrange("(kt p) t -> p kt t", p=128)
        for hg in range(2):
            A.push()
            KTt = A.alloc([128, 4, 3072], BF16)
            VTt = A.alloc([128, 4, 3072], BF16)
            QTt = A.alloc([128, 4, 1024], BF16)
            A.push()
            xg3 = A.alloc([128, 16, 1024], BF16)
            rs3 = A.alloc([128, 1024], F32)
            wr3 = make_wring(3)
            for sc in range(3):
                t_x3 = P.dma("sp", xg3, xg_sv[:, :, 1024 * sc:1024 * sc + 1024], [P.all_tokens()] if (sc > 0 or hg > 0) else [])
                t_r3 = P.dma("sp", rs3, rs_s[:, 1024 * sc:1024 * sc + 1024], [P.all_tokens()] if (sc > 0 or hg > 0) else [])
                jobs = [(1024 + 512 * hg, KTt), (1024 + 512 * hg + 256, KTt), (2048 + 512 * hg, VTt), (2048 + 512 * hg + 256, VTt)]
                if sc == 2:
                    jobs += [(512 * hg, QTt), (512 * hg + 256, QTt)]
                for (wc0, dstT) in jobs:
                    hl0 = ((wc0 % 1024) - 512 * hg) // 128

                    def ev3(m, nt, b, t_mm, dstT=dstT, hl0=hl0, sc=sc):
                        hl = hl0 + m
                        if dstT is QTt:
                            dst = dstT[:, hl, nt * 512:(nt + 1) * 512]
                        else:
                            dst = dstT[:, hl, 1024 * sc + nt * 512:1024 * sc + (nt + 1) * 512]
                        return P.op("dve", lambda E, dst=dst, b=b, nt=nt: E.tensor_tensor(dst, banks[b][:, :], rs3[:, nt * 512:(nt + 1) * 512], ALU.mult), [t_mm, t_r3])
                    linear(w_in_v[:, :, wc0:wc0 + 256], 16, [0, 1], lambda kt, nt: xg3[:, kt, nt * 512:(nt + 1) * 512], ev3, wr3,
                           [[0, 1, 2, 3], [4, 5, 6, 7]], [t_x3])
            P.barrier()
            A.pop()
            if hg == 0:
                tap("KT0", KTt[:, 0, :], [128, 3072])
                tap("VT0", VTt[:, 0, :], [128, 3072])
                tap("QT0", QTt[:, 0, :], [128, 1024])
            A.push()
            Vtoks = [A.alloc([128, NVT + 3, 128], BF16) for _ in range(2)]
            NES = 6
            es = [A.alloc([128, 4, 128], BF16) for _ in range(NES)]
            pm = [A.alloc([128, 4, 128], BF16) for _ in range(NES)]
            rec = A.alloc([128, 512], F32)
            TB = banks[0][:, :].bitcast(BF16)
            es_free = [None] * NES
            pm_free = [None] * NES
            vtok_free = [None, None]
            for hl in range(4):
                h = 4 * hg + hl

                def do_head(hl, h):
                    Vtok = Vtoks[hl % 2]
                    vt_toks = []
                    for g0 in range(0, NVT, 8):
                        n = min(8, NVT - g0)
                        t_tr = None
                        for q in range(n):
                            d, r, m, nk = VT_LIST[g0 + q]
                            te0 = 2048 + r + d * m
                            t_tr = P.op("pe", lambda E, q=q, te0=te0, d=d, nk=nk: E.transpose(TB[0:nk, q * 128:(q + 1) * 128], VTt[:, hl, te0:te0 + d * (nk - 1) + 1:d], ident_b),
                                        [bfree[0], vtok_free[hl % 2]] if q == 0 else [], inc=(q == n - 1))
                        t_cp = P.op("act", lambda E, g0=g0, n=n: E.copy(Vtok[:, g0:g0 + n, :].rearrange("p a b -> p (a b)"), TB[:, 0:128 * n]), [t_tr])
                        bfree[0] = t_cp
                        vt_toks.append(t_cp)
                    tiles = []
                    for d in (1, 4, 16):
                        QB = min(128, NO // d)
                        for r in range(d):
                            for blk in range((NO // d) // QB):
                                m0 = blk * QB
                                for (mk, nk, mask) in ((m0 - 128, 128, mprev_b), (m0, QB, mcur_b)):
                                    vt = VT_IDX[(d, r, mk)] if (d, r, mk) in VT_IDX else None
                                    if vt is None:
                                        raise AssertionError((d, r, mk))
                                    outs = []
                                    if d == 16:
                                        outs = [(0, r, 16, 0, 32), (1, r, 16, 32, 64)]
                                    elif d == 4:
                                        outs = [(blk, r, 4, 0, 128)]
                                    else:
                                        outs = [(blk // 4, (blk % 4) * 128, 1, 0, 128)]
                                    tiles.append(dict(d=d, r=r, m0=m0, mk=mk, nk=nk, QB=QB, mask=mask, vt=vt, outs=outs))
                    started = set()
                    tix = {(T["d"], T["r"], T["m0"], T["mk"]): T for T in tiles}
                    groups = []
                    for half in range(2):
                        for ab in range(2):
                            grp = [tix[(1, 0, 128 * blk, 128 * blk - 128 if ab == 0 else 128 * blk)] for blk in range(4 * half, 4 * half + 4)]
                            groups.append((grp, ("d1", half)))
                    for blk in range(2):
                        for ab in range(2):
                            grp = [tix[(4, r, 128 * blk, 128 * blk - 128 if ab == 0 else 128 * blk)] for r in range(4)]
                            groups.append((grp, ("d4", blk)))
                    for r0 in range(0, 16, 4):
                        for ab in range(2):
                            grp = [tix[(16, r, 0, -128 if ab == 0 else 0)] for r in range(r0, r0 + 4)]
                            groups.append((grp, ("d16", r0)))
                    assert sum(len(g[0]) for g in groups) == len(tiles)
                    for (grp, gkind) in groups:
                        gi = (head_state["gi"]) % NES
                        sb_ = (1, 2, 7)[head_state["gi"] % 3]
                        head_state["gi"] += 1
                        t_s = None
                        for q, T in enumerate(grp):
                            d, r = T["d"], T["r"]
                            k0 = 2048 + r + d * T["mk"]
                            q0 = r + d * T["m0"]
                            t_s = P.op("pe", lambda E, q=q, k0=k0, q0=q0, d=d, nk=T["nk"], QB=T["QB"], sb_=sb_: E.matmul(
                                banks[sb_][0:nk, q * 128:q * 128 + QB], KTt[:, hl, k0:k0 + d * (nk - 1) + 1:d], QTt[:, hl, q0:q0 + d * (QB - 1) + 1:d],
                                start=True, stop=True, skip_group_check=True), [bfree[sb_]] if q == 0 else [], inc=(q == len(grp) - 1))
                        ng = len(grp)
                        t_e = P.op("act", lambda E, gi=gi, sb_=sb_, ng=ng: E.activation(es[gi][:, 0:ng, :].rearrange("p a b -> p (a b)"), banks[sb_][:, 0:128 * ng], AF.Exp, scale=SCALE),
                                   [t_s, es_free[gi]])
                        bfree[sb_] = t_e
                        t_ms = []
                        for q, T in enumerate(grp):
                            nk, QB = T["nk"], T["QB"]
                            t_ms.append(P.op("dve", lambda E, gi=gi, q=q, nk=nk, QB=QB, mask=T["mask"], vt=T["vt"]: E.scalar_tensor_tensor(
                                pm[gi][0:nk, q, 0:QB], es[gi][0:nk, q, 0:QB], vcols[0:nk, vt:vt + 1], mask[0:nk, 0:QB], ALU.mult, ALU.mult),
                                [t_e, pm_free[gi]] if q == 0 else [t_e]))
                        es_free[gi] = t_ms
                        t_pv = None
                        for q, T in enumerate(grp):
                            nk = T["nk"]
                            for (half, off, step, c0, c1) in T["outs"]:
                                ncol = c1 - c0
                                ob = 3 + half
                                st = ob not in started
                                started.add(ob)
                                lhs = Vtok[0:nk, T["vt"], :]
                                t_pv = P.op("pe", lambda E, ob=ob, off=off, step=step, ncol=ncol, lhs=lhs, gi=gi, q=q, nk=nk, c0=c0, c1=c1, st=st: E.matmul(
                                    banks[ob][:, off:off + step * (ncol - 1) + 1:step], lhs, pm[gi][0:nk, q, c0:c1], start=st, stop=False, skip_group_check=True),
                                    [t_ms, vt_toks, bfree[ob]] if st else [t_ms[q]], inc=True)
                        nkg = grp[0]["nk"]
                        if gkind[0] == "d1":
                            dens = [(5 + gkind[1], banks[5 + gkind[1]][:, :], pm[gi][0:nkg].rearrange("p a b -> p (a b)"))]
                        elif gkind[0] == "d4":
                            dens = [(5 + gkind[1], banks[5 + gkind[1]][:, :].rearrange("p (i r) -> p r i", r=4)[:, :, 64 * hf:64 * hf + 64], pm[gi][0:nkg, :, 64 * hf:64 * hf + 64])
                                    for hf in range(2)]
                        else:
                            r0 = gkind[1]
                            dens = [(5 + hf, banks[5 + hf][:, :].rearrange("p (i r) -> p r i", r=16)[:, r0:r0 + 4, :], pm[gi][0:nkg, :, 32 * hf:32 * hf + 32]) for hf in range(2)]
                        for (ob, oap, rap) in dens:
                            st = ob not in started
                            started.add(ob)
                            t_pv = P.op("pe", lambda E, oap=oap, rap=rap, nkg=nkg, st=st: E.matmul(oap, ones_b[0:nkg, :], rap, start=st, stop=False, skip_group_check=True),
                                        [t_ms, bfree[ob]] if st else [t_ms], inc=True)
                        pm_free[gi] = t_pv
                    fin = []
                    for half in range(2):
                        t_r = P.op("dve", lambda E, half=half: E.reciprocal(rec, banks[5 + half][:, :]), [t_pv, head_state["rec_free"]])
                        t_f = P.op("dve", lambda E, half=half: E.tensor_tensor(attnT[:, h, half * 512:(half + 1) * 512], banks[3 + half][:, :], rec, ALU.mult), [t_r])
                        head_state["rec_free"] = t_f
                        bfree[5 + half] = t_r
                        bfree[3 + half] = t_f
                        fin.append(t_f)
                    vtok_free[hl % 2] = t_pv
                do_head(hl, h)
            P.barrier()
            A.pop()
            A.pop()
        tap("attn", attnT.rearrange("p a b -> p (a b)"), [128, 8 * 1024])
        A.push()
        sqb4 = [A.alloc([128, 512], BF16) for _ in range(2)]
        rs4 = A.alloc([128, 1024], F32)
        t_rs4 = stats_of(lambda i, nt: attnT[:, i, nt * 512:(nt + 1) * 512], 8, rs4, 1024.0, sqb4, [])
        for hh in range(8):
            for nt in range(2):
                P.op("dve", lambda E, hh=hh, nt=nt: E.scalar_tensor_tensor(mixed[:, hh, nt * 512:(nt + 1) * 512], attnT[:, hh, nt * 512:(nt + 1) * 512],
                                                                          gcol("attn_n", hh), rs4[:, nt * 512:(nt + 1) * 512], ALU.mult, ALU.mult), [t_rs4])
        P.barrier()
        A.pop()
        A.pop()
        tap("mixed", mixed.rearrange("p a b -> p (a b)"), [128, 16 * 1024])

        def residual_pass(src_load, base_load, gname, rs_, dst_store, tmp_ring, deps, inplace_src=None):
            fr = [None] * len(tmp_ring)
            k = 0
            outs = []
            for m in range(16):
                for nt in range(2):
                    a_, b_ = tmp_ring[k % len(tmp_ring)]
                    if inplace_src is not None:
                        a_ = inplace_src(m, nt)
                        t_a = None
                    else:
                        t_a = src_load(m, nt, a_, [fr[k % len(tmp_ring)], deps])
                    t_b = base_load(m, nt, b_, [fr[k % len(tmp_ring)], deps])
                    t1_ = P.op("dve", lambda E, a_=a_, m=m, nt=nt: E.scalar_tensor_tensor(a_, a_, gcol(gname, m), rs_[:, nt * 512:(nt + 1) * 512], ALU.mult, ALU.mult), [t_a, deps])
                    t2_ = P.op("dve", lambda E, a_=a_, b_=b_: E.tensor_tensor(b_, b_, a_, ALU.add), [t1_, t_b])
                    t3_ = dst_store(m, nt, b_, [t2_])
                    fr[k % len(tmp_ring)] = t3_
                    outs.append(t3_)
                    k += 1
            return outs

        def load_full(dst, src_dram, deps):
            toks = []
            for m in range(16):
                toks.append(P.dma("sp", dst[:, m, :], src_dram[m * 128:(m + 1) * 128, :], deps))
            return toks

        def prenorm(hT_, gname, hn_, sqb_, rs_, deps):
            t_rs = stats_of(lambda i, nt: hT_[:, i, nt * 512:(nt + 1) * 512], 16, rs_, float(D), sqb_, deps)
            toks = []
            for m in range(16):
                for nt in range(2):
                    toks.append(P.op("dve", lambda E, m=m, nt=nt: E.scalar_tensor_tensor(hn_[:, m, nt * 512:(nt + 1) * 512], hT_[:, m, nt * 512:(nt + 1) * 512],
                                                                                         gcol(gname, m), rs_[:, nt * 512:(nt + 1) * 512], ALU.mult, ALU.mult), [t_rs]))
            return toks

        A.push()
        mixT = A.alloc([128, 16, 1024], F32)
        wr4 = make_wring(3)
        sqb5 = [A.alloc([128, 512], BF16) for _ in range(2)]
        rs5 = A.alloc([128, 1024], F32)

        def ev4(m, nt, b, t_mm):
            return P.op("act", lambda E, m=m, nt=nt, b=b: E.copy(mixT[:, m, nt * 512:(nt + 1) * 512], banks[b][:, :]), [t_mm])
        linear(w_out.rearrange("(kt p) m -> p kt m", p=128), 16, list(range(16)), lambda kt, nt: mixed[:, kt, nt * 512:(nt + 1) * 512], ev4, wr4,
               [[0, 1, 2, 3], [4, 5, 6, 7]], [])
        P.barrier()
        t_rs5 = stats_of(lambda i, nt: mixT[:, i, nt * 512:(nt + 1) * 512], 16, rs5, float(D), sqb5, [])
        xoff = NE - NO
        xring4 = [A.alloc([128, 512], F32) for _ in range(4)]
        x4free = [None] * 4
        h_toks = []
        k4 = 0
        for m in range(16):
            for nt in range(2):
                sl = slice(nt * 512, (nt + 1) * 512)
                xb = xring4[k4 % 4]
                t_x = P.dma("sp", xb, xT[m * 128:(m + 1) * 128, xoff + nt * 512:xoff + (nt + 1) * 512], [x4free[k4 % 4]])
                t1_ = P.op("dve", lambda E, m=m, sl=sl: E.scalar_tensor_tensor(mixT[:, m, sl], mixT[:, m, sl], gcol("mix_post", m), rs5[:, sl], ALU.mult, ALU.mult), [t_rs5])
                t2_ = P.op("dve", lambda E, m=m, sl=sl, xb=xb: E.tensor_tensor(mixT[:, m, sl], mixT[:, m, sl], xb, ALU.add), [t1_, t_x])
                x4free[k4 % 4] = t2_
                P.dma("pool", h_s[m * 128:(m + 1) * 128, sl], mixT[:, m, sl], [t2_])
                h_toks.append(t2_)
                k4 += 1
        tap("h1", mixT.rearrange("p a b -> p (a b)"), [128, 16 * 1024])
        hn2 = mixed
        rs5b = A.alloc([128, 1024], F32)
        prenorm(mixT, "mlp_pre", hn2, sqb5, rs5b, lambda i, nt: h_toks[2 * i + nt])
        P.barrier()
        A.pop()

        act = A.alloc([128, 64, 1024], BF16)
        wr5 = make_wring(3)
        rtmp = [A.alloc([128, 512], F32) for _ in range(3)]
        rfree = [None] * 3
        rk = [0]

        def ev_up(m, nt, b, t_mm):
            i = rk[0] % 3
            rk[0] += 1
            t_r = P.op("act", lambda E, i=i, b=b: E.activation(rtmp[i], banks[b][:, :], AF.Relu), [t_mm, rfree[i]])
            t_q = P.op("dve", lambda E, i=i, m=m, nt=nt: E.tensor_tensor(act[:, m, nt * 512:(nt + 1) * 512], rtmp[i], rtmp[i], ALU.mult), [t_r])
            rfree[i] = t_q
            return t_r
        linear(w_up.rearrange("(kt p) m -> p kt m", p=128), 16, list(range(64)), lambda kt, nt: hn2[:, kt, nt * 512:(nt + 1) * 512], ev_up, wr5,
               [[0, 1, 2, 3], [4, 5, 6, 7]], [])
        P.barrier()

        def ev_dn(m, nt, b, t_mm):
            i = rk[0] % 3
            rk[0] += 1
            t_c = P.op("act", lambda E, i=i, b=b: E.copy(rtmp[i], banks[b][:, :]), [t_mm, rfree[i]])
            rfree[i] = P.dma("sp", ff_s[m * 128:(m + 1) * 128, nt * 512:(nt + 1) * 512], rtmp[i], [t_c])
            return t_c
        linear(w_down.rearrange("(kt p) m -> p kt m", p=128), 64, list(range(16)), lambda kt, nt: act[:, kt, nt * 512:(nt + 1) * 512], ev_dn, wr5,
               [[0, 1, 2, 3], [4, 5, 6, 7]], [])
        P.barrier()
        A.pop()

        A.push()
        h2 = A.alloc([128, 16, 1024], F32)
        prod = A.alloc([128, 16, 1024], F32)
        hn3 = A.alloc([128, 16, 1024], BF16)
        sqb7 = [A.alloc([128, 512], BF16) for _ in range(2)]
        rs7 = A.alloc([128, 1024], F32)
        t_ff = load_full(prod, ff_s, [])
        t_h2 = load_full(h2, h_s, [])
        t_rs7 = stats_of(lambda i, nt: prod[:, i, nt * 512:(nt + 1) * 512], 16, rs7, float(D), sqb7, lambda i, nt: t_ff[i])
        res_t = []
        for m in range(16):
            for nt in range(2):
                sl = slice(nt * 512, (nt + 1) * 512)
                t1_ = P.op("dve", lambda E, m=m, sl=sl: E.scalar_tensor_tensor(prod[:, m, sl], prod[:, m, sl], gcol("mlp_post", m), rs7[:, sl], ALU.mult, ALU.mult), [t_rs7])
                res_t.append(P.op("dve", lambda E, m=m, sl=sl: E.tensor_tensor(h2[:, m, sl], h2[:, m, sl], prod[:, m, sl], ALU.add), [t1_, t_h2]))
        P.barrier()
        tap("h2", h2.rearrange("p a b -> p (a b)"), [128, 16 * 1024])
        rs7b = A.alloc([128, 1024], F32)
        prenorm(h2, "ple_pre", hn3, sqb7, rs7b, lambda i, nt: res_t[2 * i + nt])
        P.barrier()
        wr6 = make_wring(2)
        wpp = A.alloc([128, 2, D], BF16)
        pTb = A.alloc([128, 2, 1024], BF16)
        gt6 = [A.alloc([128, 512], F32) for _ in range(2)]
        g6free = [None, None]
        g6k = [0]
        t_wpp = P.dma("pool", wpp, w_pp.rearrange("(kt p) m -> p kt m", p=128))
        t_pT = P.dma("pool", pTb, pT.rearrange("(kt p) t -> p kt t", p=128))

        prod_t = {}

        def ev6(m, nt, b, t_mm):
            i = g6k[0] % 2
            g6k[0] += 1
            eb = 4 + (b % 4)
            P.op("pe", lambda E, eb=eb, m=m, nt=nt: E.matmul(banks[eb][:, :], wpp[:, 0, m * 128:(m + 1) * 128], pTb[:, 0, nt * 512:(nt + 1) * 512], start=True, stop=False),
                 [t_wpp, t_pT, bfree[eb]], inc=False)
            t_e = P.op("pe", lambda E, eb=eb, m=m, nt=nt: E.matmul(banks[eb][:, :], wpp[:, 1, m * 128:(m + 1) * 128], pTb[:, 1, nt * 512:(nt + 1) * 512], start=False, stop=True), [])
            t_s = P.op("act", lambda E, i=i, b=b: E.activation(gt6[i], banks[b][:, :], AF.Sigmoid), [t_mm, g6free[i]])
            t_p = P.op("dve", lambda E, i=i, eb=eb, m=m, nt=nt: E.tensor_tensor(prod[:, m, nt * 512:(nt + 1) * 512], gt6[i], banks[eb][:, :], ALU.mult), [t_s, t_e])
            g6free[i] = t_p
            bfree[eb] = t_p
            prod_t[(m, nt)] = t_p
            return t_s
        linear(w_pg.rearrange("(kt p) m -> p kt m", p=128), 16, list(range(16)), lambda kt, nt: hn3[:, kt, nt * 512:(nt + 1) * 512], ev6, wr6,
               [[0, 1, 2, 3]], [])
        P.barrier()
        t_rs8 = stats_of(lambda i, nt: prod[:, i, nt * 512:(nt + 1) * 512], 16, rs7, float(D), sqb7, [])
        for m in range(16):
            for nt in range(2):
                sl = slice(nt * 512, (nt + 1) * 512)
                t1_ = P.op("dve", lambda E, m=m, sl=sl: E.scalar_tensor_tensor(prod[:, m, sl], prod[:, m, sl], gcol("ple_post", m), rs7[:, sl], ALU.mult, ALU.mult), [t_rs8])
                t2_ = P.op("dve", lambda E, m=m, sl=sl: E.tensor_tensor(h2[:, m, sl], h2[:, m, sl], prod[:, m, sl], ALU.add), [t1_])
                P.dma("sp", out_d[m * 128:(m + 1) * 128, sl], h2[:, m, sl], [t2_])
        P.barrier()
        A.pop()
        P.emit()
    return nc, tap_out


def col_layout(v):
    v = np.asarray(v, np.float32).reshape(-1)
    return v.reshape(-1, 128).T


def prep_shared(inp):
    sh = {}
    colsv = [inp["mix_norm_pre"][0], inp["attn_out_norm"][0], inp["ssm_out_norm"][0], inp["mix_norm_post"][0],
             inp["mlp_norm_pre"][0], inp["mlp_norm_post"][0], inp["ple_norm_pre"][0], inp["ple_norm_post"][0],
             inp["ssm_d"][0], inp["b_glu"][0]]
    sh["cols"] = np.ascontiguousarray(np.concatenate([col_layout(v) for v in colsv], axis=1))

    def st(v):
        v = np.asarray(v, np.float32).reshape(32, 2, 64)
        return v.transpose(1, 2, 0).reshape(128, 32)
    ldt = np.broadcast_to(np.asarray(inp["log_dt"][0], np.float32)[:, None], (64, 64))
    sh["sp"] = np.ascontiguousarray(np.concatenate([st(inp["lam_re"][0]), st(inp["lam_im"][0]), st(ldt)], axis=1))

    def padB(B):
        B = np.asarray(B, np.float32).reshape(32, 2, 64, 16)
        o = np.zeros((2, 64, 32, 2, 16), np.float32)
        for gl in range(2):
            o[gl, :, :, gl, :] = B[:, gl].transpose(1, 0, 2)
        return o.reshape(128, 1024)

    def padC(C):
        C = np.asarray(C, np.float32).reshape(32, 2, 16, 64)
        o = np.zeros((2, 64, 32, 2, 16), np.float32)
        for gl in range(2):
            o[gl, :, :, gl, :] = C[:, gl].transpose(2, 0, 1)
        return o.reshape(128, 1024)
    sh["bpr"] = padB(inp["ssm_b_re"][0])
    sh["bpi"] = padB(inp["ssm_b_im"][0])
    sh["cpr"] = padC(inp["ssm_c_re"][0])
    sh["cpi"] = padC(inp["ssm_c_im"][0])
    for k_, n_ in (("w_in", "w_in"), ("w_glu", "w_glu"), ("w_out", "w_out"), ("w_up", "w_up"), ("w_down", "w_down"),
                   ("w_ple_gate", "w_pg"), ("w_ple_proj", "w_pp")):
        sh[n_] = np.ascontiguousarray(np.asarray(inp[k_][0], np.float32))
    return sh


def consts_for_core(j):
    kk = np.arange(128)[:, None]
    ii = np.arange(128)[None, :]
    ident = (kk == ii).astype(np.float32)
    mprev = (kk >= ii).astype(np.float32)
    mcur = (kk <= ii).astype(np.float32)
    bmask = ((kk // 16) == (ii // 16)).astype(np.float32)
    pmask = ((kk // 32) == np.arange(4)[None, :]).astype(np.float32)
    T0 = 1024 * j
    vc = np.zeros((128, NVT), np.float32)
    for i, (d, r, m, nk) in enumerate(VT_LIST):
        t_abs = T0 + r + d * (m + np.arange(nk))
        vc[:nk, i] = (t_abs >= 0).astype(np.float32)
    return np.ascontiguousarray(np.concatenate([ident, mprev, mcur, bmask, pmask, vc], axis=1))


def prep_core(c, inp, sh):
    b, j = c // 4, c % 4
    T0 = 1024 * j
    x = np.asarray(inp["x"], np.float32)
    xe = np.zeros((NE, D), np.float32)
    lo = T0 - (NE - NO)
    s0 = max(lo, 0)
    xe[s0 - lo:, :] = x[b, s0:T0 + NO, :]
    m = dict(sh)
    m["xT"] = np.ascontiguousarray(xe.T)
    m["pT"] = np.ascontiguousarray(np.asarray(inp["p"], np.float32)[0, b, T0:T0 + NO, :].T)
    m["consts"] = consts_for_core(j)
    return m


_CACHE = {}


def kernel(**inputs):
    if "nc" not in _CACHE:
        _CACHE["nc"] = build_nc()[0]
    nc = _CACHE["nc"]
    sh = prep_shared(inputs)
    in_maps = [prep_core(c, inputs, sh) for c in range(8)]
    res = run_bass_kernel_spmd(nc, in_maps, core_ids=list(range(8)))
    out = np.zeros((2, 4096, D), np.float32)
    for c in range(8):
        b, j = c // 4, c % 4
        out[b, 1024 * j:1024 * (j + 1), :] = res.results[c]["out"].T
    return out
```

```python
import math
import os
KSTOP = int(os.environ.get('KSTOP', '0'))
KNG = int(os.environ.get('KNG', '8'))
KNH = int(os.environ.get('KNH', '8'))
KNOF = int(os.environ.get('KNOF', '0'))
KSIDE = int(os.environ.get('KSIDE', '1'))
from contextlib import ExitStack

import numpy as np
import concourse.bass as bass
import concourse.mybir as mybir
from concourse.bass_utils import run_bass_kernel_spmd

F32 = mybir.dt.float32
BF16 = mybir.dt.bfloat16
ALU = mybir.AluOpType
AF = mybir.ActivationFunctionType

D = 2048
KT = 16
NE = 4096
NO = 1024
NCH = 4
DFF = 8192
EPS = 1e-6
MAGIC = 12582912.0
TWO_PI = 2.0 * math.pi

COLS = [("mix_pre", 16), ("attn_n", 8), ("ssm_n", 8), ("mix_post", 16), ("mlp_pre", 16),
        ("mlp_post", 16), ("ple_pre", 16), ("ple_post", 16), ("ssm_d", 8), ("b_glu", 8)]
COL_OFF = {}
_o = 0
for _n, _c in COLS:
    COL_OFF[_n] = _o
    _o += _c
NCOLS = _o


def vtile_list():
    tiles = []
    for d in (1, 4, 16):
        M = NO // d
        for r in range(d):
            m = -128
            while m < M:
                nk = min(128, M - m)
                tiles.append((d, r, m, nk))
                m += 128
    return tiles


VT_LIST = vtile_list()
VT_IDX = {(d, r, m): i for i, (d, r, m, nk) in enumerate(VT_LIST)}
NVT = len(VT_LIST)


class Prog:
    ENG = ("pe", "act", "dve", "pool", "sp")

    def __init__(self, nc, stack, n_dma_sems=32):
        self.nc = nc
        self.q = {e: [] for e in self.ENG}
        self.cnt = {e: 0 for e in self.ENG}
        self.sem = {e: stack.enter_context(nc.semaphore("s_" + e)) for e in self.ENG}
        self.waited = {}
        n_sw = 16
        self.dsem = [stack.enter_context(nc.semaphore("d%d" % i)) for i in range(n_dma_sems + n_sw)]
        self.dcnt = [0] * (n_dma_sems + n_sw)
        self.drange = {"sp": (0, n_dma_sems), "pool": (n_dma_sems, n_dma_sems + n_sw)}
        self.dnext = {"sp": 0, "pool": n_dma_sems}
        self.ninst = 0

    def _waits(self, eng, deps):
        for d in deps:
            if d is None:
                continue
            if isinstance(d, list) or (isinstance(d, tuple) and len(d) and not isinstance(d[0], str)):
                self._waits(eng, d)
                continue
            kind, key, val = d
            wk = (eng, kind, key)
            if self.waited.get(wk, 0) >= val:
                continue
            self.waited[wk] = val
            sem = self.sem[key] if kind == "e" else self.dsem[key]
            self.q[eng].append(lambda E, sem=sem, val=val: E.wait_ge(sem, val))
            self.ninst += 1

    def op(self, eng, fn, deps=(), inc=True):
        self._waits(eng, deps)
        self.ninst += 1
        if inc:
            self.cnt[eng] += 1
            v = self.cnt[eng]
            sem = self.sem[eng]
            self.q[eng].append(lambda E, fn=fn, sem=sem: fn(E).then_inc(sem, 1))
            return ("e", eng, v)
        self.q[eng].append(lambda E, fn=fn: fn(E))
        return None

    def dma(self, eng, out, in_, deps=(), **kw):
        lo, hi = self.drange[eng]
        i = self.dnext[eng]
        self.dnext[eng] = lo + (i + 1 - lo) % (hi - lo)
        if self.dcnt[i] > 0:
            self._waits(eng, [("d", i, self.dcnt[i])])
        self._waits(eng, deps)
        self.dcnt[i] += 16
        sem = self.dsem[i]
        self.ninst += 1
        self.q[eng].append(lambda E, out=out, in_=in_, sem=sem, kw=kw: E.dma_start(out=out, in_=in_, **kw).then_inc(sem, 16))
        return ("d", i, self.dcnt[i])

    def all_tokens(self):
        toks = [("e", e, self.cnt[e]) for e in self.ENG if self.cnt[e] > 0]
        toks += [("d", i, self.dcnt[i]) for i in range(len(self.dsem)) if self.dcnt[i] > 0]
        return toks

    def barrier(self):
        toks = self.all_tokens()
        for e in self.ENG:
            self._waits(e, toks)

    def emit(self):
        nc = self.nc
        self._waits("sp", self.all_tokens())
        with nc.Block() as block:
            @block.tensor
            def _(E):
                for f in self.q["pe"]:
                    f(E)

            @block.scalar
            def _(E):
                for f in self.q["act"]:
                    f(E)

            @block.vector
            def _(E):
                for f in self.q["dve"]:
                    f(E)

            @block.gpsimd
            def _(E):
                for f in self.q["pool"]:
                    f(E)

            @block.sync
            def _(E):
                for f in self.q["sp"]:
                    f(E)


class Seq:
    def __init__(self, P, eng, deps=()):
        self.P = P
        self.eng = eng
        self.last = list(deps)

    def op(self, fn, deps=(), eng=None):
        e = eng or self.eng
        t = self.P.op(e, fn, [self.last, list(deps)])
        self.last = [t]
        return t

    def par(self, fns):
        toks = [self.P.op(self.eng, fn, [self.last]) for fn in fns]
        self.last = toks
        return toks


class Arena:
    def __init__(self, big, total):
        self.big = big
        self.total = total
        self.top = 0
        self.marks = []
        self.peak = 0

    def push(self):
        self.marks.append(self.top)

    def pop(self):
        self.top = self.marks.pop()

    def alloc(self, shape, dt):
        n = 1
        for s in shape[1:]:
            n *= s
        es = 2 if dt == BF16 else 4
        nbytes = (n * es + 63) // 64 * 64
        off = self.top
        self.top += nbytes
        self.peak = max(self.peak, self.top)
        assert self.top <= self.total, ("arena overflow", self.top, self.total)
        ap = self.big[:, off // 4:(off + nbytes) // 4]
        if dt == BF16:
            ap = ap.bitcast(BF16)
        ap = ap[:, 0:n]
        fs = shape[1:]
        if len(fs) == 2:
            ap = ap.rearrange("p (a b) -> p a b", b=fs[1])
        elif len(fs) == 3:
            ap = ap.rearrange("p (a b c) -> p a b c", b=fs[1], c=fs[2])
        elif len(fs) == 4:
            ap = ap.rearrange("p (a b c d) -> p a b c d", b=fs[1], c=fs[2], d=fs[3])
        if shape[0] < 128:
            ap = ap[0:shape[0]]
        return ap


def bcast_last(ap2, n):
    return ap2.unsqueeze(2).to_broadcast([ap2.shape[0], ap2.shape[1], n])


def build_nc(taps=()):
    nc = bass.Bass("TRN2", target_bir_lowering=False)
    dr = lambda name, shape, dt=F32: nc.dram_tensor(name, shape, dt, kind="ExternalInput").ap()
    xT = dr("xT", [D, NE])
    pT = dr("pT", [256, NO])
    cols_d = dr("cols", [128, NCOLS])
    sp_d = dr("sp", [128, 96])
    bpr_d = dr("bpr", [128, 1024])
    bpi_d = dr("bpi", [128, 1024])
    cpr_d = dr("cpr", [128, 1024])
    cpi_d = dr("cpi", [128, 1024])
    consts_d = dr("consts", [128, 4 * 128 + 4 + NVT])
    w_in = dr("w_in", [D, 4096])
    w_glu = dr("w_glu", [1024, 1024])
    w_out = dr("w_out", [D, D])
    w_up = dr("w_up", [D, DFF])
    w_down = dr("w_down", [DFF, D])
    w_pg = dr("w_pg", [D, D])
    w_pp = dr("w_pp", [256, D])
    out_d = nc.dram_tensor("out", [D, NO], F32, kind="ExternalOutput").ap()
    xg_s = nc.dram_tensor("xg_s", [D, 3072], BF16).ap()
    rs_s = nc.dram_tensor("rs_s", [128, 3072], F32).ap()
    h_s = nc.dram_tensor("h_s", [D, NO], F32).ap()
    ff_s = nc.dram_tensor("ff_s", [D, NO], F32).ap()
    yg_s = nc.dram_tensor("yg_s", [1024, NO], F32).ap()
    tap_out = {}

    w_in_v = w_in.rearrange("(kt p) m -> p kt m", p=128)

    with ExitStack() as top:
        P = Prog(nc, top)
        TOTAL = 207 * 1024
        big = top.enter_context(nc.sbuf_tensor("big", [128, TOTAL // 4], F32))
        A = Arena(big, TOTAL)
        banks = [top.enter_context(nc.psum_tensor("bank%d" % i, [128, 512], F32)) for i in range(8)]
        bfree = [None] * 8

        def tap(name, ap, shape):
            if name not in taps:
                return
            P.barrier()
            t = nc.dram_tensor("tap_" + name, list(shape), ap.dtype, kind="ExternalOutput").ap()
            tap_out[name] = t
            P.dma("sp", t, ap)
            P.barrier()

        cols = A.alloc([128, NCOLS], F32)
        cst = A.alloc([128, 4 * 128 + 4 + NVT], F32)
        ident_f = cst[:, 0:128]
        mprev_f = cst[:, 128:256]
        mcur_f = cst[:, 256:384]
        bmask = cst[:, 384:512]
        pmask = cst[:, 512:516]
        vcols = cst[:, 516:516 + NVT]
        ident_b = A.alloc([128, 128], BF16)
        ones_b = A.alloc([128, 128], BF16)
        mprev_b = A.alloc([128, 128], BF16)
        mcur_b = A.alloc([128, 128], BF16)
        epsc = A.alloc([128, 1], F32)
        t_cols = P.dma("sp", cols, cols_d)
        t_cst = P.dma("sp", cst, consts_d)
        t0 = P.op("dve", lambda E: E.tensor_copy(ident_b, ident_f), [t_cst])
        t1 = P.op("dve", lambda E: E.tensor_copy(mprev_b, mprev_f), [t_cst])
        t2 = P.op("dve", lambda E: E.tensor_copy(mcur_b, mcur_f), [t_cst])
        t3 = P.op("pool", lambda E: E.memset(ones_b, 1.0))
        t4 = P.op("pool", lambda E: E.memset(epsc, EPS))
        P.barrier()

        if KSTOP == 3:
            tap("cst", cst, [128, 4 * 128 + 4 + NVT])
            P.emit()
            return nc, tap_out

        def gcol(name, i):
            o = COL_OFF[name] + i
            return cols[:, o:o + 1]

        def rms_stats(src_fn, ntiles, ntok, rs_out, nfeat, sqbufs, stat_banks, deps):
            toks = []
            nnt = ntok // 512
            sqfree = [None] * len(sqbufs)
            j = 0
            last_mm = [None] * nnt
            for i in range(ntiles):
                for nt in range(nnt):
                    sq = sqbufs[j % len(sqbufs)]
                    dp = deps(i, nt) if callable(deps) else deps
                    t_sq = P.op("act", lambda E, sq=sq, i=i, nt=nt: E.activation(sq, src_fn(i, nt), AF.Square), [dp, sqfree[j % len(sqbufs)]])
                    b = stat_banks[nt]
                    t_mm = P.op("pe", lambda E, sq=sq, b=b, i=i: E.matmul(banks[b][:, :], ones_b, sq, start=(i == 0), stop=(i == ntiles - 1)),
                                [t_sq, bfree[b] if i == 0 else None])
                    sqfree[j % len(sqbufs)] = t_mm
                    last_mm[nt] = t_mm
                    j += 1
            for nt in range(nnt):
                b = stat_banks[nt]
                ta = P.op("act", lambda E, b=b, nt=nt: E.activation(rs_out[:, nt * 512:(nt + 1) * 512], banks[b][:, :], AF.Sqrt, bias=epsc, scale=1.0 / nfeat), [last_mm[nt]])
                bfree[b] = ta
                tb = P.op("dve", lambda E, nt=nt: E.reciprocal(rs_out[:, nt * 512:(nt + 1) * 512], rs_out[:, nt * 512:(nt + 1) * 512]), [ta])
                toks.append(tb)
            return toks

        A.push()
        yg = None
        spv = A.alloc([128, 96], F32)
        G = A.alloc([128, 8, 2, 8, 128], BF16)
        H = A.alloc([128, 8, 8, 128], BF16)
        ctab = A.alloc([128, 32, 128], F32)
        stab = A.alloc([128, 32, 128], F32)
        sm = A.alloc([128, 48, 32], F32)
        Zin_r = sm[:, 0, :]
        Zin_i = sm[:, 1, :]
        rho8 = sm[:, 2, :]
        e8r = sm[:, 3, :]
        e8i = sm[:, 4, :]
        apow = [(sm[:, 5 + 2 * t, :], sm[:, 6 + 2 * t, :]) for t in range(9)]
        _n = [23]

        def smalloc():
            i = _n[0]
            _n[0] += 1
            assert i < 48
            return sm[:, i, :]

        t_sp = P.dma("sp", spv, sp_d)
        S = Seq(P, "dve", [t_sp, t_cols])
        S.op(lambda E: E.memset(Zin_r, 0.0))
        S.op(lambda E: E.memset(Zin_i, 0.0))

        def cmul(S, or_, oi_, ar, ai, br, bi, t1, t2):
            S.par([lambda E: E.tensor_tensor(or_, ar, br, ALU.mult),
                   lambda E: E.tensor_tensor(t1, ai, bi, ALU.mult),
                   lambda E: E.tensor_tensor(oi_, ar, bi, ALU.mult),
                   lambda E: E.tensor_tensor(t2, ai, br, ALU.mult)])
            S.par([lambda E: E.tensor_tensor(or_, or_, t1, ALU.subtract),
                   lambda E: E.tensor_tensor(oi_, oi_, t2, ALU.add)])

        A.push()
        CPr = A.alloc([128, 32, 32], F32)
        CPni = A.alloc([128, 32, 32], F32)
        CPi0 = A.alloc([128, 32, 32], F32)
        t_cpr = P.dma("sp", CPr, cpr_d.rearrange("p (a b) -> p a b", b=32))
        t_cpi = P.dma("sp", CPi0, cpi_d.rearrange("p (a b) -> p a b", b=32))
        S.last = [S.last, t_cpr, t_cpi]
        S.op(lambda E: E.memset(CPni, 0.0))
        S.op(lambda E: E.tensor_tensor(CPni, CPni, CPi0, ALU.subtract))
        lamr = spv[:, 0:32]
        lami = spv[:, 32:64]
        ldt = spv[:, 64:96]
        dt_ = smalloc(); zr = smalloc(); zi = smalloc(); em1 = smalloc(); mag = smalloc()
        c1 = smalloc(); s1 = smalloc(); sh = smalloc(); ta = smalloc(); tb = smalloc()
        numr = smalloc(); numi = smalloc(); cfr = smalloc(); cfi = smalloc(); tc = smalloc(); td = smalloc()
        S.op(lambda E: E.activation(dt_, ldt, AF.Exp), eng="act")
        S.op(lambda E: E.tensor_tensor(zr, lamr, dt_, ALU.mult))
        S.op(lambda E: E.tensor_tensor(zi, lami, dt_, ALU.mult))
        S.op(lambda E: E.tensor_scalar(em1, zr, 1.0 / 120.0, None, ALU.mult))
        for cst_ in (1.0 / 24.0, 1.0 / 6.0, 0.5, 1.0):
            S.op(lambda E, c=cst_: E.scalar_tensor_tensor(em1, em1, c, zr, ALU.add, ALU.mult))
        S.op(lambda E: E.tensor_scalar(mag, em1, 1.0, None, ALU.add))

        def sin_of(dst, src, shift, scale):
            S.op(lambda E: E.tensor_scalar(ta, src, scale, shift, ALU.mult, ALU.add))
            S.op(lambda E: E.tensor_scalar(tb, ta, 1.0 / TWO_PI, MAGIC, ALU.mult, ALU.add))
            S.op(lambda E: E.tensor_scalar(tb, tb, MAGIC, None, ALU.subtract))
            S.op(lambda E: E.scalar_tensor_tensor(ta, tb, -TWO_PI, ta, ALU.mult, ALU.add))
            S.op(lambda E: E.tensor_scalar(ta, ta, math.pi, -math.pi, ALU.min, ALU.max))
            S.op(lambda E: E.activation(dst, ta, AF.Sin), eng="act")

        sin_of(s1, zi, 0.0, 1.0)
        sin_of(c1, zi, math.pi / 2.0, 1.0)
        sin_of(sh, zi, 0.0, 0.5)
        S.op(lambda E: E.tensor_tensor(ta, sh, sh, ALU.mult))
        S.op(lambda E: E.tensor_tensor(tb, em1, c1, ALU.mult))
        S.op(lambda E: E.scalar_tensor_tensor(numr, ta, -2.0, tb, ALU.mult, ALU.add))
        S.op(lambda E: E.tensor_tensor(numi, mag, s1, ALU.mult))
        S.op(lambda E: E.tensor_tensor(ta, lamr, lamr, ALU.mult))
        S.op(lambda E: E.tensor_tensor(tb, lami, lami, ALU.mult))
        S.op(lambda E: E.tensor_tensor(ta, ta, tb, ALU.add))
        S.op(lambda E: E.reciprocal(ta, ta))
        S.op(lambda E: E.tensor_tensor(tb, numr, lamr, ALU.mult))
        S.op(lambda E: E.tensor_tensor(tc, numi, lami, ALU.mult))
        S.op(lambda E: E.tensor_tensor(tb, tb, tc, ALU.add))
        S.op(lambda E: E.tensor_tensor(cfr, tb, ta, ALU.mult))
        S.op(lambda E: E.tensor_tensor(tb, numi, lamr, ALU.mult))
        S.op(lambda E: E.tensor_tensor(tc, numr, lami, ALU.mult))
        S.op(lambda E: E.tensor_tensor(tb, tb, tc, ALU.subtract))
        S.op(lambda E: E.tensor_tensor(cfi, tb, ta, ALU.mult))
        S.op(lambda E: E.memset(apow[0][0], 1.0))
        S.op(lambda E: E.memset(apow[0][1], 0.0))
        S.op(lambda E: E.tensor_tensor(apow[1][0], mag, c1, ALU.mult))
        S.op(lambda E: E.tensor_tensor(apow[1][1], mag, s1, ALU.mult))
        for t in range(1, 8):
            cmul(S, apow[t + 1][0], apow[t + 1][1], apow[t][0], apow[t][1], apow[1][0], apow[1][1], tc, td)
        S.op(lambda E: E.tensor_tensor(rho8, mag, mag, ALU.mult))
        S.op(lambda E: E.tensor_tensor(rho8, rho8, rho8, ALU.mult))
        S.op(lambda E: E.tensor_tensor(rho8, rho8, rho8, ALU.mult))
        e2r = smalloc(); e2i = smalloc()
        cmul(S, e2r, e2i, c1, s1, c1, s1, tc, td)
        e4r = smalloc(); e4i = smalloc()
        cmul(S, e4r, e4i, e2r, e2i, e2r, e2i, tc, td)
        cmul(S, e8r, e8i, e4r, e4i, e4r, e4i, tc, td)
        if KSTOP == 4:
            tap("sm", sm.rearrange("p a b -> p (a b)"), [128, 48 * 32])
            P.emit()
            return nc, tap_out
        big1 = A.alloc([128, 32, 64], F32)
        big2 = A.alloc([128, 32, 64], F32)
        S.op(lambda E: E.memset(ctab[:, :, 0:1], 1.0))
        S.op(lambda E: E.memset(stab[:, :, 0:1], 0.0))
        Er, Ei = e8r, e8i
        epp = [(smalloc(), smalloc()), (smalloc(), smalloc())]
        epi = 0
        L = 1
        while L < 128:
            br_ = bcast_last(Er, L)
            bi_ = bcast_last(Ei, L)
            cmul(S, ctab[:, :, L:2 * L], stab[:, :, L:2 * L], ctab[:, :, 0:L], stab[:, :, 0:L], br_, bi_,
                 big1[:, :, 0:L], big2[:, :, 0:L])
            if 2 * L < 128:
                nr, ni = epp[epi % 2]
                epi += 1
                cmul(S, nr, ni, Er, Ei, Er, Ei, tc, td)
                Er, Ei = nr, ni
            L *= 2
        if KSTOP == 5:
            tap("ctab", ctab.rearrange("p a b -> p (a b)"), [128, 32 * 128])
            tap("sm", sm.rearrange("p a b -> p (a b)"), [128, 48 * 32])
            P.emit()
            return nc, tap_out
        BPr = A.alloc([128, 32, 32], F32)
        BPi = A.alloc([128, 32, 32], F32)
        BBr = A.alloc([128, 32, 32], F32)
        BBi = A.alloc([128, 32, 32], F32)
        ABr = A.alloc([128, 32, 32], F32)
        ABi = A.alloc([128, 32, 32], F32)
        T1 = A.alloc([128, 32, 32], F32)
        T2 = A.alloc([128, 32, 32], F32)
        tmpH4 = A.alloc([128, 4, 128], F32)
        t_bpr = P.dma("sp", BPr, bpr_d.rearrange("p (a b) -> p a b", b=32))
        t_bpi = P.dma("sp", BPi, bpi_d.rearrange("p (a b) -> p a b", b=32))
        S.last = [S.last, t_bpr, t_bpi]
        cmul(S, BBr, BBi, BPr, BPi, bcast_last(cfr, 32), bcast_last(cfi, 32), T1, T2)
        if KSTOP == 6:
            tap("BBr", BBr.rearrange("p a b -> p (a b)"), [128, 1024])
            P.emit()
            return nc, tap_out
        ABh = [A.alloc([128, 1024], BF16) for _ in range(2)]
        ABl = [A.alloc([128, 1024], BF16) for _ in range(2)]
        Chl = [A.alloc([128, 1024], BF16) for _ in range(4)]
        for pp, src_ in ((0, CPr), (1, CPni)):
            srcf = src_.rearrange("p a b -> p (a b)")
            S.op(lambda E, pp=pp, srcf=srcf: E.tensor_copy(Chl[2 * pp], srcf))
            S.op(lambda E, pp=pp, srcf=srcf: E.tensor_tensor(T1.rearrange("p a b -> p (a b)"), srcf, Chl[2 * pp], ALU.subtract))
            S.op(lambda E, pp=pp: E.tensor_copy(Chl[2 * pp + 1], T1.rearrange("p a b -> p (a b)")))
        gh_done = []
        for tau in range(8 if KSTOP != 7 else 1):
            cmul(S, ABr, ABi, BBr, BBi, bcast_last(apow[tau][0], 32), bcast_last(apow[tau][1], 32), T1, T2)
            for pp, src_ in ((0, ABr), (1, ABi)):
                srcf = src_.rearrange("p a b -> p (a b)")
                S.op(lambda E, pp=pp, srcf=srcf: E.tensor_copy(ABh[pp], srcf))
                S.op(lambda E, pp=pp, srcf=srcf: E.tensor_tensor(T1.rearrange("p a b -> p (a b)"), srcf, ABh[pp], ALU.subtract))
                S.op(lambda E, pp=pp: E.tensor_copy(ABl[pp], T1.rearrange("p a b -> p (a b)")))
            t_hl = S.last
            t_ab = S.last
            ABrf = ABr.rearrange("p a b -> p (a b)")
            ABif = ABi.rearrange("p a b -> p (a b)")
            CPrf = CPr.rearrange("p a b -> p (a b)")
            CPnif = CPni.rearrange("p a b -> p (a b)")
            rd = []
            for part, src in ((0, ABrf), (1, ABif)):
                for half in range(2):
                    b = part * 2 + half
                    t_tr = None
                    for q in range(4):
                        ct = half * 4 + q
                        sl = slice(q * 128, q * 128 + 128)
                        t_tr = P.op("pe", lambda E, b=b, sl=sl, src=src, ct=ct: E.transpose(banks[b][:, sl], src[:, ct * 128:(ct + 1) * 128], ident_f),
                                    [t_ab, bfree[b]], inc=(q == 3))
                    t_cp = P.op("act", lambda E, b=b, tau=tau, part=part, half=half: E.copy(
                        G[:, tau, part, half * 4:half * 4 + 4, :].rearrange("p a b -> p (a b)"), banks[b][:, :]), [t_tr])
                    rd.append((b, t_cp))
            for half in range(2):
                b = 4 + half
                t_mm = None
                pairs = [(ABh[0], Chl[0]), (ABh[0], Chl[1]), (ABl[0], Chl[0]), (ABh[1], Chl[2]), (ABh[1], Chl[3]), (ABl[1], Chl[2])]
                for q in range(4):
                    ct = half * 4 + q
                    sl = slice(q * 128, q * 128 + 128)
                    for pi_, (aa, cc) in enumerate(pairs):
                        t_mm = P.op("pe", lambda E, b=b, sl=sl, ct=ct, aa=aa, cc=cc, pi_=pi_, q=q: E.matmul(
                            banks[b][:, sl], aa[:, ct * 128:(ct + 1) * 128], cc[:, ct * 128:(ct + 1) * 128],
                            start=(pi_ == 0 and q == 0), stop=(pi_ == 5), skip_group_check=True),
                            [t_hl, bfree[b]] if (q == 0 and pi_ == 0) else [], inc=(q == 3 and pi_ == 5))
                bm4 = bmask.unsqueeze(1).to_broadcast([128, 4, 128])
                bk4 = banks[b][:, :].rearrange("p (a b) -> p a b", b=128)
                if tau == 0:
                    t_e1 = P.op("dve", lambda E, bk4=bk4, bm4=bm4: E.tensor_tensor(tmpH4, bk4, bm4, ALU.mult), [t_mm, S.last])
                    rd.append((b, t_e1))
                    tl = t_e1
                    for q in range(4):
                        ct = half * 4 + q
                        tl = P.op("dve", lambda E, ct=ct, q=q: E.scalar_tensor_tensor(H[:, 0, ct, :], ident_f, gcol("ssm_d", ct), tmpH4[:, q, :], ALU.mult, ALU.add), [tl])
                    S.last = [tl]
                else:
                    t_ev = P.op("dve", lambda E, bk4=bk4, bm4=bm4, tau=tau, half=half: E.tensor_tensor(H[:, tau, half * 4:half * 4 + 4, :], bk4, bm4, ALU.mult), [t_mm])
                    rd.append((b, t_ev))
            for b in range(6):
                bfree[b] = [t for (bb, t) in rd if bb == b]
            gh_done.append([t for (_, t) in rd])
            S.last = [S.last, gh_done[-1]]
            if tau == 7:
                tap("ABr", ABr.rearrange("p a b -> p (a b)"), [128, 1024])
                tap("ABi", ABi.rearrange("p a b -> p (a b)"), [128, 1024])
                tap("CPr", CPr.rearrange("p a b -> p (a b)"), [128, 1024])
                tap("CPni", CPni.rearrange("p a b -> p (a b)"), [128, 1024])
                tap("Chl0", Chl[0], [128, 1024])
                tap("ABh0", ABh[0], [128, 1024])
        A.pop()
        P.barrier()
        tap("G", G.rearrange("p a b c d -> p (a b c d)"), [128, 8 * 2 * 8 * 128])
        tap("H", H.rearrange("p a b c -> p (a b c)"), [128, 8 * 8 * 128])
        tap("ctab", ctab.rearrange("p a b -> p (a b)"), [128, 32 * 128])
        tap("stab", stab.rearrange("p a b -> p (a b)"), [128, 32 * 128])
        tap("sm", sm.rearrange("p a b -> p (a b)"), [128, 48 * 32])

        ucs = [A.alloc([128, 8, 1024], BF16) for _ in range(2)]
        um_bufs = [A.alloc([128, 4, 1024], BF16) for _ in range(2)]
        wblk = A.alloc([128, 6, 512], F32)
        Wr, Wi, Sr, Si, t1a, t2a = [wblk[:, q, :].rearrange("p (a b) -> p a b", b=128) for q in range(6)]
        Zpr = A.alloc([128, 4, 128], BF16)
        Zpi = A.alloc([128, 4, 128], BF16)
        w0 = A.alloc([128, 6, 4], F32)
        udi = A.alloc([128, 1024], BF16)
        um_free = [None, None]
        ssm_state = {"dve": [], "zp_free": None}

        def norm_proj_chunk(c, wcol0, n_mt, dst_fn, save_scratch, env, nts=(0, 1)):
            xring, sqb, xg, rs, wring = env["xring"], env["sqb"], env["xg"], env["rs"], env["wring"]
            for nt in nts:
                tok0 = 1024 * c + 512 * nt
                last_mm = None
                xg_toks = []
                for kt in range(KT):
                    xi = env["xi"]; env["xi"] = (xi + 1) % len(xring)
                    xb = xring[xi]
                    t_ld = P.dma("sp", xb, xT[kt * 128:(kt + 1) * 128, tok0:tok0 + 512], [env["xfree"][xi]])
                    si = kt % 2
                    t_sq = P.op("act", lambda E, xb=xb, si=si: E.activation(sqb[si], xb, AF.Square), [t_ld, env["sqfree"][si]])
                    last_mm = P.op("pe", lambda E, si=si, kt=kt: E.matmul(banks[6][:, :], ones_b, sqb[si], start=(kt == 0), stop=(kt == KT - 1)),
                                   [t_sq, bfree[6] if kt == 0 else None])
                    env["sqfree"][si] = last_mm
                    t_xg = P.op("dve", lambda E, xb=xb, kt=kt: E.tensor_scalar(xg[:, kt, :], xb, gcol("mix_pre", kt), None, ALU.mult),
                                [t_ld, env["xg_free"]])
                    xg_toks.append(t_xg)
                    env["xfree"][xi] = [t_sq, t_xg]
                ta_ = P.op("act", lambda E: E.activation(rs, banks[6][:, :], AF.Sqrt, bias=epsc, scale=1.0 / D), [last_mm, env["rs_free"]])
                bfree[6] = ta_
                t_rs = P.op("dve", lambda E: E.reciprocal(rs, rs), [ta_])
                readers = []
                if save_scratch:
                    tcol = tok0 - 1024
                    readers.append(P.dma("sp", xg_s.rearrange("(kt p) t -> p kt t", p=128)[:, :, tcol:tcol + 512], xg, [xg_toks]))
                    readers.append(P.dma("sp", rs_s[:, tcol:tcol + 512], rs, [t_rs]))
                evs = []
                for m in range(n_mt):
                    if m % 2 == 0:
                        wi_ = env["wi"]; env["wi"] = (wi_ + 1) % len(wring)
                        wsl = wring[wi_]
                        t_w = P.dma("pool", wsl, w_in_v[:, :, wcol0 + m * 128:wcol0 + m * 128 + 256], [env["wfree"][wi_]])
                        env["wfree"][wi_] = []
                        cur = (wi_, wsl, t_w)
                    wi_, wsl, t_w = cur
                    b = (4, 5, 7)[env["bi"] % 3]; env["bi"] += 1
                    t_mm = None
                    for kt in range(KT):
                        t_mm = P.op("pe", lambda E, b=b, wsl=wsl, kt=kt, mo=(m % 2) * 128: E.matmul(banks[b][:, :], wsl[:, kt, mo:mo + 128], xg[:, kt, :],
                                                                                                  start=(kt == 0), stop=(kt == KT - 1)),
                                    [t_w, xg_toks, bfree[b]] if kt == 0 else [], inc=(kt == KT - 1))
                    env["wfree"][wi_].append(t_mm)
                    dst = dst_fn(m, nt)
                    t_ev = P.op("dve", lambda E, b=b, dst=dst: E.tensor_tensor(dst, banks[b][:, :], rs, ALU.mult), [t_mm, t_rs, env["dst_free"]])
                    bfree[b] = t_ev
                    evs.append(t_ev)
                    readers.append(t_mm)
                env["xg_free"] = readers
                env["rs_free"] = [evs, readers]
                env["done"].setdefault(c, []).append(evs)
            return

        def ssm_chunk(c, own, u_ready, uc, Fm=None, side=None):
            readers = []

            def do_ct(ct):
                ui = ct % 2
                um = um_bufs[ui]
                if own:
                    vb0, vb1 = (0, 1) if ct % 2 == 0 else (6, 7)
                else:
                    vb0, vb1 = (0, 1) if ct % 2 == 0 else (2, 3)
                ybase = 2 if ct % 2 == 0 else 4
                tm = []
                for j in range(4):
                    tm.append(P.op("act", lambda E, j=j, ct=ct: E.activation(um[:, j, :].rearrange("p (i k) -> p i k", k=128), uc[:, ct, :].rearrange("p (k i) -> p i k", i=8),
                                                                                AF.Copy, scale=pmask[:, j:j + 1]),
                                   [u_ready, um_free[ui]]))
                if own:
                    tm.append(P.op("act", lambda E, ct=ct: E.copy(udi.rearrange("p (i k) -> p i k", k=128), uc[:, ct, :].rearrange("p (k i) -> p i k", i=8)),
                                   [u_ready, ssm_state["zp_free"]]))
                t_v = None
                for part in range(2):
                    for i in range(8):
                        last = (part == 1 and i == 7)
                        vb = vb0 if part == 0 else vb1
                        t_v = P.op("pe", lambda E, part=part, i=i, ct=ct, vb=vb: E.matmul(banks[vb][:, :].rearrange("p (a b) -> p a b", b=128), G[:, 7 - i, part, ct, :], um[:, :, i * 128:(i + 1) * 128],
                                                                              start=(i == 0), stop=(i == 7)),
                                   [tm, bfree[vb0], bfree[vb1]] if (part == 0 and i == 0) else [], inc=last)
                um_free[ui] = t_v
                cs = ctab[:, 4 * ct:4 * ct + 4, :]
                sn = stab[:, 4 * ct:4 * ct + 4, :]
                Vr = banks[vb0][:, :].rearrange("p (a b) -> p a b", b=128)
                Vi = banks[vb1][:, :].rearrange("p (a b) -> p a b", b=128)
                prev = ssm_state["dve"]
                B0, B1, B2, B3, Sr_, Si_ = Wr, Wi, Sr, Si, t1a, t2a
                dv = lambda fn, deps: P.op("dve", fn, deps)
                zr_ = Zin_r[:, 4 * ct:4 * ct + 4]
                zi_ = Zin_i[:, 4 * ct:4 * ct + 4]
                er_ = e8r[:, 4 * ct:4 * ct + 4]
                ei_ = e8i[:, 4 * ct:4 * ct + 4]
                a1 = dv(lambda E: E.tensor_tensor(B0, Vr, cs, ALU.mult), [t_v, prev])
                a2 = dv(lambda E: E.tensor_tensor(B1, Vi, sn, ALU.mult), [t_v, prev])
                a3 = dv(lambda E: E.tensor_tensor(B2, Vi, cs, ALU.mult), [t_v, prev])
                a4 = dv(lambda E: E.tensor_tensor(B3, Vr, sn, ALU.mult), [t_v, prev])
                bfree[vb0] = [a1, a2, a3, a4]
                bfree[vb1] = [a1, a2, a3, a4]
                w1 = dv(lambda E: E.tensor_tensor(w0[:, 2, :], er_, zr_, ALU.mult), [prev])
                w2 = dv(lambda E: E.tensor_tensor(w0[:, 3, :], ei_, zi_, ALU.mult), [prev])
                w3 = dv(lambda E: E.tensor_tensor(w0[:, 4, :], er_, zi_, ALU.mult), [prev])
                w4 = dv(lambda E: E.tensor_tensor(w0[:, 5, :], ei_, zr_, ALU.mult), [prev])
                a5 = dv(lambda E: E.tensor_tensor(B0, B0, B1, ALU.add), [a1, a2])
                a6 = dv(lambda E: E.tensor_tensor(B2, B2, B3, ALU.subtract), [a3, a4])
                w5 = dv(lambda E: E.tensor_tensor(w0[:, 0, :], w0[:, 2, :], w0[:, 3, :], ALU.subtract), [w1, w2])
                w6 = dv(lambda E: E.tensor_tensor(w0[:, 1, :], w0[:, 4, :], w0[:, 5, :], ALU.add), [w3, w4])
                sr_t, si_t = [], []
                for j in range(4):
                    pr = 4 * ct + j
                    d0 = rho8[:, pr:pr + 1].to_broadcast([128, 128])
                    sr_t.append(dv(lambda E, j=j, d0=d0: E.tensor_tensor_scan(Sr_[:, j, :], d0, B0[:, j, :], w0[:, 0, j:j + 1], ALU.mult, ALU.add), [a5, w5, prev]))
                    si_t.append(dv(lambda E, j=j, d0=d0: E.tensor_tensor_scan(Si_[:, j, :], d0, B2[:, j, :], w0[:, 1, j:j + 1], ALU.mult, ALU.add), [a6, w6, prev]))
                c0 = []
                if own:
                    c0.append(dv(lambda E: E.tensor_copy(Zpr[:, :, 0:1], zr_.unsqueeze(2)), [ssm_state["zp_free"], prev]))
                    c0.append(dv(lambda E: E.tensor_copy(Zpi[:, :, 0:1], zi_.unsqueeze(2)), [ssm_state["zp_free"], prev]))
                c127 = cs[:, :, 127:128]
                s127 = sn[:, :, 127:128]
                z1 = dv(lambda E: E.tensor_tensor(w0[:, 2, :].unsqueeze(2), c127, Sr_[:, :, 127:128], ALU.mult), [sr_t, w5])
                z2 = dv(lambda E: E.tensor_tensor(w0[:, 3, :].unsqueeze(2), s127, Si_[:, :, 127:128], ALU.mult), [si_t, w5])
                z3 = dv(lambda E: E.tensor_tensor(w0[:, 4, :].unsqueeze(2), s127, Sr_[:, :, 127:128], ALU.mult), [sr_t, w6])
                z4 = dv(lambda E: E.tensor_tensor(w0[:, 5, :].unsqueeze(2), c127, Si_[:, :, 127:128], ALU.mult), [si_t, w6])
                z5 = dv(lambda E: E.tensor_tensor(zr_, w0[:, 2, :], w0[:, 3, :], ALU.subtract), [z1, z2, w1, w4, c0])
                z6 = dv(lambda E: E.tensor_tensor(zi_, w0[:, 4, :], w0[:, 5, :], ALU.add), [z3, z4, w2, w3, c0])
                all_t = [a1, a2, a3, a4, w1, w2, w3, w4, a5, a6, w5, w6, sr_t, si_t, c0, z1, z2, z3, z4, z5, z6]
                if own:
                    cs7 = cs[:, :, 0:127]; sn7 = sn[:, :, 0:127]
                    d1 = dv(lambda E: E.tensor_tensor(B1[:, :, 0:127], cs7, Sr_[:, :, 0:127], ALU.mult), [sr_t, a5])
                    d2 = dv(lambda E: E.tensor_tensor(B3[:, :, 0:127], sn7, Si_[:, :, 0:127], ALU.mult), [si_t, a6])
                    d3 = dv(lambda E: E.tensor_tensor(Zpr[:, :, 1:128], B1[:, :, 0:127], B3[:, :, 0:127], ALU.subtract), [d1, d2, ssm_state["zp_free"]])
                    d4 = dv(lambda E: E.tensor_tensor(B1[:, :, 0:127], sn7, Sr_[:, :, 0:127], ALU.mult), [d3])
                    d5 = dv(lambda E: E.tensor_tensor(B3[:, :, 0:127], cs7, Si_[:, :, 0:127], ALU.mult), [d3])
                    d6 = dv(lambda E: E.tensor_tensor(Zpi[:, :, 1:128], B1[:, :, 0:127], B3[:, :, 0:127], ALU.add), [d4, d5])
                    t_zp = [d3, d6, c0]
                    all_t += [d1, d2, d3, d4, d5, d6]

                class _L:
                    last = all_t
                Sq = _L()
                if own:
                    t_y = [None, None]
                    for hb in range(2):
                        b = ybase + hb
                        first = True
                        for j in range(4 * hb, 4 * hb + 4):
                            jc = (j % 4) * 128
                            for i in range(j + 1):
                                P.op("pe", lambda E, b=b, j=j, i=i, ct=ct, jc=jc, first=first: E.matmul(banks[b][:, jc:jc + 128], H[:, j - i, ct, :], udi[:, i * 128:(i + 1) * 128],
                                                                                                 start=first, stop=False, skip_group_check=True),
                                     [u_ready, bfree[b], t_zp, tm] if first else [], inc=False)
                                first = False
                            for pl in range(4):
                                for part in range(2):
                                    zp = Zpr if part == 0 else Zpi
                                    lastm = (j == 4 * hb + 3 and pl == 3 and part == 1)
                                    t_ = P.op("pe", lambda E, b=b, j=j, pl=pl, part=part, zp=zp, ct=ct, jc=jc: E.matmul(
                                        banks[b][32 * pl:32 * pl + 32, jc:jc + 128], Fm[:, j, part, (4 * ct + pl) * 32:(4 * ct + pl) * 32 + 32], zp[:, pl, :],
                                        start=False, stop=(pl == 3 and part == 1), tile_position=(0, 32 * pl), skip_group_check=True), [], inc=lastm)
                                    if lastm:
                                        t_y[hb] = t_
                    ssm_state["zp_free"] = t_y[1]
                    readers.append(t_y[1])
                    yi = ssm_state.get("yi", 0)
                    ssm_state["yi"] = yi + 1
                    yrow = yring[yi % 2]
                    t_fin = []
                    for hb in range(2):
                        b = ybase + hb
                        yb = banks[b][:, :]
                        dst = yrow.rearrange("p (k j) -> p j k", j=8)[:, 4 * hb:4 * hb + 4, :]
                        G1 = gtmp[0]; G2 = gtmp[1]
                        ta1 = P.op("act", lambda E, yb=yb, G1=G1: E.activation(G1, yb, AF.Square), [t_y[hb], ssm_state.get("g_free")])
                        tb1 = P.op("dve", lambda E, G1=G1: E.tensor_scalar(G1, G1, 0.044715, 1.0, ALU.mult, ALU.add), [ta1, Sq.last])
                        tb2 = P.op("dve", lambda E, G1=G1, yb=yb: E.tensor_tensor(G1, G1, yb, ALU.mult), [tb1])
                        ta2 = P.op("act", lambda E, G1=G1, G2=G2: E.activation(G2, G1, AF.Sigmoid, scale=1.5957691216057308), [tb2])
                        tb3 = P.op("dve", lambda E, G2=G2, yb=yb, dst=dst: E.tensor_tensor(dst, yb.rearrange("p (j k) -> p j k", k=128), G2.rearrange("p (j k) -> p j k", k=128), ALU.mult),
                                   [ta2, yfree[yi % 2]])
                        bfree[b] = tb3
                        ssm_state["g_free"] = tb3
                        Sq.last = [tb3]
                        t_fin.append(tb3)
                    yfree[yi % 2] = P.dma("sp", yg_s[ct * 128:(ct + 1) * 128, :], yrow, [t_fin])
                else:
                    readers.append(t_v)
                ssm_state["dve"] = Sq.last

            for ct_ in range(8):
                do_ct(ct_)
                if side and ct_ in side:
                    side[ct_]()
            return readers

        A.push()
        env = {"xring": [A.alloc([128, 512], F32) for _ in range(4)], "sqb": [A.alloc([128, 512], BF16) for _ in range(2)],
               "xg": A.alloc([128, KT, 512], BF16), "rs": A.alloc([128, 512], F32),
               "wring": [A.alloc([128, KT, 256], BF16) for _ in range(3)],
               "xi": 0, "wi": 0, "bi": 0, "xfree": [None] * 4, "sqfree": [None] * 2, "wfree": [[], [], []],
               "xg_free": None, "rs_free": None, "dst_free": None}
        env["done"] = {}
        uc_readers = [None, None]

        def proj_job(c, nts):
            def f():
                ucb = ucs[c % 2]
                env["dst_free"] = uc_readers[c % 2]
                norm_proj_chunk(c, 3072, 8, lambda m, nt, ucb=ucb: ucb[:, m, 512 * nt:512 * nt + 512], c >= 1, env, nts=nts)
            return f
        proj_job(0, (0, 1))()
        for c in range(NCH - 1):
            if KSIDE:
                side = {1: proj_job(c + 1, (0,)), 4: proj_job(c + 1, (1,))}
                uc_readers[c % 2] = ssm_chunk(c, False, env["done"][c], ucs[c % 2], side=side)
            else:
                uc_readers[c % 2] = ssm_chunk(c, False, env["done"][c], ucs[c % 2])
                proj_job(c + 1, (0, 1))()
        P.barrier()
        A.pop()
        if "u_own" in taps:
            tap("u_own", ucs[(NCH - 1) % 2].rearrange("p a b -> p (a b)"), [128, 8 * 1024])
        tap("zin", sm.rearrange("p a b -> p (a b)"), [128, 48 * 32])
        Fm = A.alloc([128, 8, 2, 1024], BF16)
        yring = [A.alloc([128, 1024], F32) for _ in range(2)]
        yfree = [None, None]
        gtmp = [A.alloc([128, 512], F32) for _ in range(2)]
        A.push()
        Ft1b = wblk[:, 0:2, :].rearrange("p a (b c) -> p (a b) c", c=32)
        Ft2b = wblk[:, 2:4, :].rearrange("p a (b c) -> p (a b) c", c=32)
        CPr2 = A.alloc([128, 32, 32], F32)
        CPni2 = A.alloc([128, 32, 32], F32)
        CPi02 = wblk[:, 4:6, :].rearrange("p a (b c) -> p (a b) c", c=32)
        t_cpr2 = P.dma("sp", CPr2, cpr_d.rearrange("p (a b) -> p a b", b=32))
        t_cpi2 = P.dma("sp", CPi02, cpi_d.rearrange("p (a b) -> p a b", b=32))
        SF = Seq(P, "dve", [t_cpr2, t_cpi2])
        SF.op(lambda E: E.memset(CPni2, 0.0))
        SF.op(lambda E: E.tensor_tensor(CPni2, CPni2, CPi02, ALU.subtract))
        for j in range(8):
            pr_ = bcast_last(apow[j + 1][0], 32)
            pi_ = bcast_last(apow[j + 1][1], 32)
            SF.op(lambda E, pr_=pr_: E.tensor_tensor(Ft1b, CPr2, pr_, ALU.mult))
            SF.op(lambda E, pi_=pi_: E.tensor_tensor(Ft2b, CPni2, pi_, ALU.mult))
            SF.op(lambda E, j=j: E.tensor_tensor(Fm[:, j, 0, :].rearrange("p (a b) -> p a b", b=32), Ft1b, Ft2b, ALU.add))
            SF.op(lambda E, pr_=pr_: E.tensor_tensor(Ft1b, CPni2, pr_, ALU.mult))
            SF.op(lambda E, pi_=pi_: E.tensor_tensor(Ft2b, CPr2, pi_, ALU.mult))
            SF.op(lambda E, j=j: E.tensor_tensor(Fm[:, j, 1, :].rearrange("p (a b) -> p a b", b=32), Ft1b, Ft2b, ALU.subtract))
        P.barrier()
        A.pop()
        ssm_chunk(NCH - 1, True, P.all_tokens(), ucs[(NCH - 1) % 2], Fm=Fm)
        P.barrier()
        A.pop()

        def make_wring(n):
            return {"slots": [A.alloc([128, 16, 256], BF16) for _ in range(n)], "free": [[] for _ in range(n)], "i": 0}

        def linear(w_view, KTn, m_tiles, rhs_fn, evac_fn, wr, bank_sets, rhs_deps):
            nkg = KTn // 16
            for pi_ in range(0, len(m_tiles), 2):
                mp = m_tiles[pi_:pi_ + 2]
                bset = bank_sets[(pi_ // 2) % len(bank_sets)]
                lastmm = {}
                for g in range(nkg):
                    wi_ = wr["i"]; wr["i"] = (wi_ + 1) % len(wr["slots"])
                    wsl = wr["slots"][wi_]
                    t_w = P.dma("pool", wsl[:, :, 0:128 * len(mp)], w_view[:, g * 16:(g + 1) * 16, mp[0] * 128:mp[0] * 128 + 128 * len(mp)], [wr["free"][wi_]])
                    wr["free"][wi_] = []
                    for mi in range(len(mp)):
                        for nt in range(2):
                            b = bset[mi * 2 + nt]
                            t_mm = None
                            for kl in range(16):
                                kt = g * 16 + kl
                                firstb = (g == 0 and kl == 0)
                                t_mm = P.op("pe", lambda E, b=b, wsl=wsl, kl=kl, mi=mi, kt=kt, nt=nt, firstb=firstb, g=g: E.matmul(
                                    banks[b][:, :], wsl[:, kl, mi * 128:mi * 128 + 128], rhs_fn(kt, nt), start=firstb, stop=(g == nkg - 1 and kl == 15)),
                                    ([t_w, rhs_deps, bfree[b]] if firstb else ([t_w] if kl == 0 else [])), inc=(kl == 15))
                            lastmm[(mi, nt)] = t_mm
                            wr["free"][wi_].append(t_mm)
                for mi, m in enumerate(mp):
                    for nt in range(2):
                        b = bset[mi * 2 + nt]
                        bfree[b] = evac_fn(m, nt, b, lastmm[(mi, nt)])

        def stats_of(src_fn, ntiles, rs_out, nfeat, sqb_, deps, stat_banks=(6, 7)):
            return rms_stats(src_fn, ntiles, 1024, rs_out, nfeat, sqb_, list(stat_banks), deps)

        A.push()
        mixed = A.alloc([128, 16, 1024], BF16)
        A.push()
        yg = A.alloc([128, 8, 1024], F32)
        ygb = A.alloc([128, 8, 1024], BF16)
        wg = A.alloc([128, 8, 1024], BF16)
        gt2 = [A.alloc([128, 512], F32) for _ in range(2)]
        sqb2 = [A.alloc([128, 512], BF16) for _ in range(2)]
        rs2 = A.alloc([128, 1024], F32)
        t_wg = P.dma("pool", wg, w_glu.rearrange("(kt p) m -> p kt m", p=128))
        t_yb = []
        for ct in range(8):
            t_l = P.dma("sp", yg[:, ct, :], yg_s[ct * 128:(ct + 1) * 128, :])
            t_yb.append(P.op("act", lambda E, ct=ct: E.copy(ygb[:, ct, :], yg[:, ct, :]), [t_l]))
        tap("yg", yg.rearrange("p a b -> p (a b)"), [128, 8 * 1024])
        gfree = [None, None]
        glu_done = []
        for m in range(8):
            for nt in range(2):
                b = (m * 2 + nt) % 4
                t_mm = None
                for kt in range(8):
                    t_mm = P.op("pe", lambda E, b=b, kt=kt, m=m, nt=nt: E.matmul(banks[b][:, :], wg[:, kt, m * 128:(m + 1) * 128], ygb[:, kt, nt * 512:(nt + 1) * 512],
                                                                              start=(kt == 0), stop=(kt == 7)),
                                [t_wg, t_yb, bfree[b]] if kt == 0 else [], inc=(kt == 7))
                gi = (m * 2 + nt) % 2
                t_s = P.op("act", lambda E, b=b, gi=gi, m=m: E.activation(gt2[gi], banks[b][:, :], AF.Sigmoid, bias=gcol("b_glu", m)), [t_mm, gfree[gi]])
                bfree[b] = t_s
                t_g = P.op("dve", lambda E, gi=gi, m=m, nt=nt: E.tensor_tensor(yg[:, m, nt * 512:(nt + 1) * 512], yg[:, m, nt * 512:(nt + 1) * 512], gt2[gi], ALU.mult), [t_s])
                gfree[gi] = t_g
                glu_done.append(t_g)
        tap("ssm", yg.rearrange("p a b -> p (a b)"), [128, 8 * 1024])
        t_rs2 = stats_of(lambda i, nt: yg[:, i, nt * 512:(nt + 1) * 512], 8, rs2, 1024.0, sqb2, glu_done)
        for ct in range(8):
            for nt in range(2):
                P.op("dve", lambda E, ct=ct, nt=nt: E.scalar_tensor_tensor(mixed[:, 8 + ct, nt * 512:(nt + 1) * 512], yg[:, ct, nt * 512:(nt + 1) * 512],
                                                                          gcol("ssm_n", ct), rs2[:, nt * 512:(nt + 1) * 512], ALU.mult, ALU.mult), [t_rs2])
        P.barrier()
        A.pop()

        A.push()
        attnT = A.alloc([128, 8, 1024], F32)
        head_state = {"gi": 0, "vtok_free": None, "rec_free": None}
        SCALE = 1.0 / math.sqrt(128.0)
        xg_sv = xg_s.rearrange("(kt p) t -> p kt t", p=128)
        for hg in range(2):
            A.push()
            KTt = A.alloc([128, 4, 3072], BF16)
            VTt = A.alloc([128, 4, 3072], BF16)
            QTt = A.alloc([128, 4, 1024], BF16)
            A.push()
            xg3 = A.alloc([128, 16, 1024], BF16)
            rs3 = A.alloc([128, 1024], F32)
            wr3 = make_wring(3)
            for sc in range(3):
                t_x3 = P.dma("sp", xg3, xg_sv[:, :, 1024 * sc:1024 * sc + 1024], [P.all_tokens()] if (sc > 0 or hg > 0) else [])
                t_r3 = P.dma("sp", rs3, rs_s[:, 1024 * sc:1024 * sc + 1024], [P.all_tokens()] if (sc > 0 or hg > 0) else [])
                jobs = [(1024 + 512 * hg, KTt), (1024 + 512 * hg + 256, KTt), (2048 + 512 * hg, VTt), (2048 + 512 * hg + 256, VTt)]
                if sc == 2:
                    jobs += [(512 * hg, QTt), (512 * hg + 256, QTt)]
                for (wc0, dstT) in jobs:
                    hl0 = ((wc0 % 1024) - 512 * hg) // 128

                    def ev3(m, nt, b, t_mm, dstT=dstT, hl0=hl0, sc=sc):
                        hl = hl0 + m
                        if dstT is QTt:
                            dst = dstT[:, hl, nt * 512:(nt + 1) * 512]
                        else:
                            dst = dstT[:, hl, 1024 * sc + nt * 512:1024 * sc + (nt + 1) * 512]
                        return P.op("dve", lambda E, dst=dst, b=b, nt=nt: E.tensor_tensor(dst, banks[b][:, :], rs3[:, nt * 512:(nt + 1) * 512], ALU.mult), [t_mm, t_r3])
                    linear(w_in_v[:, :, wc0:wc0 + 256], 16, [0, 1], lambda kt, nt: xg3[:, kt, nt * 512:(nt + 1) * 512], ev3, wr3,
                           [[0, 1, 2, 3], [4, 5, 6, 7]], [t_x3])
            P.barrier()
            A.pop()
            if hg == 0:
                tap("KT0", KTt[:, 0, :], [128, 3072])
                tap("VT0", VTt[:, 0, :], [128, 3072])
                tap("QT0", QTt[:, 0, :], [128, 1024])
            A.push()
            Vtoks = [A.alloc([128, NVT + 3, 128], BF16) for _ in range(2)]
            NES = 6
            es = [A.alloc([128, 4, 128], BF16) for _ in range(NES)]
            pm = [A.alloc([128, 4, 128], BF16) for _ in range(NES)]
            rec = A.alloc([128, 512], F32)
            TB = banks[0][:, :].bitcast(BF16)
            es_free = [None] * NES
            pm_free = [None] * NES
            vtok_free = [None, None]
            for hl in range(4):
                h = 4 * hg + hl

                def do_head(hl, h):
                    Vtok = Vtoks[hl % 2]
                    vt_toks = []
                    for g0 in range(0, NVT, 8):
                        n = min(8, NVT - g0)
                        t_tr = None
                        for q in range(n):
                            d, r, m, nk = VT_LIST[g0 + q]
                            te0 = 2048 + r + d * m
                            t_tr = P.op("pe", lambda E, q=q, te0=te0, d=d, nk=nk: E.transpose(TB[0:nk, q * 128:(q + 1) * 128], VTt[:, hl, te0:te0 + d * (nk - 1) + 1:d], ident_b),
                                        [bfree[0], vtok_free[hl % 2]] if q == 0 else [], inc=(q == n - 1))
                        t_cp = P.op("act", lambda E, g0=g0, n=n: E.copy(Vtok[:, g0:g0 + n, :].rearrange("p a b -> p (a b)"), TB[:, 0:128 * n]), [t_tr])
                        bfree[0] = t_cp
                        vt_toks.append(t_cp)
                    tiles = []
                    for d in (1, 4, 16):
                        QB = min(128, NO // d)
                        for r in range(d):
                            for blk in range((NO // d) // QB):
                                m0 = blk * QB
                                for (mk, nk, mask) in ((m0 - 128, 128, mprev_b), (m0, QB, mcur_b)):
                                    vt = VT_IDX[(d, r, mk)] if (d, r, mk) in VT_IDX else None
                                    if vt is None:
                                        raise AssertionError((d, r, mk))
                                    outs = []
                                    if d == 16:
                                        outs = [(0, r, 16, 0, 32), (1, r, 16, 32, 64)]
                                    elif d == 4:
                                        outs = [(blk, r, 4, 0, 128)]
                                    else:
                                        outs = [(blk // 4, (blk % 4) * 128, 1, 0, 128)]
                                    tiles.append(dict(d=d, r=r, m0=m0, mk=mk, nk=nk, QB=QB, mask=mask, vt=vt, outs=outs))
                    started = set()
                    tix = {(T["d"], T["r"], T["m0"], T["mk"]): T for T in tiles}
                    groups = []
                    for half in range(2):
                        for ab in range(2):
                            grp = [tix[(1, 0, 128 * blk, 128 * blk - 128 if ab == 0 else 128 * blk)] for blk in range(4 * half, 4 * half + 4)]
                            groups.append((grp, ("d1", half)))
                    for blk in range(2):
                        for ab in range(2):
                            grp = [tix[(4, r, 128 * blk, 128 * blk - 128 if ab == 0 else 128 * blk)] for r in range(4)]
                            groups.append((grp, ("d4", blk)))
                    for r0 in range(0, 16, 4):
                        for ab in range(2):
                            grp = [tix[(16, r, 0, -128 if ab == 0 else 0)] for r in range(r0, r0 + 4)]
                            groups.append((grp, ("d16", r0)))
                    assert sum(len(g[0]) for g in groups) == len(tiles)
                    for (grp, gkind) in groups:
                        gi = (head_state["gi"]) % NES
                        sb_ = (1, 2, 7)[head_state["gi"] % 3]
                        head_state["gi"] += 1
                        t_s = None
                        for q, T in enumerate(grp):
                            d, r = T["d"], T["r"]
                            k0 = 2048 + r + d * T["mk"]
                            q0 = r + d * T["m0"]
                            t_s = P.op("pe", lambda E, q=q, k0=k0, q0=q0, d=d, nk=T["nk"], QB=T["QB"], sb_=sb_: E.matmul(
                                banks[sb_][0:nk, q * 128:q * 128 + QB], KTt[:, hl, k0:k0 + d * (nk - 1) + 1:d], QTt[:, hl, q0:q0 + d * (QB - 1) + 1:d],
                                start=True, stop=True, skip_group_check=True), [bfree[sb_]] if q == 0 else [], inc=(q == len(grp) - 1))
                        ng = len(grp)
                        t_e = P.op("act", lambda E, gi=gi, sb_=sb_, ng=ng: E.activation(es[gi][:, 0:ng, :].rearrange("p a b -> p (a b)"), banks[sb_][:, 0:128 * ng], AF.Exp, scale=SCALE),
                                   [t_s, es_free[gi]])
                        bfree[sb_] = t_e
                        t_ms = []
                        for q, T in enumerate(grp):
                            nk, QB = T["nk"], T["QB"]
                            t_ms.append(P.op("dve", lambda E, gi=gi, q=q, nk=nk, QB=QB, mask=T["mask"], vt=T["vt"]: E.scalar_tensor_tensor(
                                pm[gi][0:nk, q, 0:QB], es[gi][0:nk, q, 0:QB], vcols[0:nk, vt:vt + 1], mask[0:nk, 0:QB], ALU.mult, ALU.mult),
                                [t_e, pm_free[gi]] if q == 0 else [t_e]))
                        es_free[gi] = t_ms
                        t_pv = None
                        for q, T in enumerate(grp):
                            nk = T["nk"]
                            for (half, off, step, c0, c1) in T["outs"]:
                                ncol = c1 - c0
                                ob = 3 + half
                                st = ob not in started
                                started.add(ob)
                                lhs = Vtok[0:nk, T["vt"], :]
                                t_pv = P.op("pe", lambda E, ob=ob, off=off, step=step, ncol=ncol, lhs=lhs, gi=gi, q=q, nk=nk, c0=c0, c1=c1, st=st: E.matmul(
                                    banks[ob][:, off:off + step * (ncol - 1) + 1:step], lhs, pm[gi][0:nk, q, c0:c1], start=st, stop=False, skip_group_check=True),
                                    [t_ms, vt_toks, bfree[ob]] if st else [t_ms[q]], inc=True)
                        nkg = grp[0]["nk"]
                        if gkind[0] == "d1":
                            dens = [(5 + gkind[1], banks[5 + gkind[1]][:, :], pm[gi][0:nkg].rearrange("p a b -> p (a b)"))]
                        elif gkind[0] == "d4":
                            dens = [(5 + gkind[1], banks[5 + gkind[1]][:, :].rearrange("p (i r) -> p r i", r=4)[:, :, 64 * hf:64 * hf + 64], pm[gi][0:nkg, :, 64 * hf:64 * hf + 64])
                                    for hf in range(2)]
                        else:
                            r0 = gkind[1]
                            dens = [(5 + hf, banks[5 + hf][:, :].rearrange("p (i r) -> p r i", r=16)[:, r0:r0 + 4, :], pm[gi][0:nkg, :, 32 * hf:32 * hf + 32]) for hf in range(2)]
                        for (ob, oap, rap) in dens:
                            st = ob not in started
                            started.add(ob)
                            t_pv = P.op("pe", lambda E, oap=oap, rap=rap, nkg=nkg, st=st: E.matmul(oap, ones_b[0:nkg, :], rap, start=st, stop=False, skip_group_check=True),
                                        [t_ms, bfree[ob]] if st else [t_ms], inc=True)
                        pm_free[gi] = t_pv
                    fin = []
                    for half in range(2):
                        t_r = P.op("dve", lambda E, half=half: E.reciprocal(rec, banks[5 + half][:, :]), [t_pv, head_state["rec_free"]])
                        t_f = P.op("dve", lambda E, half=half: E.tensor_tensor(attnT[:, h, half * 512:(half + 1) * 512], banks[3 + half][:, :], rec, ALU.mult), [t_r])
                        head_state["rec_free"] = t_f
                        bfree[5 + half] = t_r
                        bfree[3 + half] = t_f
                        fin.append(t_f)
                    vtok_free[hl % 2] = t_pv
                do_head(hl, h)
            P.barrier()
            A.pop()
            A.pop()
        tap("attn", attnT.rearrange("p a b -> p (a b)"), [128, 8 * 1024])
        A.push()
        sqb4 = [A.alloc([128, 512], BF16) for _ in range(2)]
        rs4 = A.alloc([128, 1024], F32)
        t_rs4 = stats_of(lambda i, nt: attnT[:, i, nt * 512:(nt + 1) * 512], 8, rs4, 1024.0, sqb4, [])
        for hh in range(8):
            for nt in range(2):
                P.op("dve", lambda E, hh=hh, nt=nt: E.scalar_tensor_tensor(mixed[:, hh, nt * 512:(nt + 1) * 512], attnT[:, hh, nt * 512:(nt + 1) * 512],
                                                                          gcol("attn_n", hh), rs4[:, nt * 512:(nt + 1) * 512], ALU.mult, ALU.mult), [t_rs4])
        P.barrier()
        A.pop()
        A.pop()
        tap("mixed", mixed.rearrange("p a b -> p (a b)"), [128, 16 * 1024])

        def residual_pass(src_load, base_load, gname, rs_, dst_store, tmp_ring, deps, inplace_src=None):
            fr = [None] * len(tmp_ring)
            k = 0
            outs = []
            for m in range(16):
                for nt in range(2):
                    a_, b_ = tmp_ring[k % len(tmp_ring)]
                    if inplace_src is not None:
                        a_ = inplace_src(m, nt)
                        t_a = None
                    else:
                        t_a = src_load(m, nt, a_, [fr[k % len(tmp_ring)], deps])
                    t_b = base_load(m, nt, b_, [fr[k % len(tmp_ring)], deps])
                    t1_ = P.op("dve", lambda E, a_=a_, m=m, nt=nt: E.scalar_tensor_tensor(a_, a_, gcol(gname, m), rs_[:, nt * 512:(nt + 1) * 512], ALU.mult, ALU.mult), [t_a, deps])
                    t2_ = P.op("dve", lambda E, a_=a_, b_=b_: E.tensor_tensor(b_, b_, a_, ALU.add), [t1_, t_b])
                    t3_ = dst_store(m, nt, b_, [t2_])
                    fr[k % len(tmp_ring)] = t3_
                    outs.append(t3_)
                    k += 1
            return outs

        def load_full(dst, src_dram, deps):
            toks = []
            for m in range(16):
                toks.append(P.dma("sp", dst[:, m, :], src_dram[m * 128:(m + 1) * 128, :], deps))
            return toks

        def prenorm(hT_, gname, hn_, sqb_, rs_, deps):
            t_rs = stats_of(lambda i, nt: hT_[:, i, nt * 512:(nt + 1) * 512], 16, rs_, float(D), sqb_, deps)
            toks = []
            for m in range(16):
                for nt in range(2):
                    toks.append(P.op("dve", lambda E, m=m, nt=nt: E.scalar_tensor_tensor(hn_[:, m, nt * 512:(nt + 1) * 512], hT_[:, m, nt * 512:(nt + 1) * 512],
                                                                                         gcol(gname, m), rs_[:, nt * 512:(nt + 1) * 512], ALU.mult, ALU.mult), [t_rs]))
            return toks

        A.push()
        mixT = A.alloc([128, 16, 1024], F32)
        wr4 = make_wring(3)
        sqb5 = [A.alloc([128, 512], BF16) for _ in range(2)]
        rs5 = A.alloc([128, 1024], F32)

        def ev4(m, nt, b, t_mm):
            return P.op("act", lambda E, m=m, nt=nt, b=b: E.copy(mixT[:, m, nt * 512:(nt + 1) * 512], banks[b][:, :]), [t_mm])
        linear(w_out.rearrange("(kt p) m -> p kt m", p=128), 16, list(range(16)), lambda kt, nt: mixed[:, kt, nt * 512:(nt + 1) * 512], ev4, wr4,
               [[0, 1, 2, 3], [4, 5, 6, 7]], [])
        P.barrier()
        t_rs5 = stats_of(lambda i, nt: mixT[:, i, nt * 512:(nt + 1) * 512], 16, rs5, float(D), sqb5, [])
        xoff = NE - NO
        xring4 = [A.alloc([128, 512], F32) for _ in range(4)]
        x4free = [None] * 4
        h_toks = []
        k4 = 0
        for m in range(16):
            for nt in range(2):
                sl = slice(nt * 512, (nt + 1) * 512)
                xb = xring4[k4 % 4]
                t_x = P.dma("sp", xb, xT[m * 128:(m + 1) * 128, xoff + nt * 512:xoff + (nt + 1) * 512], [x4free[k4 % 4]])
                t1_ = P.op("dve", lambda E, m=m, sl=sl: E.scalar_tensor_tensor(mixT[:, m, sl], mixT[:, m, sl], gcol("mix_post", m), rs5[:, sl], ALU.mult, ALU.mult), [t_rs5])
                t2_ = P.op("dve", lambda E, m=m, sl=sl, xb=xb: E.tensor_tensor(mixT[:, m, sl], mixT[:, m, sl], xb, ALU.add), [t1_, t_x])
                x4free[k4 % 4] = t2_
                P.dma("pool", h_s[m * 128:(m + 1) * 128, sl], mixT[:, m, sl], [t2_])
                h_toks.append(t2_)
                k4 += 1
        tap("h1", mixT.rearrange("p a b -> p (a b)"), [128, 16 * 1024])
        hn2 = mixed
        rs5b = A.alloc([128, 1024], F32)
        prenorm(mixT, "mlp_pre", hn2, sqb5, rs5b, lambda i, nt: h_toks[2 * i + nt])
        P.barrier()
        A.pop()

        act = A.alloc([128, 64, 1024], BF16)
        wr5 = make_wring(3)
        rtmp = [A.alloc([128, 512], F32) for _ in range(3)]
        rfree = [None] * 3
        rk = [0]

        def ev_up(m, nt, b, t_mm):
            i = rk[0] % 3
            rk[0] += 1
            t_r = P.op("act", lambda E, i=i, b=b: E.activation(rtmp[i], banks[b][:, :], AF.Relu), [t_mm, rfree[i]])
            t_q = P.op("dve", lambda E, i=i, m=m, nt=nt: E.tensor_tensor(act[:, m, nt * 512:(nt + 1) * 512], rtmp[i], rtmp[i], ALU.mult), [t_r])
            rfree[i] = t_q
            return t_r
        linear(w_up.rearrange("(kt p) m -> p kt m", p=128), 16, list(range(64)), lambda kt, nt: hn2[:, kt, nt * 512:(nt + 1) * 512], ev_up, wr5,
               [[0, 1, 2, 3], [4, 5, 6, 7]], [])
        P.barrier()

        def ev_dn(m, nt, b, t_mm):
            i = rk[0] % 3
            rk[0] += 1
            t_c = P.op("act", lambda E, i=i, b=b: E.copy(rtmp[i], banks[b][:, :]), [t_mm, rfree[i]])
            rfree[i] = P.dma("sp", ff_s[m * 128:(m + 1) * 128, nt * 512:(nt + 1) * 512], rtmp[i], [t_c])
            return t_c
        linear(w_down.rearrange("(kt p) m -> p kt m", p=128), 64, list(range(16)), lambda kt, nt: act[:, kt, nt * 512:(nt + 1) * 512], ev_dn, wr5,
               [[0, 1, 2, 3], [4, 5, 6, 7]], [])
        P.barrier()
        A.pop()

        A.push()
        h2 = A.alloc([128, 16, 1024], F32)
        prod = A.alloc([128, 16, 1024], F32)
        hn3 = A.alloc([128, 16, 1024], BF16)
        sqb7 = [A.alloc([128, 512], BF16) for _ in range(2)]
        rs7 = A.alloc([128, 1024], F32)
        t_ff = load_full(prod, ff_s, [])
        t_h2 = load_full(h2, h_s, [])
        t_rs7 = stats_of(lambda i, nt: prod[:, i, nt * 512:(nt + 1) * 512], 16, rs7, float(D), sqb7, lambda i, nt: t_ff[i])
        res_t = []
        for m in range(16):
            for nt in range(2):
                sl = slice(nt * 512, (nt + 1) * 512)
                t1_ = P.op("dve", lambda E, m=m, sl=sl: E.scalar_tensor_tensor(prod[:, m, sl], prod[:, m, sl], gcol("mlp_post", m), rs7[:, sl], ALU.mult, ALU.mult), [t_rs7])
                res_t.append(P.op("dve", lambda E, m=m, sl=sl: E.tensor_tensor(h2[:, m, sl], h2[:, m, sl], prod[:, m, sl], ALU.add), [t1_, t_h2]))
        P.barrier()
        tap("h2", h2.rearrange("p a b -> p (a b)"), [128, 16 * 1024])
        rs7b = A.alloc([128, 1024], F32)
        prenorm(h2, "ple_pre", hn3, sqb7, rs7b, lambda i, nt: res_t[2 * i + nt])
        P.barrier()
        wr6 = make_wring(2)
        wpp = A.alloc([128, 2, D], BF16)
        pTb = A.alloc([128, 2, 1024], BF16)
        gt6 = [A.alloc([128, 512], F32) for _ in range(2)]
        g6free = [None, None]
        g6k = [0]
        t_wpp = P.dma("pool", wpp, w_pp.rearrange("(kt p) m -> p kt m", p=128))
        t_pT = P.dma("pool", pTb, pT.rearrange("(kt p) t -> p kt t", p=128))

        prod_t = {}

        def ev6(m, nt, b, t_mm):
            i = g6k[0] % 2
            g6k[0] += 1
            eb = 4 + (b % 4)
            P.op("pe", lambda E, eb=eb, m=m, nt=nt: E.matmul(banks[eb][:, :], wpp[:, 0, m * 128:(m + 1) * 128], pTb[:, 0, nt * 512:(nt + 1) * 512], start=True, stop=False),
                 [t_wpp, t_pT, bfree[eb]], inc=False)
            t_e = P.op("pe", lambda E, eb=eb, m=m, nt=nt: E.matmul(banks[eb][:, :], wpp[:, 1, m * 128:(m + 1) * 128], pTb[:, 1, nt * 512:(nt + 1) * 512], start=False, stop=True), [])
            t_s = P.op("act", lambda E, i=i, b=b: E.activation(gt6[i], banks[b][:, :], AF.Sigmoid), [t_mm, g6free[i]])
            t_p = P.op("dve", lambda E, i=i, eb=eb, m=m, nt=nt: E.tensor_tensor(prod[:, m, nt * 512:(nt + 1) * 512], gt6[i], banks[eb][:, :], ALU.mult), [t_s, t_e])
            g6free[i] = t_p
            bfree[eb] = t_p
            prod_t[(m, nt)] = t_p
            return t_s
        linear(w_pg.rearrange("(kt p) m -> p kt m", p=128), 16, list(range(16)), lambda kt, nt: hn3[:, kt, nt * 512:(nt + 1) * 512], ev6, wr6,
               [[0, 1, 2, 3]], [])
        P.barrier()
        t_rs8 = stats_of(lambda i, nt: prod[:, i, nt * 512:(nt + 1) * 512], 16, rs7, float(D), sqb7, [])
        for m in range(16):
            for nt in range(2):
                sl = slice(nt * 512, (nt + 1) * 512)
                t1_ = P.op("dve", lambda E, m=m, sl=sl: E.scalar_tensor_tensor(prod[:, m, sl], prod[:, m, sl], gcol("ple_post", m), rs7[:, sl], ALU.mult, ALU.mult), [t_rs8])
                t2_ = P.op("dve", lambda E, m=m, sl=sl: E.tensor_tensor(h2[:, m, sl], h2[:, m, sl], prod[:, m, sl], ALU.add), [t1_])
                P.dma("sp", out_d[m * 128:(m + 1) * 128, sl], h2[:, m, sl], [t2_])
        P.barrier()
        A.pop()
        P.emit()
    return nc, tap_out


def col_layout(v):
    v = np.asarray(v, np.float32).reshape(-1)
    return v.reshape(-1, 128).T


def prep_shared(inp):
    sh = {}
    colsv = [inp["mix_norm_pre"][0], inp["attn_out_norm"][0], inp["ssm_out_norm"][0], inp["mix_norm_post"][0],
             inp["mlp_norm_pre"][0], inp["mlp_norm_post"][0], inp["ple_norm_pre"][0], inp["ple_norm_post"][0],
             inp["ssm_d"][0], inp["b_glu"][0]]
    sh["cols"] = np.ascontiguousarray(np.concatenate([col_layout(v) for v in colsv], axis=1))

    def st(v):
        v = np.asarray(v, np.float32).reshape(32, 2, 64)
        return v.transpose(1, 2, 0).reshape(128, 32)
    ldt = np.broadcast_to(np.asarray(inp["log_dt"][0], np.float32)[:, None], (64, 64))
    sh["sp"] = np.ascontiguousarray(np.concatenate([st(inp["lam_re"][0]), st(inp["lam_im"][0]), st(ldt)], axis=1))

    def padB(B):
        B = np.asarray(B, np.float32).reshape(32, 2, 64, 16)
        o = np.zeros((2, 64, 32, 2, 16), np.float32)
        for gl in range(2):
            o[gl, :, :, gl, :] = B[:, gl].transpose(1, 0, 2)
        return o.reshape(128, 1024)

    def padC(C):
        C = np.asarray(C, np.float32).reshape(32, 2, 16, 64)
        o = np.zeros((2, 64, 32, 2, 16), np.float32)
        for gl in range(2):
            o[gl, :, :, gl, :] = C[:, gl].transpose(2, 0, 1)
        return o.reshape(128, 1024)
    sh["bpr"] = padB(inp["ssm_b_re"][0])
    sh["bpi"] = padB(inp["ssm_b_im"][0])
    sh["cpr"] = padC(inp["ssm_c_re"][0])
    sh["cpi"] = padC(inp["ssm_c_im"][0])
    for k_, n_ in (("w_in", "w_in"), ("w_glu", "w_glu"), ("w_out", "w_out"), ("w_up", "w_up"), ("w_down", "w_down"),
                   ("w_ple_gate", "w_pg"), ("w_ple_proj", "w_pp")):
        sh[n_] = np.ascontiguousarray(np.asarray(inp[k_][0], np.float32))
    return sh


def consts_for_core(j):
    kk = np.arange(128)[:, None]
    ii = np.arange(128)[None, :]
    ident = (kk == ii).astype(np.float32)
    mprev = (kk >= ii).astype(np.float32)
    mcur = (kk <= ii).astype(np.float32)
    bmask = ((kk // 16) == (ii // 16)).astype(np.float32)
    pmask = ((kk // 32) == np.arange(4)[None, :]).astype(np.float32)
    T0 = 1024 * j
    vc = np.zeros((128, NVT), np.float32)
    for i, (d, r, m, nk) in enumerate(VT_LIST):
        t_abs = T0 + r + d * (m + np.arange(nk))
        vc[:nk, i] = (t_abs >= 0).astype(np.float32)
    return np.ascontiguousarray(np.concatenate([ident, mprev, mcur, bmask, pmask, vc], axis=1))


def prep_core(c, inp, sh):
    b, j = c // 4, c % 4
    T0 = 1024 * j
    x = np.asarray(inp["x"], np.float32)
    xe = np.zeros((NE, D), np.float32)
    lo = T0 - (NE - NO)
    s0 = max(lo, 0)
    xe[s0 - lo:, :] = x[b, s0:T0 + NO, :]
    m = dict(sh)
    m["xT"] = np.ascontiguousarray(xe.T)
    m["pT"] = np.ascontiguousarray(np.asarray(inp["p"], np.float32)[0, b, T0:T0 + NO, :].T)
    m["consts"] = consts_for_core(j)
    return m


_CACHE = {}


def kernel(**inputs):
    if "nc" not in _CACHE:
        _CACHE["nc"] = build_nc()[0]
    nc = _CACHE["nc"]
    sh = prep_shared(inputs)
    in_maps = [prep_core(c, inputs, sh) for c in range(8)]
    res = run_bass_kernel_spmd(nc, in_maps, core_ids=list(range(8)))
    out = np.zeros((2, 4096, D), np.float32)
    for c in range(8):
        b, j = c // 4, c % 4
        out[b, 1024 * j:1024 * (j + 1), :] = res.results[c]["out"].T
    return out
```

```python
import math
import os
KSTOP = int(os.environ.get('KSTOP', '0'))
KNG = int(os.environ.get('KNG', '8'))
KNH = int(os.environ.get('KNH', '8'))
KNOF = int(os.environ.get('KNOF', '0'))
KSIDE = int(os.environ.get('KSIDE', '1'))
from contextlib import ExitStack

import numpy as np
import concourse.bass as bass
import concourse.mybir as mybir
from concourse.bass_utils import run_bass_kernel_spmd

F32 = mybir.dt.float32
BF16 = mybir.dt.bfloat16
ALU = mybir.AluOpType
AF = mybir.ActivationFunctionType

D = 2048
KT = 16
NE = 4096
NO = 1024
NCH = 4
DFF = 8192
EPS = 1e-6
MAGIC = 12582912.0
TWO_PI = 2.0 * math.pi

COLS = [("mix_pre", 16), ("attn_n", 8), ("ssm_n", 8), ("mix_post", 16), ("mlp_pre", 16),
        ("mlp_post", 16), ("ple_pre", 16), ("ple_post", 16), ("ssm_d", 8), ("b_glu", 8)]
COL_OFF = {}
_o = 0
for _n, _c in COLS:
    COL_OFF[_n] = _o
    _o += _c
NCOLS = _o


def vtile_list():
    tiles = []
    for d in (1, 4, 16):
        M = NO // d
        for r in range(d):
            m = -128
            while m < M:
                nk = min(128, M - m)
                tiles.append((d, r, m, nk))
                m += 128
    return tiles


VT_LIST = vtile_list()
VT_IDX = {(d, r, m): i for i, (d, r, m, nk) in enumerate(VT_LIST)}
NVT = len(VT_LIST)


class Prog:
    ENG = ("pe", "act", "dve", "pool", "sp")

    def __init__(self, nc, stack, n_dma_sems=32):
        self.nc = nc
        self.q = {e: [] for e in self.ENG}
        self.cnt = {e: 0 for e in self.ENG}
        self.sem = {e: stack.enter_context(nc.semaphore("s_" + e)) for e in self.ENG}
        self.waited = {}
        n_sw = 16
        self.dsem = [stack.enter_context(nc.semaphore("d%d" % i)) for i in range(n_dma_sems + n_sw)]
        self.dcnt = [0] * (n_dma_sems + n_sw)
        self.drange = {"sp": (0, n_dma_sems), "pool": (n_dma_sems, n_dma_sems + n_sw)}
        self.dnext = {"sp": 0, "pool": n_dma_sems}
        self.ninst = 0

    def _waits(self, eng, deps):
        for d in deps:
            if d is None:
                continue
            if isinstance(d, list) or (isinstance(d, tuple) and len(d) and not isinstance(d[0], str)):
                self._waits(eng, d)
                continue
            kind, key, val = d
            wk = (eng, kind, key)
            if self.waited.get(wk, 0) >= val:
                continue
            self.waited[wk] = val
            sem = self.sem[key] if kind == "e" else self.dsem[key]
            self.q[eng].append(lambda E, sem=sem, val=val: E.wait_ge(sem, val))
            self.ninst += 1

    def op(self, eng, fn, deps=(), inc=True):
        self._waits(eng, deps)
        self.ninst += 1
        if inc:
            self.cnt[eng] += 1
            v = self.cnt[eng]
            sem = self.sem[eng]
            self.q[eng].append(lambda E, fn=fn, sem=sem: fn(E).then_inc(sem, 1))
            return ("e", eng, v)
        self.q[eng].append(lambda E, fn=fn: fn(E))
        return None

    def dma(self, eng, out, in_, deps=(), **kw):
        lo, hi = self.drange[eng]
        i = self.dnext[eng]
        self.dnext[eng] = lo + (i + 1 - lo) % (hi - lo)
        if self.dcnt[i] > 0:
            self._waits(eng, [("d", i, self.dcnt[i])])
        self._waits(eng, deps)
        self.dcnt[i] += 16
        sem = self.dsem[i]
        self.ninst += 1
        self.q[eng].append(lambda E, out=out, in_=in_, sem=sem, kw=kw: E.dma_start(out=out, in_=in_, **kw).then_inc(sem, 16))
        return ("d", i, self.dcnt[i])

    def all_tokens(self):
        toks = [("e", e, self.cnt[e]) for e in self.ENG if self.cnt[e] > 0]
        toks += [("d", i, self.dcnt[i]) for i in range(len(self.dsem)) if self.dcnt[i] > 0]
        return toks

    def barrier(self):
        toks = self.all_tokens()
        for e in self.ENG:
            self._waits(e, toks)

    def emit(self):
        nc = self.nc
        self._waits("sp", self.all_tokens())
        with nc.Block() as block:
            @block.tensor
            def _(E):
                for f in self.q["pe"]:
                    f(E)

            @block.scalar
            def _(E):
                for f in self.q["act"]:
                    f(E)

            @block.vector
            def _(E):
                for f in self.q["dve"]:
                    f(E)

            @block.gpsimd
            def _(E):
                for f in self.q["pool"]:
                    f(E)

            @block.sync
            def _(E):
                for f in self.q["sp"]:
                    f(E)


class Seq:
    def __init__(self, P, eng, deps=()):
        self.P = P
        self.eng = eng
        self.last = list(deps)

    def op(self, fn, deps=(), eng=None):
        e = eng or self.eng
        t = self.P.op(e, fn, [self.last, list(deps)])
        self.last = [t]
        return t

    def par(self, fns):
        toks = [self.P.op(self.eng, fn, [self.last]) for fn in fns]
        self.last = toks
        return toks


class Arena:
    def __init__(self, big, total):
        self.big = big
        self.total = total
        self.top = 0
        self.marks = []
        self.peak = 0

    def push(self):
        self.marks.append(self.top)

    def pop(self):
        self.top = self.marks.pop()

    def alloc(self, shape, dt):
        n = 1
        for s in shape[1:]:
            n *= s
        es = 2 if dt == BF16 else 4
        nbytes = (n * es + 63) // 64 * 64
        off = self.top
        self.top += nbytes
        self.peak = max(self.peak, self.top)
        assert self.top <= self.total, ("arena overflow", self.top, self.total)
        ap = self.big[:, off // 4:(off + nbytes) // 4]
        if dt == BF16:
            ap = ap.bitcast(BF16)
        ap = ap[:, 0:n]
        fs = shape[1:]
        if len(fs) == 2:
            ap = ap.rearrange("p (a b) -> p a b", b=fs[1])
        elif len(fs) == 3:
            ap = ap.rearrange("p (a b c) -> p a b c", b=fs[1], c=fs[2])
        elif len(fs) == 4:
            ap = ap.rearrange("p (a b c d) -> p a b c d", b=fs[1], c=fs[2], d=fs[3])
        if shape[0] < 128:
            ap = ap[0:shape[0]]
        return ap


def bcast_last(ap2, n):
    return ap2.unsqueeze(2).to_broadcast([ap2.shape[0], ap2.shape[1], n])


def build_nc(taps=()):
    nc = bass.Bass("TRN2", target_bir_lowering=False)
    dr = lambda name, shape, dt=F32: nc.dram_tensor(name, shape, dt, kind="ExternalInput").ap()
    xT = dr("xT", [D, NE])
    pT = dr("pT", [256, NO])
    cols_d = dr("cols", [128, NCOLS])
    sp_d = dr("sp", [128, 96])
    bpr_d = dr("bpr", [128, 1024])
    bpi_d = dr("bpi", [128, 1024])
    cpr_d = dr("cpr", [128, 1024])
    cpi_d = dr("cpi", [128, 1024])
    consts_d = dr("consts", [128, 4 * 128 + 4 + NVT])
    w_in = dr("w_in", [D, 4096])
    w_glu = dr("w_glu", [1024, 1024])
    w_out = dr("w_out", [D, D])
    w_up = dr("w_up", [D, DFF])
    w_down = dr("w_down", [DFF, D])
    w_pg = dr("w_pg", [D, D])
    w_pp = dr("w_pp", [256, D])
    out_d = nc.dram_tensor("out", [D, NO], F32, kind="ExternalOutput").ap()
    xg_s = nc.dram_tensor("xg_s", [D, 3072], BF16).ap()
    rs_s = nc.dram_tensor("rs_s", [128, 3072], F32).ap()
    h_s = nc.dram_tensor("h_s", [D, NO], F32).ap()
    ff_s = nc.dram_tensor("ff_s", [D, NO], F32).ap()
    yg_s = nc.dram_tensor("yg_s", [1024, NO], F32).ap()
    tap_out = {}

    w_in_v = w_in.rearrange("(kt p) m -> p kt m", p=128)

    with ExitStack() as top:
        P = Prog(nc, top)
        TOTAL = 207 * 1024
        big = top.enter_context(nc.sbuf_tensor("big", [128, TOTAL // 4], F32))
        A = Arena(big, TOTAL)
        banks = [top.enter_context(nc.psum_tensor("bank%d" % i, [128, 512], F32)) for i in range(8)]
        bfree = [None] * 8

        def tap(name, ap, shape):
            if name not in taps:
                return
            P.barrier()
            t = nc.dram_tensor("tap_" + name, list(shape), ap.dtype, kind="ExternalOutput").ap()
            tap_out[name] = t
            P.dma("sp", t, ap)
            P.barrier()

        cols = A.alloc([128, NCOLS], F32)
        cst = A.alloc([128, 4 * 128 + 4 + NVT], F32)
        ident_f = cst[:, 0:128]
        mprev_f = cst[:, 128:256]
        mcur_f = cst[:, 256:384]
        bmask = cst[:, 384:512]
        pmask = cst[:, 512:516]
        vcols = cst[:, 516:516 + NVT]
        ident_b = A.alloc([128, 128], BF16)
        ones_b = A.alloc([128, 128], BF16)
        mprev_b = A.alloc([128, 128], BF16)
        mcur_b = A.alloc([128, 128], BF16)
        epsc = A.alloc([128, 1], F32)
        t_cols = P.dma("sp", cols, cols_d)
        t_cst = P.dma("sp", cst, consts_d)
        t0 = P.op("dve", lambda E: E.tensor_copy(ident_b, ident_f), [t_cst])
        t1 = P.op("dve", lambda E: E.tensor_copy(mprev_b, mprev_f), [t_cst])
        t2 = P.op("dve", lambda E: E.tensor_copy(mcur_b, mcur_f), [t_cst])
        t3 = P.op("pool", lambda E: E.memset(ones_b, 1.0))
        t4 = P.op("pool", lambda E: E.memset(epsc, EPS))
        P.barrier()

        if KSTOP == 3:
            tap("cst", cst, [128, 4 * 128 + 4 + NVT])
            P.emit()
            return nc, tap_out

        def gcol(name, i):
            o = COL_OFF[name] + i
            return cols[:, o:o + 1]

        def rms_stats(src_fn, ntiles, ntok, rs_out, nfeat, sqbufs, stat_banks, deps):
            toks = []
            nnt = ntok // 512
            sqfree = [None] * len(sqbufs)
            j = 0
            last_mm = [None] * nnt
            for i in range(ntiles):
                for nt in range(nnt):
                    sq = sqbufs[j % len(sqbufs)]
                    dp = deps(i, nt) if callable(deps) else deps
                    t_sq = P.op("act", lambda E, sq=sq, i=i, nt=nt: E.activation(sq, src_fn(i, nt), AF.Square), [dp, sqfree[j % len(sqbufs)]])
                    b = stat_banks[nt]
                    t_mm = P.op("pe", lambda E, sq=sq, b=b, i=i: E.matmul(banks[b][:, :], ones_b, sq, start=(i == 0), stop=(i == ntiles - 1)),
                                [t_sq, bfree[b] if i == 0 else None])
                    sqfree[j % len(sqbufs)] = t_mm
                    last_mm[nt] = t_mm
                    j += 1
            for nt in range(nnt):
                b = stat_banks[nt]
                ta = P.op("act", lambda E, b=b, nt=nt: E.activation(rs_out[:, nt * 512:(nt + 1) * 512], banks[b][:, :], AF.Sqrt, bias=epsc, scale=1.0 / nfeat), [last_mm[nt]])
                bfree[b] = ta
                tb = P.op("dve", lambda E, nt=nt: E.reciprocal(rs_out[:, nt * 512:(nt + 1) * 512], rs_out[:, nt * 512:(nt + 1) * 512]), [ta])
                toks.append(tb)
            return toks

        A.push()
        yg = None
        spv = A.alloc([128, 96], F32)
        G = A.alloc([128, 8, 2, 8, 128], BF16)
        H = A.alloc([128, 8, 8, 128], BF16)
        ctab = A.alloc([128, 32, 128], F32)
        stab = A.alloc([128, 32, 128], F32)
        sm = A.alloc([128, 48, 32], F32)
        Zin_r = sm[:, 0, :]
        Zin_i = sm[:, 1, :]
        rho8 = sm[:, 2, :]
        e8r = sm[:, 3, :]
        e8i = sm[:, 4, :]
        apow = [(sm[:, 5 + 2 * t, :], sm[:, 6 + 2 * t, :]) for t in range(9)]
        _n = [23]

        def smalloc():
            i = _n[0]
            _n[0] += 1
            assert i < 48
            return sm[:, i, :]

        t_sp = P.dma("sp", spv, sp_d)
        S = Seq(P, "dve", [t_sp, t_cols])
        S.op(lambda E: E.memset(Zin_r, 0.0))
        S.op(lambda E: E.memset(Zin_i, 0.0))

        def cmul(S, or_, oi_, ar, ai, br, bi, t1, t2):
            S.par([lambda E: E.tensor_tensor(or_, ar, br, ALU.mult),
                   lambda E: E.tensor_tensor(t1, ai, bi, ALU.mult),
                   lambda E: E.tensor_tensor(oi_, ar, bi, ALU.mult),
                   lambda E: E.tensor_tensor(t2, ai, br, ALU.mult)])
            S.par([lambda E: E.tensor_tensor(or_, or_, t1, ALU.subtract),
                   lambda E: E.tensor_tensor(oi_, oi_, t2, ALU.add)])

        A.push()
        CPr = A.alloc([128, 32, 32], F32)
        CPni = A.alloc([128, 32, 32], F32)
        CPi0 = A.alloc([128, 32, 32], F32)
        t_cpr = P.dma("sp", CPr, cpr_d.rearrange("p (a b) -> p a b", b=32))
        t_cpi = P.dma("sp", CPi0, cpi_d.rearrange("p (a b) -> p a b", b=32))
        S.last = [S.last, t_cpr, t_cpi]
        S.op(lambda E: E.memset(CPni, 0.0))
        S.op(lambda E: E.tensor_tensor(CPni, CPni, CPi0, ALU.subtract))
        lamr = spv[:, 0:32]
        lami = spv[:, 32:64]
        ldt = spv[:, 64:96]
        dt_ = smalloc(); zr = smalloc(); zi = smalloc(); em1 = smalloc(); mag = smalloc()
        c1 = smalloc(); s1 = smalloc(); sh = smalloc(); ta = smalloc(); tb = smalloc()
        numr = smalloc(); numi = smalloc(); cfr = smalloc(); cfi = smalloc(); tc = smalloc(); td = smalloc()
        S.op(lambda E: E.activation(dt_, ldt, AF.Exp), eng="act")
        S.op(lambda E: E.tensor_tensor(zr, lamr, dt_, ALU.mult))
        S.op(lambda E: E.tensor_tensor(zi, lami, dt_, ALU.mult))
        S.op(lambda E: E.tensor_scalar(em1, zr, 1.0 / 120.0, None, ALU.mult))
        for cst_ in (1.0 / 24.0, 1.0 / 6.0, 0.5, 1.0):
            S.op(lambda E, c=cst_: E.scalar_tensor_tensor(em1, em1, c, zr, ALU.add, ALU.mult))
        S.op(lambda E: E.tensor_scalar(mag, em1, 1.0, None, ALU.add))

        def sin_of(dst, src, shift, scale):
            S.op(lambda E: E.tensor_scalar(ta, src, scale, shift, ALU.mult, ALU.add))
            S.op(lambda E: E.tensor_scalar(tb, ta, 1.0 / TWO_PI, MAGIC, ALU.mult, ALU.add))
            S.op(lambda E: E.tensor_scalar(tb, tb, MAGIC, None, ALU.subtract))
            S.op(lambda E: E.scalar_tensor_tensor(ta, tb, -TWO_PI, ta, ALU.mult, ALU.add))
            S.op(lambda E: E.tensor_scalar(ta, ta, math.pi, -math.pi, ALU.min, ALU.max))
            S.op(lambda E: E.activation(dst, ta, AF.Sin), eng="act")

        sin_of(s1, zi, 0.0, 1.0)
        sin_of(c1, zi, math.pi / 2.0, 1.0)
        sin_of(sh, zi, 0.0, 0.5)
        S.op(lambda E: E.tensor_tensor(ta, sh, sh, ALU.mult))
        S.op(lambda E: E.tensor_tensor(tb, em1, c1, ALU.mult))
        S.op(lambda E: E.scalar_tensor_tensor(numr, ta, -2.0, tb, ALU.mult, ALU.add))
        S.op(lambda E: E.tensor_tensor(numi, mag, s1, ALU.mult))
        S.op(lambda E: E.tensor_tensor(ta, lamr, lamr, ALU.mult))
        S.op(lambda E: E.tensor_tensor(tb, lami, lami, ALU.mult))
        S.op(lambda E: E.tensor_tensor(ta, ta, tb, ALU.add))
        S.op(lambda E: E.reciprocal(ta, ta))
        S.op(lambda E: E.tensor_tensor(tb, numr, lamr, ALU.mult))
        S.op(lambda E: E.tensor_tensor(tc, numi, lami, ALU.mult))
        S.op(lambda E: E.tensor_tensor(tb, tb, tc, ALU.add))
        S.op(lambda E: E.tensor_tensor(cfr, tb, ta, ALU.mult))
        S.op(lambda E: E.tensor_tensor(tb, numi, lamr, ALU.mult))
        S.op(lambda E: E.tensor_tensor(tc, numr, lami, ALU.mult))
        S.op(lambda E: E.tensor_tensor(tb, tb, tc, ALU.subtract))
        S.op(lambda E: E.tensor_tensor(cfi, tb, ta, ALU.mult))
        S.op(lambda E: E.memset(apow[0][0], 1.0))
        S.op(lambda E: E.memset(apow[0][1], 0.0))
        S.op(lambda E: E.tensor_tensor(apow[1][0], mag, c1, ALU.mult))
        S.op(lambda E: E.tensor_tensor(apow[1][1], mag, s1, ALU.mult))
        for t in range(1, 8):
            cmul(S, apow[t + 1][0], apow[t + 1][1], apow[t][0], apow[t][1], apow[1][0], apow[1][1], tc, td)
        S.op(lambda E: E.tensor_tensor(rho8, mag, mag, ALU.mult))
        S.op(lambda E: E.tensor_tensor(rho8, rho8, rho8, ALU.mult))
        S.op(lambda E: E.tensor_tensor(rho8, rho8, rho8, ALU.mult))
        e2r = smalloc(); e2i = smalloc()
        cmul(S, e2r, e2i, c1, s1, c1, s1, tc, td)
        e4r = smalloc(); e4i = smalloc()
        cmul(S, e4r, e4i, e2r, e2i, e2r, e2i, tc, td)
        cmul(S, e8r, e8i, e4r, e4i, e4r, e4i, tc, td)
        if KSTOP == 4:
            tap("sm", sm.rearrange("p a b -> p (a b)"), [128, 48 * 32])
            P.emit()
            return nc, tap_out
        big1 = A.alloc([128, 32, 64], F32)
        big2 = A.alloc([128, 32, 64], F32)
        S.op(lambda E: E.memset(ctab[:, :, 0:1], 1.0))
        S.op(lambda E: E.memset(stab[:, :, 0:1], 0.0))
        Er, Ei = e8r, e8i
        epp = [(smalloc(), smalloc()), (smalloc(), smalloc())]
        epi = 0
        L = 1
        while L < 128:
            br_ = bcast_last(Er, L)
            bi_ = bcast_last(Ei, L)
            cmul(S, ctab[:, :, L:2 * L], stab[:, :, L:2 * L], ctab[:, :, 0:L], stab[:, :, 0:L], br_, bi_,
                 big1[:, :, 0:L], big2[:, :, 0:L])
            if 2 * L < 128:
                nr, ni = epp[epi % 2]
                epi += 1
                cmul(S, nr, ni, Er, Ei, Er, Ei, tc, td)
                Er, Ei = nr, ni
            L *= 2
        if KSTOP == 5:
            tap("ctab", ctab.rearrange("p a b -> p (a b)"), [128, 32 * 128])
            tap("sm", sm.rearrange("p a b -> p (a b)"), [128, 48 * 32])
            P.emit()
            return nc, tap_out
        BPr = A.alloc([128, 32, 32], F32)
        BPi = A.alloc([128, 32, 32], F32)
        BBr = A.alloc([128, 32, 32], F32)
        BBi = A.alloc([128, 32, 32], F32)
        ABr = A.alloc([128, 32, 32], F32)
        ABi = A.alloc([128, 32, 32], F32)
        T1 = A.alloc([128, 32, 32], F32)
        T2 = A.alloc([128, 32, 32], F32)
        tmpH4 = A.alloc([128, 4, 128], F32)
        t_bpr = P.dma("sp", BPr, bpr_d.rearrange("p (a b) -> p a b", b=32))
        t_bpi = P.dma("sp", BPi, bpi_d.rearrange("p (a b) -> p a b", b=32))
        S.last = [S.last, t_bpr, t_bpi]
        cmul(S, BBr, BBi, BPr, BPi, bcast_last(cfr, 32), bcast_last(cfi, 32), T1, T2)
        if KSTOP == 6:
            tap("BBr", BBr.rearrange("p a b -> p (a b)"), [128, 1024])
            P.emit()
            return nc, tap_out
        ABh = [A.alloc([128, 1024], BF16) for _ in range(2)]
        ABl = [A.alloc([128, 1024], BF16) for _ in range(2)]
        Chl = [A.alloc([128, 1024], BF16) for _ in range(4)]
        for pp, src_ in ((0, CPr), (1, CPni)):
            srcf = src_.rearrange("p a b -> p (a b)")
            S.op(lambda E, pp=pp, srcf=srcf: E.tensor_copy(Chl[2 * pp], srcf))
            S.op(lambda E, pp=pp, srcf=srcf: E.tensor_tensor(T1.rearrange("p a b -> p (a b)"), srcf, Chl[2 * pp], ALU.subtract))
            S.op(lambda E, pp=pp: E.tensor_copy(Chl[2 * pp + 1], T1.rearrange("p a b -> p (a b)")))
        gh_done = []
        for tau in range(8 if KSTOP != 7 else 1):
            cmul(S, ABr, ABi, BBr, BBi, bcast_last(apow[tau][0], 32), bcast_last(apow[tau][1], 32), T1, T2)
            for pp, src_ in ((0, ABr), (1, ABi)):
                srcf = src_.rearrange("p a b -> p (a b)")
                S.op(lambda E, pp=pp, srcf=srcf: E.tensor_copy(ABh[pp], srcf))
                S.op(lambda E, pp=pp, srcf=srcf: E.tensor_tensor(T1.rearrange("p a b -> p (a b)"), srcf, ABh[pp], ALU.subtract))
                S.op(lambda E, pp=pp: E.tensor_copy(ABl[pp], T1.rearrange("p a b -> p (a b)")))
            t_hl = S.last
            t_ab = S.last
            ABrf = ABr.rearrange("p a b -> p (a b)")
            ABif = ABi.rearrange("p a b -> p (a b)")
            CPrf = CPr.rearrange("p a b -> p (a b)")
            CPnif = CPni.rearrange("p a b -> p (a b)")
            rd = []
            for part, src in ((0, ABrf), (1, ABif)):
                for half in range(2):
                    b = part * 2 + half
                    t_tr = None
                    for q in range(4):
                        ct = half * 4 + q
                        sl = slice(q * 128, q * 128 + 128)
                        t_tr = P.op("pe", lambda E, b=b, sl=sl, src=src, ct=ct: E.transpose(banks[b][:, sl], src[:, ct * 128:(ct + 1) * 128], ident_f),
                                    [t_ab, bfree[b]], inc=(q == 3))
                    t_cp = P.op("act", lambda E, b=b, tau=tau, part=part, half=half: E.copy(
                        G[:, tau, part, half * 4:half * 4 + 4, :].rearrange("p a b -> p (a b)"), banks[b][:, :]), [t_tr])
                    rd.append((b, t_cp))
            for half in range(2):
                b = 4 + half
                t_mm = None
                pairs = [(ABh[0], Chl[0]), (ABh[0], Chl[1]), (ABl[0], Chl[0]), (ABh[1], Chl[2]), (ABh[1], Chl[3]), (ABl[1], Chl[2])]
                for q in range(4):
                    ct = half * 4 + q
                    sl = slice(q * 128, q * 128 + 128)
                    for pi_, (aa, cc) in enumerate(pairs):
                        t_mm = P.op("pe", lambda E, b=b, sl=sl, ct=ct, aa=aa, cc=cc, pi_=pi_, q=q: E.matmul(
                            banks[b][:, sl], aa[:, ct * 128:(ct + 1) * 128], cc[:, ct * 128:(ct + 1) * 128],
                            start=(pi_ == 0 and q == 0), stop=(pi_ == 5), skip_group_check=True),
                            [t_hl, bfree[b]] if (q == 0 and pi_ == 0) else [], inc=(q == 3 and pi_ == 5))
                bm4 = bmask.unsqueeze(1).to_broadcast([128, 4, 128])
                bk4 = banks[b][:, :].rearrange("p (a b) -> p a b", b=128)
                if tau == 0:
                    t_e1 = P.op("dve", lambda E, bk4=bk4, bm4=bm4: E.tensor_tensor(tmpH4, bk4, bm4, ALU.mult), [t_mm, S.last])
                    rd.append((b, t_e1))
                    tl = t_e1
                    for q in range(4):
                        ct = half * 4 + q
                        tl = P.op("dve", lambda E, ct=ct, q=q: E.scalar_tensor_tensor(H[:, 0, ct, :], ident_f, gcol("ssm_d", ct), tmpH4[:, q, :], ALU.mult, ALU.add), [tl])
                    S.last = [tl]
                else:
                    t_ev = P.op("dve", lambda E, bk4=bk4, bm4=bm4, tau=tau, half=half: E.tensor_tensor(H[:, tau, half * 4:half * 4 + 4, :], bk4, bm4, ALU.mult), [t_mm])
                    rd.append((b, t_ev))
            for b in range(6):
                bfree[b] = [t for (bb, t) in rd if bb == b]
            gh_done.append([t for (_, t) in rd])
            S.last = [S.last, gh_done[-1]]
            if tau == 7:
                tap("ABr", ABr.rearrange("p a b -> p (a b)"), [128, 1024])
                tap("ABi", ABi.rearrange("p a b -> p (a b)"), [128, 1024])
                tap("CPr", CPr.rearrange("p a b -> p (a b)"), [128, 1024])
                tap("CPni", CPni.rearrange("p a b -> p (a b)"), [128, 1024])
                tap("Chl0", Chl[0], [128, 1024])
                tap("ABh0", ABh[0], [128, 1024])
        A.pop()
        P.barrier()
        tap("G", G.rearrange("p a b c d -> p (a b c d)"), [128, 8 * 2 * 8 * 128])
        tap("H", H.rearrange("p a b c -> p (a b c)"), [128, 8 * 8 * 128])
        tap("ctab", ctab.rearrange("p a b -> p (a b)"), [128, 32 * 128])
        tap("stab", stab.rearrange("p a b -> p (a b)"), [128, 32 * 128])
        tap("sm", sm.rearrange("p a b -> p (a b)"), [128, 48 * 32])

        ucs = [A.alloc([128, 8, 1024], BF16) for _ in range(2)]
        um_bufs = [A.alloc([128, 4, 1024], BF16) for _ in range(2)]
        wblk = A.alloc([128, 6, 512], F32)
        Wr, Wi, Sr, Si, t1a, t2a = [wblk[:, q, :].rearrange("p (a b) -> p a b", b=128) for q in range(6)]
        Zpr = A.alloc([128, 4, 128], BF16)
        Zpi = A.alloc([128, 4, 128], BF16)
        w0 = A.alloc([128, 6, 4], F32)
        udi = A.alloc([128, 1024], BF16)
        um_free = [None, None]
        ssm_state = {"dve": [], "zp_free": None}

        def norm_proj_chunk(c, wcol0, n_mt, dst_fn, save_scratch, env, nts=(0, 1)):
            xring, sqb, xg, rs, wring = env["xring"], env["sqb"], env["xg"], env["rs"], env["wring"]
            for nt in nts:
                tok0 = 1024 * c + 512 * nt
                last_mm = None
                xg_toks = []
                for kt in range(KT):
                    xi = env["xi"]; env["xi"] = (xi + 1) % len(xring)
                    xb = xring[xi]
                    t_ld = P.dma("sp", xb, xT[kt * 128:(kt + 1) * 128, tok0:tok0 + 512], [env["xfree"][xi]])
                    si = kt % 2
                    t_sq = P.op("act", lambda E, xb=xb, si=si: E.activation(sqb[si], xb, AF.Square), [t_ld, env["sqfree"][si]])
                    last_mm = P.op("pe", lambda E, si=si, kt=kt: E.matmul(banks[6][:, :], ones_b, sqb[si], start=(kt == 0), stop=(kt == KT - 1)),
                                   [t_sq, bfree[6] if kt == 0 else None])
                    env["sqfree"][si] = last_mm
                    t_xg = P.op("dve", lambda E, xb=xb, kt=kt: E.tensor_scalar(xg[:, kt, :], xb, gcol("mix_pre", kt), None, ALU.mult),
                                [t_ld, env["xg_free"]])
                    xg_toks.append(t_xg)
                    env["xfree"][xi] = [t_sq, t_xg]
                ta_ = P.op("act", lambda E: E.activation(rs, banks[6][:, :], AF.Sqrt, bias=epsc, scale=1.0 / D), [last_mm, env["rs_free"]])
                bfree[6] = ta_
                t_rs = P.op("dve", lambda E: E.reciprocal(rs, rs), [ta_])
                readers = []
                if save_scratch:
                    tcol = tok0 - 1024
                    readers.append(P.dma("sp", xg_s.rearrange("(kt p) t -> p kt t", p=128)[:, :, tcol:tcol + 512], xg, [xg_toks]))
                    readers.append(P.dma("sp", rs_s[:, tcol:tcol + 512], rs, [t_rs]))
                evs = []
                for m in range(n_mt):
                    if m % 2 == 0:
                        wi_ = env["wi"]; env["wi"] = (wi_ + 1) % len(wring)
                        wsl = wring[wi_]
                        t_w = P.dma("pool", wsl, w_in_v[:, :, wcol0 + m * 128:wcol0 + m * 128 + 256], [env["wfree"][wi_]])
                        env["wfree"][wi_] = []
                        cur = (wi_, wsl, t_w)
                    wi_, wsl, t_w = cur
                    b = (4, 5, 7)[env["bi"] % 3]; env["bi"] += 1
                    t_mm = None
                    for kt in range(KT):
                        t_mm = P.op("pe", lambda E, b=b, wsl=wsl, kt=kt, mo=(m % 2) * 128: E.matmul(banks[b][:, :], wsl[:, kt, mo:mo + 128], xg[:, kt, :],
                                                                                                  start=(kt == 0), stop=(kt == KT - 1)),
                                    [t_w, xg_toks, bfree[b]] if kt == 0 else [], inc=(kt == KT - 1))
                    env["wfree"][wi_].append(t_mm)
                    dst = dst_fn(m, nt)
                    t_ev = P.op("dve", lambda E, b=b, dst=dst: E.tensor_tensor(dst, banks[b][:, :], rs, ALU.mult), [t_mm, t_rs, env["dst_free"]])
                    bfree[b] = t_ev
                    evs.append(t_ev)
                    readers.append(t_mm)
                env["xg_free"] = readers
                env["rs_free"] = [evs, readers]
                env["done"].setdefault(c, []).append(evs)
            return

        def ssm_chunk(c, own, u_ready, uc, Fm=None, side=None):
            readers = []

            def do_ct(ct):
                ui = ct % 2
                um = um_bufs[ui]
                vb0, vb1 = (0, 1) if (own or ct % 2 == 0) else (2, 3)
                tm = []
                for j in range(4):
                    tm.append(P.op("act", lambda E, j=j, ct=ct: E.activation(um[:, j, :].rearrange("p (i k) -> p i k", k=128), uc[:, ct, :].rearrange("p (k i) -> p i k", i=8),
                                                                                AF.Copy, scale=pmask[:, j:j + 1]),
                                   [u_ready, um_free[ui]]))
                if own:
                    tm.append(P.op("act", lambda E, ct=ct: E.copy(udi.rearrange("p (i k) -> p i k", k=128), uc[:, ct, :].rearrange("p (k i) -> p i k", i=8)),
                                   [u_ready, ssm_state["zp_free"]]))
                t_v = None
                for part in range(2):
                    for i in range(8):
                        last = (part == 1 and i == 7)
                        vb = vb0 if part == 0 else vb1
                        t_v = P.op("pe", lambda E, part=part, i=i, ct=ct, vb=vb: E.matmul(banks[vb][:, :].rearrange("p (a b) -> p a b", b=128), G[:, 7 - i, part, ct, :], um[:, :, i * 128:(i + 1) * 128],
                                                                              start=(i == 0), stop=(i == 7)),
                                   [tm, bfree[vb0], bfree[vb1]] if (part == 0 and i == 0) else [], inc=last)
                um_free[ui] = t_v
                cs = ctab[:, 4 * ct:4 * ct + 4, :]
                sn = stab[:, 4 * ct:4 * ct + 4, :]
                Vr = banks[vb0][:, :].rearrange("p (a b) -> p a b", b=128)
                Vi = banks[vb1][:, :].rearrange("p (a b) -> p a b", b=128)
                prev = ssm_state["dve"]
                B0, B1, B2, B3, Sr_, Si_ = Wr, Wi, Sr, Si, t1a, t2a
                dv = lambda fn, deps: P.op("dve", fn, deps)
                zr_ = Zin_r[:, 4 * ct:4 * ct + 4]
                zi_ = Zin_i[:, 4 * ct:4 * ct + 4]
                er_ = e8r[:, 4 * ct:4 * ct + 4]
                ei_ = e8i[:, 4 * ct:4 * ct + 4]
                a1 = dv(lambda E: E.tensor_tensor(B0, Vr, cs, ALU.mult), [t_v, prev])
                a2 = dv(lambda E: E.tensor_tensor(B1, Vi, sn, ALU.mult), [t_v, prev])
                a3 = dv(lambda E: E.tensor_tensor(B2, Vi, cs, ALU.mult), [t_v, prev])
                a4 = dv(lambda E: E.tensor_tensor(B3, Vr, sn, ALU.mult), [t_v, prev])
                bfree[vb0] = [a1, a2, a3, a4]
                bfree[vb1] = [a1, a2, a3, a4]
                w1 = dv(lambda E: E.tensor_tensor(w0[:, 2, :], er_, zr_, ALU.mult), [prev])
                w2 = dv(lambda E: E.tensor_tensor(w0[:, 3, :], ei_, zi_, ALU.mult), [prev])
                w3 = dv(lambda E: E.tensor_tensor(w0[:, 4, :], er_, zi_, ALU.mult), [prev])
                w4 = dv(lambda E: E.tensor_tensor(w0[:, 5, :], ei_, zr_, ALU.mult), [prev])
                a5 = dv(lambda E: E.tensor_tensor(B0, B0, B1, ALU.add), [a1, a2])
                a6 = dv(lambda E: E.tensor_tensor(B2, B2, B3, ALU.subtract), [a3, a4])
                w5 = dv(lambda E: E.tensor_tensor(w0[:, 0, :], w0[:, 2, :], w0[:, 3, :], ALU.subtract), [w1, w2])
                w6 = dv(lambda E: E.tensor_tensor(w0[:, 1, :], w0[:, 4, :], w0[:, 5, :], ALU.add), [w3, w4])
                sr_t, si_t = [], []
                for j in range(4):
                    pr = 4 * ct + j
                    d0 = rho8[:, pr:pr + 1].to_broadcast([128, 128])
                    sr_t.append(dv(lambda E, j=j, d0=d0: E.tensor_tensor_scan(Sr_[:, j, :], d0, B0[:, j, :], w0[:, 0, j:j + 1], ALU.mult, ALU.add), [a5, w5, prev]))
                    si_t.append(dv(lambda E, j=j, d0=d0: E.tensor_tensor_scan(Si_[:, j, :], d0, B2[:, j, :], w0[:, 1, j:j + 1], ALU.mult, ALU.add), [a6, w6, prev]))
                c0 = []
                if own:
                    c0.append(dv(lambda E: E.tensor_copy(Zpr[:, :, 0:1], zr_.unsqueeze(2)), [ssm_state["zp_free"], prev]))
                    c0.append(dv(lambda E: E.tensor_copy(Zpi[:, :, 0:1], zi_.unsqueeze(2)), [ssm_state["zp_free"], prev]))
                c127 = cs[:, :, 127:128]
                s127 = sn[:, :, 127:128]
                z1 = dv(lambda E: E.tensor_tensor(w0[:, 2, :].unsqueeze(2), c127, Sr_[:, :, 127:128], ALU.mult), [sr_t, w5])
                z2 = dv(lambda E: E.tensor_tensor(w0[:, 3, :].unsqueeze(2), s127, Si_[:, :, 127:128], ALU.mult), [si_t, w5])
                z3 = dv(lambda E: E.tensor_tensor(w0[:, 4, :].unsqueeze(2), s127, Sr_[:, :, 127:128], ALU.mult), [sr_t, w6])
                z4 = dv(lambda E: E.tensor_tensor(w0[:, 5, :].unsqueeze(2), c127, Si_[:, :, 127:128], ALU.mult), [si_t, w6])
                z5 = dv(lambda E: E.tensor_tensor(zr_, w0[:, 2, :], w0[:, 3, :], ALU.subtract), [z1, z2, w1, w4, c0])
                z6 = dv(lambda E: E.tensor_tensor(zi_, w0[:, 4, :], w0[:, 5, :], ALU.add), [z3, z4, w2, w3, c0])
                all_t = [a1, a2, a3, a4, w1, w2, w3, w4, a5, a6, w5, w6, sr_t, si_t, c0, z1, z2, z3, z4, z5, z6]
                if own:
                    cs7 = cs[:, :, 0:127]; sn7 = sn[:, :, 0:127]
                    d1 = dv(lambda E: E.tensor_tensor(B1[:, :, 0:127], cs7, Sr_[:, :, 0:127], ALU.mult), [sr_t, a5])
                    d2 = dv(lambda E: E.tensor_tensor(B3[:, :, 0:127], sn7, Si_[:, :, 0:127], ALU.mult), [si_t, a6])
                    d3 = dv(lambda E: E.tensor_tensor(Zpr[:, :, 1:128], B1[:, :, 0:127], B3[:, :, 0:127], ALU.subtract), [d1, d2, ssm_state["zp_free"]])
                    d4 = dv(lambda E: E.tensor_tensor(B1[:, :, 0:127], sn7, Sr_[:, :, 0:127], ALU.mult), [d3])
                    d5 = dv(lambda E: E.tensor_tensor(B3[:, :, 0:127], cs7, Si_[:, :, 0:127], ALU.mult), [d3])
                    d6 = dv(lambda E: E.tensor_tensor(Zpi[:, :, 1:128], B1[:, :, 0:127], B3[:, :, 0:127], ALU.add), [d4, d5])
                    t_zp = [d3, d6, c0]
                    all_t += [d1, d2, d3, d4, d5, d6]

                class _L:
                    last = all_t
                Sq = _L()
                if own:
                    t_y = [None, None]
                    for hb in range(2):
                        b = 2 + hb
                        first = True
                        for j in range(4 * hb, 4 * hb + 4):
                            jc = (j % 4) * 128
                            for i in range(j + 1):
                                P.op("pe", lambda E, b=b, j=j, i=i, ct=ct, jc=jc, first=first: E.matmul(banks[b][:, jc:jc + 128], H[:, j - i, ct, :], udi[:, i * 128:(i + 1) * 128],
                                                                                                 start=first, stop=False, skip_group_check=True),
                                     [u_ready, bfree[b], t_zp, tm] if first else [], inc=False)
                                first = False
                            for pl in range(4):
                                for part in range(2):
                                    zp = Zpr if part == 0 else Zpi
                                    lastm = (j == 4 * hb + 3 and pl == 3 and part == 1)
                                    t_ = P.op("pe", lambda E, b=b, j=j, pl=pl, part=part, zp=zp, ct=ct, jc=jc: E.matmul(
                                        banks[b][32 * pl:32 * pl + 32, jc:jc + 128], Fm[:, j, part, (4 * ct + pl) * 32:(4 * ct + pl) * 32 + 32], zp[:, pl, :],
                                        start=False, stop=(pl == 3 and part == 1), tile_position=(0, 32 * pl), skip_group_check=True), [], inc=lastm)
                                    if lastm:
                                        t_y[hb] = t_
                    ssm_state["zp_free"] = t_y[1]
                    readers.append(t_y[1])
                    yi = ssm_state.get("yi", 0)
                    ssm_state["yi"] = yi + 1
                    yrow = yring[yi % 2]
                    t_fin = []
                    for hb in range(2):
                        b = 2 + hb
                        yb = banks[b][:, :]
                        dst = yrow.rearrange("p (k j) -> p j k", j=8)[:, 4 * hb:4 * hb + 4, :]
                        G1 = gtmp[0]; G2 = gtmp[1]
                        ta1 = P.op("act", lambda E, yb=yb, G1=G1: E.activation(G1, yb, AF.Square), [t_y[hb], ssm_state.get("g_free")])
                        tb1 = P.op("dve", lambda E, G1=G1: E.tensor_scalar(G1, G1, 0.044715, 1.0, ALU.mult, ALU.add), [ta1, Sq.last])
                        tb2 = P.op("dve", lambda E, G1=G1, yb=yb: E.tensor_tensor(G1, G1, yb, ALU.mult), [tb1])
                        ta2 = P.op("act", lambda E, G1=G1, G2=G2: E.activation(G2, G1, AF.Sigmoid, scale=1.5957691216057308), [tb2])
                        tb3 = P.op("dve", lambda E, G2=G2, yb=yb, dst=dst: E.tensor_tensor(dst, yb.rearrange("p (j k) -> p j k", k=128), G2.rearrange("p (j k) -> p j k", k=128), ALU.mult),
                                   [ta2, yfree[yi % 2]])
                        bfree[b] = tb3
                        ssm_state["g_free"] = tb3
                        Sq.last = [tb3]
                        t_fin.append(tb3)
                    yfree[yi % 2] = P.dma("sp", yg_s[ct * 128:(ct + 1) * 128, :], yrow, [t_fin])
                else:
                    readers.append(t_v)
                ssm_state["dve"] = Sq.last

            for ct_ in range(8):
                do_ct(ct_)
                if side and ct_ in side:
                    side[ct_]()
            return readers

        A.push()
        env = {"xring": [A.alloc([128, 512], F32) for _ in range(4)], "sqb": [A.alloc([128, 512], BF16) for _ in range(2)],
               "xg": A.alloc([128, KT, 512], BF16), "rs": A.alloc([128, 512], F32),
               "wring": [A.alloc([128, KT, 256], BF16) for _ in range(3)],
               "xi": 0, "wi": 0, "bi": 0, "xfree": [None] * 4, "sqfree": [None] * 2, "wfree": [[], [], []],
               "xg_free": None, "rs_free": None, "dst_free": None}
        env["done"] = {}
        uc_readers = [None, None]

        def proj_job(c, nts):
            def f():
                ucb = ucs[c % 2]
                env["dst_free"] = uc_readers[c % 2]
                norm_proj_chunk(c, 3072, 8, lambda m, nt, ucb=ucb: ucb[:, m, 512 * nt:512 * nt + 512], c >= 1, env, nts=nts)
            return f
        proj_job(0, (0, 1))()
        for c in range(NCH - 1):
            if KSIDE:
                side = {1: proj_job(c + 1, (0,)), 4: proj_job(c + 1, (1,))}
                uc_readers[c % 2] = ssm_chunk(c, False, env["done"][c], ucs[c % 2], side=side)
            else:
                uc_readers[c % 2] = ssm_chunk(c, False, env["done"][c], ucs[c % 2])
                proj_job(c + 1, (0, 1))()
        P.barrier()
        A.pop()
        if "u_own" in taps:
            tap("u_own", ucs[(NCH - 1) % 2].rearrange("p a b -> p (a b)"), [128, 8 * 1024])
        tap("zin", sm.rearrange("p a b -> p (a b)"), [128, 48 * 32])
        Fm = A.alloc([128, 8, 2, 1024], BF16)
        yring = [A.alloc([128, 1024], F32) for _ in range(2)]
        yfree = [None, None]
        gtmp = [A.alloc([128, 512], F32) for _ in range(2)]
        A.push()
        Ft1b = wblk[:, 0:2, :].rearrange("p a (b c) -> p (a b) c", c=32)
        Ft2b = wblk[:, 2:4, :].rearrange("p a (b c) -> p (a b) c", c=32)
        CPr2 = A.alloc([128, 32, 32], F32)
        CPni2 = A.alloc([128, 32, 32], F32)
        CPi02 = wblk[:, 4:6, :].rearrange("p a (b c) -> p (a b) c", c=32)
        t_cpr2 = P.dma("sp", CPr2, cpr_d.rearrange("p (a b) -> p a b", b=32))
        t_cpi2 = P.dma("sp", CPi02, cpi_d.rearrange("p (a b) -> p a b", b=32))
        SF = Seq(P, "dve", [t_cpr2, t_cpi2])
        SF.op(lambda E: E.memset(CPni2, 0.0))
        SF.op(lambda E: E.tensor_tensor(CPni2, CPni2, CPi02, ALU.subtract))
        for j in range(8):
            pr_ = bcast_last(apow[j + 1][0], 32)
            pi_ = bcast_last(apow[j + 1][1], 32)
            SF.op(lambda E, pr_=pr_: E.tensor_tensor(Ft1b, CPr2, pr_, ALU.mult))
            SF.op(lambda E, pi_=pi_: E.tensor_tensor(Ft2b, CPni2, pi_, ALU.mult))
            SF.op(lambda E, j=j: E.tensor_tensor(Fm[:, j, 0, :].rearrange("p (a b) -> p a b", b=32), Ft1b, Ft2b, ALU.add))
            SF.op(lambda E, pr_=pr_: E.tensor_tensor(Ft1b, CPni2, pr_, ALU.mult))
            SF.op(lambda E, pi_=pi_: E.tensor_tensor(Ft2b, CPr2, pi_, ALU.mult))
            SF.op(lambda E, j=j: E.tensor_tensor(Fm[:, j, 1, :].rearrange("p (a b) -> p a b", b=32), Ft1b, Ft2b, ALU.subtract))
        P.barrier()
        A.pop()
        ssm_chunk(NCH - 1, True, P.all_tokens(), ucs[(NCH - 1) % 2], Fm=Fm)
        P.barrier()
        A.pop()

        def make_wring(n):
            return {"slots": [A.alloc([128, 16, 256], BF16) for _ in range(n)], "free": [[] for _ in range(n)], "i": 0}

        def linear(w_view, KTn, m_tiles, rhs_fn, evac_fn, wr, bank_sets, rhs_deps):
            nkg = KTn // 16
            for pi_ in range(0, len(m_tiles), 2):
                mp = m_tiles[pi_:pi_ + 2]
                bset = bank_sets[(pi_ // 2) % len(bank_sets)]
                lastmm = {}
                for g in range(nkg):
                    wi_ = wr["i"]; wr["i"] = (wi_ + 1) % len(wr["slots"])
                    wsl = wr["slots"][wi_]
                    t_w = P.dma("pool", wsl[:, :, 0:128 * len(mp)], w_view[:, g * 16:(g + 1) * 16, mp[0] * 128:mp[0] * 128 + 128 * len(mp)], [wr["free"][wi_]])
                    wr["free"][wi_] = []
                    for mi in range(len(mp)):
                        for nt in range(2):
                            b = bset[mi * 2 + nt]
                            t_mm = None
                            for kl in range(16):
                                kt = g * 16 + kl
                                firstb = (g == 0 and kl == 0)
                                t_mm = P.op("pe", lambda E, b=b, wsl=wsl, kl=kl, mi=mi, kt=kt, nt=nt, firstb=firstb, g=g: E.matmul(
                                    banks[b][:, :], wsl[:, kl, mi * 128:mi * 128 + 128], rhs_fn(kt, nt), start=firstb, stop=(g == nkg - 1 and kl == 15)),
                                    ([t_w, rhs_deps, bfree[b]] if firstb else ([t_w] if kl == 0 else [])), inc=(kl == 15))
                            lastmm[(mi, nt)] = t_mm
                            wr["free"][wi_].append(t_mm)
                for mi, m in enumerate(mp):
                    for nt in range(2):
                        b = bset[mi * 2 + nt]
                        bfree[b] = evac_fn(m, nt, b, lastmm[(mi, nt)])

        def stats_of(src_fn, ntiles, rs_out, nfeat, sqb_, deps, stat_banks=(6, 7)):
            return rms_stats(src_fn, ntiles, 1024, rs_out, nfeat, sqb_, list(stat_banks), deps)

        A.push()
        mixed = A.alloc([128, 16, 1024], BF16)
        A.push()
        yg = A.alloc([128, 8, 1024], F32)
        ygb = A.alloc([128, 8, 1024], BF16)
        wg = A.alloc([128, 8, 1024], BF16)
        gt2 = [A.alloc([128, 512], F32) for _ in range(2)]
        sqb2 = [A.alloc([128, 512], BF16) for _ in range(2)]
        rs2 = A.alloc([128, 1024], F32)
        t_wg = P.dma("pool", wg, w_glu.rearrange("(kt p) m -> p kt m", p=128))
        t_yb = []
        for ct in range(8):
            t_l = P.dma("sp", yg[:, ct, :], yg_s[ct * 128:(ct + 1) * 128, :])
            t_yb.append(P.op("act", lambda E, ct=ct: E.copy(ygb[:, ct, :], yg[:, ct, :]), [t_l]))
        tap("yg", yg.rearrange("p a b -> p (a b)"), [128, 8 * 1024])
        gfree = [None, None]
        glu_done = []
        for m in range(8):
            for nt in range(2):
                b = (m * 2 + nt) % 4
                t_mm = None
                for kt in range(8):
                    t_mm = P.op("pe", lambda E, b=b, kt=kt, m=m, nt=nt: E.matmul(banks[b][:, :], wg[:, kt, m * 128:(m + 1) * 128], ygb[:, kt, nt * 512:(nt + 1) * 512],
                                                                              start=(kt == 0), stop=(kt == 7)),
                                [t_wg, t_yb, bfree[b]] if kt == 0 else [], inc=(kt == 7))
                gi = (m * 2 + nt) % 2
                t_s = P.op("act", lambda E, b=b, gi=gi, m=m: E.activation(gt2[gi], banks[b][:, :], AF.Sigmoid, bias=gcol("b_glu", m)), [t_mm, gfree[gi]])
                bfree[b] = t_s
                t_g = P.op("dve", lambda E, gi=gi, m=m, nt=nt: E.tensor_tensor(yg[:, m, nt * 512:(nt + 1) * 512], yg[:, m, nt * 512:(nt + 1) * 512], gt2[gi], ALU.mult), [t_s])
                gfree[gi] = t_g
                glu_done.append(t_g)
        tap("ssm", yg.rearrange("p a b -> p (a b)"), [128, 8 * 1024])
        t_rs2 = stats_of(lambda i, nt: yg[:, i, nt * 512:(nt + 1) * 512], 8, rs2, 1024.0, sqb2, glu_done)
        for ct in range(8):
            for nt in range(2):
                P.op("dve", lambda E, ct=ct, nt=nt: E.scalar_tensor_tensor(mixed[:, 8 + ct, nt * 512:(nt + 1) * 512], yg[:, ct, nt * 512:(nt + 1) * 512],
                                                                          gcol("ssm_n", ct), rs2[:, nt * 512:(nt + 1) * 512], ALU.mult, ALU.mult), [t_rs2])
        P.barrier()
        A.pop()

        A.push()
        attnT = A.alloc([128, 8, 1024], F32)
        head_state = {"gi": 0, "vtok_free": None, "rec_free": None}
        SCALE = 1.0 / math.sqrt(128.0)
        xg_sv = xg_s.rearrange("(kt p) t -> p kt t", p=128)
        for hg in range(2):
            A.push()
            KTt = A.alloc([128, 4, 3072], BF16)
            VTt = A.alloc([128, 4, 3072], BF16)
            QTt = A.alloc([128, 4, 1024], BF16)
            A.push()
            xg3 = A.alloc([128, 16, 1024], BF16)
            rs3 = A.alloc([128, 1024], F32)
            wr3 = make_wring(3)
            for sc in range(3):
                t_x3 = P.dma("sp", xg3, xg_sv[:, :, 1024 * sc:1024 * sc + 1024], [P.all_tokens()] if (sc > 0 or hg > 0) else [])
                t_r3 = P.dma("sp", rs3, rs_s[:, 1024 * sc:1024 * sc + 1024], [P.all_tokens()] if (sc > 0 or hg > 0) else [])
                jobs = [(1024 + 512 * hg, KTt), (1024 + 512 * hg + 256, KTt), (2048 + 512 * hg, VTt), (2048 + 512 * hg + 256, VTt)]
                if sc == 2:
                    jobs += [(512 * hg, QTt), (512 * hg + 256, QTt)]
                for (wc0, dstT) in jobs:
                    hl0 = ((wc0 % 1024) - 512 * hg) // 128

                    def ev3(m, nt, b, t_mm, dstT=dstT, hl0=hl0, sc=sc):
                        hl = hl0 + m
                        if dstT is QTt:
                            dst = dstT[:, hl, nt * 512:(nt + 1) * 512]
                        else:
                            dst = dstT[:, hl, 1024 * sc + nt * 512:1024 * sc + (nt + 1) * 512]
                        return P.op("dve", lambda E, dst=dst, b=b, nt=nt: E.tensor_tensor(dst, banks[b][:, :], rs3[:, nt * 512:(nt + 1) * 512], ALU.mult), [t_mm, t_r3])
                    linear(w_in_v[:, :, wc0:wc0 + 256], 16, [0, 1], lambda kt, nt: xg3[:, kt, nt * 512:(nt + 1) * 512], ev3, wr3,
                           [[0, 1, 2, 3], [4, 5, 6, 7]], [t_x3])
            P.barrier()
            A.pop()
            if hg == 0:
                tap("KT0", KTt[:, 0, :], [128, 3072])
                tap("VT0", VTt[:, 0, :], [128, 3072])
                tap("QT0", QTt[:, 0, :], [128, 1024])
            A.push()
            Vtoks = [A.alloc([128, NVT + 3, 128], BF16) for _ in range(2)]
            NES = 6
            es = [A.alloc([128, 4, 128], BF16) for _ in range(NES)]
            pm = [A.alloc([128, 4, 128], BF16) for _ in range(NES)]
            rec = A.alloc([128, 512], F32)
            TB = banks[0][:, :].bitcast(BF16)
            es_free = [None] * NES
            pm_free = [None] * NES
            vtok_free = [None, None]
            for hl in range(4):
                h = 4 * hg + hl

                def do_head(hl, h):
                    Vtok = Vtoks[hl % 2]
                    vt_toks = []
                    for g0 in range(0, NVT, 8):
                        n = min(8, NVT - g0)
                        t_tr = None
                        for q in range(n):
                            d, r, m, nk = VT_LIST[g0 + q]
                            te0 = 2048 + r + d * m
                            t_tr = P.op("pe", lambda E, q=q, te0=te0, d=d, nk=nk: E.transpose(TB[0:nk, q * 128:(q + 1) * 128], VTt[:, hl, te0:te0 + d * (nk - 1) + 1:d], ident_b),
                                        [bfree[0], vtok_free[hl % 2]] if q == 0 else [], inc=(q == n - 1))
                        t_cp = P.op("act", lambda E, g0=g0, n=n: E.copy(Vtok[:, g0:g0 + n, :].rearrange("p a b -> p (a b)"), TB[:, 0:128 * n]), [t_tr])
                        bfree[0] = t_cp
                        vt_toks.append(t_cp)
                    tiles = []
                    for d in (1, 4, 16):
                        QB = min(128, NO // d)
                        for r in range(d):
                            for blk in range((NO // d) // QB):
                                m0 = blk * QB
                                for (mk, nk, mask) in ((m0 - 128, 128, mprev_b), (m0, QB, mcur_b)):
                                    vt = VT_IDX[(d, r, mk)] if (d, r, mk) in VT_IDX else None
                                    if vt is None:
                                        raise AssertionError((d, r, mk))
                                    outs = []
                                    if d == 16:
                                        outs = [(0, r, 16, 0, 32), (1, r, 16, 32, 64)]
                                    elif d == 4:
                                        outs = [(blk, r, 4, 0, 128)]
                                    else:
                                        outs = [(blk // 4, (blk % 4) * 128, 1, 0, 128)]
                                    tiles.append(dict(d=d, r=r, m0=m0, mk=mk, nk=nk, QB=QB, mask=mask, vt=vt, outs=outs))
                    started = set()
                    tix = {(T["d"], T["r"], T["m0"], T["mk"]): T for T in tiles}
                    groups = []
                    for half in range(2):
                        for ab in range(2):
                            grp = [tix[(1, 0, 128 * blk, 128 * blk - 128 if ab == 0 else 128 * blk)] for blk in range(4 * half, 4 * half + 4)]
                            groups.append((grp, ("d1", half)))
                    for blk in range(2):
                        for ab in range(2):
                            grp = [tix[(4, r, 128 * blk, 128 * blk - 128 if ab == 0 else 128 * blk)] for r in range(4)]
                            groups.append((grp, ("d4", blk)))
                    for r0 in range(0, 16, 4):
                        for ab in range(2):
                            grp = [tix[(16, r, 0, -128 if ab == 0 else 0)] for r in range(r0, r0 + 4)]
                            groups.append((grp, ("d16", r0)))
                    assert sum(len(g[0]) for g in groups) == len(tiles)
                    for (grp, gkind) in groups:
                        gi = (head_state["gi"]) % NES
                        sb_ = (1, 2, 7)[head_state["gi"] % 3]
                        head_state["gi"] += 1
                        t_s = None
                        for q, T in enumerate(grp):
                            d, r = T["d"], T["r"]
                            k0 = 2048 + r + d * T["mk"]
                            q0 = r + d * T["m0"]
                            t_s = P.op("pe", lambda E, q=q, k0=k0, q0=q0, d=d, nk=T["nk"], QB=T["QB"], sb_=sb_: E.matmul(
                                banks[sb_][0:nk, q * 128:q * 128 + QB], KTt[:, hl, k0:k0 + d * (nk - 1) + 1:d], QTt[:, hl, q0:q0 + d * (QB - 1) + 1:d],
                                start=True, stop=True, skip_group_check=True), [bfree[sb_]] if q == 0 else [], inc=(q == len(grp) - 1))
                        ng = len(grp)
                        t_e = P.op("act", lambda E, gi=gi, sb_=sb_, ng=ng: E.activation(es[gi][:, 0:ng, :].rearrange("p a b -> p (a b)"), banks[sb_][:, 0:128 * ng], AF.Exp, scale=SCALE),
                                   [t_s, es_free[gi]])
                        bfree[sb_] = t_e
                        t_ms = []
                        for q, T in enumerate(grp):
                            nk, QB = T["nk"], T["QB"]
                            t_ms.append(P.op("dve", lambda E, gi=gi, q=q, nk=nk, QB=QB, mask=T["mask"], vt=T["vt"]: E.scalar_tensor_tensor(
                                pm[gi][0:nk, q, 0:QB], es[gi][0:nk, q, 0:QB], vcols[0:nk, vt:vt + 1], mask[0:nk, 0:QB], ALU.mult, ALU.mult),
                                [t_e, pm_free[gi]] if q == 0 else [t_e]))
                        es_free[gi] = t_ms
                        t_pv = None
                        for q, T in enumerate(grp):
                            nk = T["nk"]
                            for (half, off, step, c0, c1) in T["outs"]:
                                ncol = c1 - c0
                                ob = 3 + half
                                st = ob not in started
                                started.add(ob)
                                lhs = Vtok[0:nk, T["vt"], :]
                                t_pv = P.op("pe", lambda E, ob=ob, off=off, step=step, ncol=ncol, lhs=lhs, gi=gi, q=q, nk=nk, c0=c0, c1=c1, st=st: E.matmul(
                                    banks[ob][:, off:off + step * (ncol - 1) + 1:step], lhs, pm[gi][0:nk, q, c0:c1], start=st, stop=False, skip_group_check=True),
                                    [t_ms, vt_toks, bfree[ob]] if st else [t_ms[q]], inc=True)
                        nkg = grp[0]["nk"]
                        dens = []
                        if gkind[0] == "d1":
                            dens.append((5 + gkind[1], banks[5 + gkind[1]][:, :], pm[gi][0:nkg].rearrange("p a b -> p (a b)"), nkg))
                        else:
                            for q, T in enumerate(grp):
                                for (half, off, step, c0, c1) in T["outs"]:
                                    ncol = c1 - c0
                                    dens.append((5 + half, banks[5 + half][:, off:off + step * (ncol - 1) + 1:step], pm[gi][0:T["nk"], q, c0:c1], T["nk"]))
                        for (ob, oap, rap, nk_) in dens:
                            st = ob not in started
                            started.add(ob)
                            t_pv = P.op("pe", lambda E, oap=oap, rap=rap, nk_=nk_, st=st: E.matmul(oap, ones_b[0:nk_, :], rap, start=st, stop=False, skip_group_check=True),
                                        [t_ms, bfree[ob]] if st else [t_ms], inc=True)
                        pm_free[gi] = t_pv
                    fin = []
                    for half in range(2):
                        t_r = P.op("dve", lambda E, half=half: E.reciprocal(rec, banks[5 + half][:, :]), [t_pv, head_state["rec_free"]])
                        t_f = P.op("dve", lambda E, half=half: E.tensor_tensor(attnT[:, h, half * 512:(half + 1) * 512], banks[3 + half][:, :], rec, ALU.mult), [t_r])
                        head_state["rec_free"] = t_f
                        bfree[5 + half] = t_r
                        bfree[3 + half] = t_f
                        fin.append(t_f)
                    vtok_free[hl % 2] = t_pv
                do_head(hl, h)
            P.barrier()
            A.pop()
            A.pop()
        tap("attn", attnT.rearrange("p a b -> p (a b)"), [128, 8 * 1024])
        A.push()
        sqb4 = [A.alloc([128, 512], BF16) for _ in range(2)]
        rs4 = A.alloc([128, 1024], F32)
        t_rs4 = stats_of(lambda i, nt: attnT[:, i, nt * 512:(nt + 1) * 512], 8, rs4, 1024.0, sqb4, [])
        for hh in range(8):
            for nt in range(2):
                P.op("dve", lambda E, hh=hh, nt=nt: E.scalar_tensor_tensor(mixed[:, hh, nt * 512:(nt + 1) * 512], attnT[:, hh, nt * 512:(nt + 1) * 512],
                                                                          gcol("attn_n", hh), rs4[:, nt * 512:(nt + 1) * 512], ALU.mult, ALU.mult), [t_rs4])
        P.barrier()
        A.pop()
        A.pop()
        tap("mixed", mixed.rearrange("p a b -> p (a b)"), [128, 16 * 1024])

        def residual_pass(src_load, base_load, gname, rs_, dst_store, tmp_ring, deps, inplace_src=None):
            fr = [None] * len(tmp_ring)
            k = 0
            outs = []
            for m in range(16):
                for nt in range(2):
                    a_, b_ = tmp_ring[k % len(tmp_ring)]
                    if inplace_src is not None:
                        a_ = inplace_src(m, nt)
                        t_a = None
                    else:
                        t_a = src_load(m, nt, a_, [fr[k % len(tmp_ring)], deps])
                    t_b = base_load(m, nt, b_, [fr[k % len(tmp_ring)], deps])
                    t1_ = P.op("dve", lambda E, a_=a_, m=m, nt=nt: E.scalar_tensor_tensor(a_, a_, gcol(gname, m), rs_[:, nt * 512:(nt + 1) * 512], ALU.mult, ALU.mult), [t_a, deps])
                    t2_ = P.op("dve", lambda E, a_=a_, b_=b_: E.tensor_tensor(b_, b_, a_, ALU.add), [t1_, t_b])
                    t3_ = dst_store(m, nt, b_, [t2_])
                    fr[k % len(tmp_ring)] = t3_
                    outs.append(t3_)
                    k += 1
            return outs

        def load_full(dst, src_dram, deps):
            toks = []
            for m in range(16):
                toks.append(P.dma("sp", dst[:, m, :], src_dram[m * 128:(m + 1) * 128, :], deps))
            return toks

        def prenorm(hT_, gname, hn_, sqb_, rs_, deps):
            t_rs = stats_of(lambda i, nt: hT_[:, i, nt * 512:(nt + 1) * 512], 16, rs_, float(D), sqb_, deps)
            toks = []
            for m in range(16):
                for nt in range(2):
                    toks.append(P.op("dve", lambda E, m=m, nt=nt: E.scalar_tensor_tensor(hn_[:, m, nt * 512:(nt + 1) * 512], hT_[:, m, nt * 512:(nt + 1) * 512],
                                                                                         gcol(gname, m), rs_[:, nt * 512:(nt + 1) * 512], ALU.mult, ALU.mult), [t_rs]))
            return toks

        A.push()
        mixT = A.alloc([128, 16, 1024], F32)
        wr4 = make_wring(3)
        sqb5 = [A.alloc([128, 512], BF16) for _ in range(2)]
        rs5 = A.alloc([128, 1024], F32)

        def ev4(m, nt, b, t_mm):
            return P.op("act", lambda E, m=m, nt=nt, b=b: E.copy(mixT[:, m, nt * 512:(nt + 1) * 512], banks[b][:, :]), [t_mm])
        linear(w_out.rearrange("(kt p) m -> p kt m", p=128), 16, list(range(16)), lambda kt, nt: mixed[:, kt, nt * 512:(nt + 1) * 512], ev4, wr4,
               [[0, 1, 2, 3], [4, 5, 6, 7]], [])
        P.barrier()
        t_rs5 = stats_of(lambda i, nt: mixT[:, i, nt * 512:(nt + 1) * 512], 16, rs5, float(D), sqb5, [])
        xoff = NE - NO
        xring4 = [A.alloc([128, 512], F32) for _ in range(4)]
        x4free = [None] * 4
        h_toks = []
        k4 = 0
        for m in range(16):
            for nt in range(2):
                sl = slice(nt * 512, (nt + 1) * 512)
                xb = xring4[k4 % 4]
                t_x = P.dma("sp", xb, xT[m * 128:(m + 1) * 128, xoff + nt * 512:xoff + (nt + 1) * 512], [x4free[k4 % 4]])
                t1_ = P.op("dve", lambda E, m=m, sl=sl: E.scalar_tensor_tensor(mixT[:, m, sl], mixT[:, m, sl], gcol("mix_post", m), rs5[:, sl], ALU.mult, ALU.mult), [t_rs5])
                t2_ = P.op("dve", lambda E, m=m, sl=sl, xb=xb: E.tensor_tensor(mixT[:, m, sl], mixT[:, m, sl], xb, ALU.add), [t1_, t_x])
                x4free[k4 % 4] = t2_
                P.dma("pool", h_s[m * 128:(m + 1) * 128, sl], mixT[:, m, sl], [t2_])
                h_toks.append(t2_)
                k4 += 1
        tap("h1", mixT.rearrange("p a b -> p (a b)"), [128, 16 * 1024])
        hn2 = mixed
        rs5b = A.alloc([128, 1024], F32)
        prenorm(mixT, "mlp_pre", hn2, sqb5, rs5b, lambda i, nt: h_toks[2 * i + nt])
        P.barrier()
        A.pop()

        act = A.alloc([128, 64, 1024], BF16)
        wr5 = make_wring(3)
        rtmp = [A.alloc([128, 512], F32) for _ in range(3)]
        rfree = [None] * 3
        rk = [0]

        def ev_up(m, nt, b, t_mm):
            i = rk[0] % 3
            rk[0] += 1
            t_r = P.op("act", lambda E, i=i, b=b: E.activation(rtmp[i], banks[b][:, :], AF.Relu), [t_mm, rfree[i]])
            t_q = P.op("dve", lambda E, i=i, m=m, nt=nt: E.tensor_tensor(act[:, m, nt * 512:(nt + 1) * 512], rtmp[i], rtmp[i], ALU.mult), [t_r])
            rfree[i] = t_q
            return t_r
        linear(w_up.rearrange("(kt p) m -> p kt m", p=128), 16, list(range(64)), lambda kt, nt: hn2[:, kt, nt * 512:(nt + 1) * 512], ev_up, wr5,
               [[0, 1, 2, 3], [4, 5, 6, 7]], [])
        P.barrier()

        def ev_dn(m, nt, b, t_mm):
            i = rk[0] % 3
            rk[0] += 1
            t_c = P.op("act", lambda E, i=i, b=b: E.copy(rtmp[i], banks[b][:, :]), [t_mm, rfree[i]])
            rfree[i] = P.dma("sp", ff_s[m * 128:(m + 1) * 128, nt * 512:(nt + 1) * 512], rtmp[i], [t_c])
            return t_c
        linear(w_down.rearrange("(kt p) m -> p kt m", p=128), 64, list(range(16)), lambda kt, nt: act[:, kt, nt * 512:(nt + 1) * 512], ev_dn, wr5,
               [[0, 1, 2, 3], [4, 5, 6, 7]], [])
        P.barrier()
        A.pop()

        A.push()
        h2 = A.alloc([128, 16, 1024], F32)
        prod = A.alloc([128, 16, 1024], F32)
        hn3 = A.alloc([128, 16, 1024], BF16)
        sqb7 = [A.alloc([128, 512], BF16) for _ in range(2)]
        rs7 = A.alloc([128, 1024], F32)
        t_ff = load_full(prod, ff_s, [])
        t_h2 = load_full(h2, h_s, [])
        t_rs7 = stats_of(lambda i, nt: prod[:, i, nt * 512:(nt + 1) * 512], 16, rs7, float(D), sqb7, lambda i, nt: t_ff[i])
        res_t = []
        for m in range(16):
            for nt in range(2):
                sl = slice(nt * 512, (nt + 1) * 512)
                t1_ = P.op("dve", lambda E, m=m, sl=sl: E.scalar_tensor_tensor(prod[:, m, sl], prod[:, m, sl], gcol("mlp_post", m), rs7[:, sl], ALU.mult, ALU.mult), [t_rs7])
                res_t.append(P.op("dve", lambda E, m=m, sl=sl: E.tensor_tensor(h2[:, m, sl], h2[:, m, sl], prod[:, m, sl], ALU.add), [t1_, t_h2]))
        P.barrier()
        tap("h2", h2.rearrange("p a b -> p (a b)"), [128, 16 * 1024])
        rs7b = A.alloc([128, 1024], F32)
        prenorm(h2, "ple_pre", hn3, sqb7, rs7b, lambda i, nt: res_t[2 * i + nt])
        P.barrier()
        wr6 = make_wring(2)
        wpp = A.alloc([128, 2, D], BF16)
        pTb = A.alloc([128, 2, 1024], BF16)
        gt6 = [A.alloc([128, 512], F32) for _ in range(2)]
        g6free = [None, None]
        g6k = [0]
        t_wpp = P.dma("pool", wpp, w_pp.rearrange("(kt p) m -> p kt m", p=128))
        t_pT = P.dma("pool", pTb, pT.rearrange("(kt p) t -> p kt t", p=128))

        prod_t = {}

        def ev6(m, nt, b, t_mm):
            i = g6k[0] % 2
            g6k[0] += 1
            eb = 4 + (b % 4)
            P.op("pe", lambda E, eb=eb, m=m, nt=nt: E.matmul(banks[eb][:, :], wpp[:, 0, m * 128:(m + 1) * 128], pTb[:, 0, nt * 512:(nt + 1) * 512], start=True, stop=False),
                 [t_wpp, t_pT, bfree[eb]], inc=False)
            t_e = P.op("pe", lambda E, eb=eb, m=m, nt=nt: E.matmul(banks[eb][:, :], wpp[:, 1, m * 128:(m + 1) * 128], pTb[:, 1, nt * 512:(nt + 1) * 512], start=False, stop=True), [])
            t_s = P.op("act", lambda E, i=i, b=b: E.activation(gt6[i], banks[b][:, :], AF.Sigmoid), [t_mm, g6free[i]])
            t_p = P.op("dve", lambda E, i=i, eb=eb, m=m, nt=nt: E.tensor_tensor(prod[:, m, nt * 512:(nt + 1) * 512], gt6[i], banks[eb][:, :], ALU.mult), [t_s, t_e])
            g6free[i] = t_p
            bfree[eb] = t_p
            prod_t[(m, nt)] = t_p
            return t_s
        linear(w_pg.rearrange("(kt p) m -> p kt m", p=128), 16, list(range(16)), lambda kt, nt: hn3[:, kt, nt * 512:(nt + 1) * 512], ev6, wr6,
               [[0, 1, 2, 3]], [])
        P.barrier()
        t_rs8 = stats_of(lambda i, nt: prod[:, i, nt * 512:(nt + 1) * 512], 16, rs7, float(D), sqb7, [])
        for m in range(16):
            for nt in range(2):
                sl = slice(nt * 512, (nt + 1) * 512)
                t1_ = P.op("dve", lambda E, m=m, sl=sl: E.scalar_tensor_tensor(prod[:, m, sl], prod[:, m, sl], gcol("ple_post", m), rs7[:, sl], ALU.mult, ALU.mult), [t_rs8])
                t2_ = P.op("dve", lambda E, m=m, sl=sl: E.tensor_tensor(h2[:, m, sl], h2[:, m, sl], prod[:, m, sl], ALU.add), [t1_])
                P.dma("sp", out_d[m * 128:(m + 1) * 128, sl], h2[:, m, sl], [t2_])
        P.barrier()
        A.pop()
        P.emit()
    return nc, tap_out


def col_layout(v):
    v = np.asarray(v, np.float32).reshape(-1)
    return v.reshape(-1, 128).T


def prep_shared(inp):
    sh = {}
    colsv = [inp["mix_norm_pre"][0], inp["attn_out_norm"][0], inp["ssm_out_norm"][0], inp["mix_norm_post"][0],
             inp["mlp_norm_pre"][0], inp["mlp_norm_post"][0], inp["ple_norm_pre"][0], inp["ple_norm_post"][0],
             inp["ssm_d"][0], inp["b_glu"][0]]
    sh["cols"] = np.ascontiguousarray(np.concatenate([col_layout(v) for v in colsv], axis=1))

    def st(v):
        v = np.asarray(v, np.float32).reshape(32, 2, 64)
        return v.transpose(1, 2, 0).reshape(128, 32)
    ldt = np.broadcast_to(np.asarray(inp["log_dt"][0], np.float32)[:, None], (64, 64))
    sh["sp"] = np.ascontiguousarray(np.concatenate([st(inp["lam_re"][0]), st(inp["lam_im"][0]), st(ldt)], axis=1))

    def padB(B):
        B = np.asarray(B, np.float32).reshape(32, 2, 64, 16)
        o = np.zeros((2, 64, 32, 2, 16), np.float32)
        for gl in range(2):
            o[gl, :, :, gl, :] = B[:, gl].transpose(1, 0, 2)
        return o.reshape(128, 1024)

    def padC(C):
        C = np.asarray(C, np.float32).reshape(32, 2, 16, 64)
        o = np.zeros((2, 64, 32, 2, 16), np.float32)
        for gl in range(2):
            o[gl, :, :, gl, :] = C[:, gl].transpose(2, 0, 1)
        return o.reshape(128, 1024)
    sh["bpr"] = padB(inp["ssm_b_re"][0])
    sh["bpi"] = padB(inp["ssm_b_im"][0])
    sh["cpr"] = padC(inp["ssm_c_re"][0])
    sh["cpi"] = padC(inp["ssm_c_im"][0])
    for k_, n_ in (("w_in", "w_in"), ("w_glu", "w_glu"), ("w_out", "w_out"), ("w_up", "w_up"), ("w_down", "w_down"),
                   ("w_ple_gate", "w_pg"), ("w_ple_proj", "w_pp")):
        sh[n_] = np.ascontiguousarray(np.asarray(inp[k_][0], np.float32))
    return sh


def consts_for_core(j):
    kk = np.arange(128)[:, None]
    ii = np.arange(128)[None, :]
    ident = (kk == ii).astype(np.float32)
    mprev = (kk >= ii).astype(np.float32)
    mcur = (kk <= ii).astype(np.float32)
    bmask = ((kk // 16) == (ii // 16)).astype(np.float32)
    pmask = ((kk // 32) == np.arange(4)[None, :]).astype(np.float32)
    T0 = 1024 * j
    vc = np.zeros((128, NVT), np.float32)
    for i, (d, r, m, nk) in enumerate(VT_LIST):
        t_abs = T0 + r + d * (m + np.arange(nk))
        vc[:nk, i] = (t_abs >= 0).astype(np.float32)
    return np.ascontiguousarray(np.concatenate([ident, mprev, mcur, bmask, pmask, vc], axis=1))


def prep_core(c, inp, sh):
    b, j = c // 4, c % 4
    T0 = 1024 * j
    x = np.asarray(inp["x"], np.float32)
    xe = np.zeros((NE, D), np.float32)
    lo = T0 - (NE - NO)
    s0 = max(lo, 0)
    xe[s0 - lo:, :] = x[b, s0:T0 + NO, :]
    m = dict(sh)
    m["xT"] = np.ascontiguousarray(xe.T)
    m["pT"] = np.ascontiguousarray(np.asarray(inp["p"], np.float32)[0, b, T0:T0 + NO, :].T)
    m["consts"] = consts_for_core(j)
    return m


_CACHE = {}


def kernel(**inputs):
    if "nc" not in _CACHE:
        _CACHE["nc"] = build_nc()[0]
    nc = _CACHE["nc"]
    sh = prep_shared(inputs)
    in_maps = [prep_core(c, inputs, sh) for c in range(8)]
    res = run_bass_kernel_spmd(nc, in_maps, core_ids=list(range(8)))
    out = np.zeros((2, 4096, D), np.float32)
    for c in range(8):
        b, j = c // 4, c % 4
        out[b, 1024 * j:1024 * (j + 1), :] = res.results[c]["out"].T
    return out
```
